# Optimizing a Trainium2 kernel written in Bass

```python
import jax, jax.numpy as jnp
from jax import lax
import numpy as np

D_MODEL = 1024
BATCH = 8
SEQ = 4096
DEPTH = 1

N_META = 16
N_HEADS = 8
HEAD_DIM = 128
D_ATTN = N_HEADS * HEAD_DIM
D_CONV = D_MODEL
CONV_WIDTH = 3
D_FF = 2816
Q_BLOCK = 128
N_BRANCH = 2
RMS_EPS = 1e-6

D_IN = 3 * D_ATTN + N_HEADS + 3 * D_CONV + N_BRANCH * D_MODEL
SPLIT_POINTS = [D_ATTN, 2 * D_ATTN, 3 * D_ATTN, 3 * D_ATTN + N_HEADS,
                3 * D_ATTN + N_HEADS + D_CONV, 3 * D_ATTN + N_HEADS + 2 * D_CONV,
                3 * D_ATTN + N_HEADS + 3 * D_CONV]

kernel_name = "hybrid_fox_shortconv_gated_merge"


def rms_norm(x, g):
    xf = x.astype(jnp.float32)
    y = xf * lax.rsqrt(jnp.mean(xf * xf, axis=-1, keepdims=True) + RMS_EPS)
    return (y * g.astype(jnp.float32)).astype(x.dtype)


def causal_dwconv(u, w):
    L = u.shape[1]
    up = jnp.pad(u, ((0, 0), (CONV_WIDTH - 1, 0), (0, 0)))
    out = up[:, 0:L] * w[0]
    for k in range(1, CONV_WIDTH):
        out = out + up[:, k:k + L] * w[k]
    return out


def forgetting_attention(q, k, v, cum_logf):
    B, L = q.shape[0], q.shape[1]
    n_real = L - N_META
    n_blk = n_real // Q_BLOCK
    scale = HEAD_DIM ** -0.5
    cum_t = jnp.transpose(cum_logf, (0, 2, 1))
    kpos = jnp.arange(L)

    def attend(args):
        q_blk, c_blk, qpos = args
        s = jnp.einsum('bqhd,bkhd->bhqk', q_blk, k,
                       preferred_element_type=jnp.float32) * scale
        s = s + (c_blk[..., :, None] - cum_t[..., None, :])
        mask = kpos[None, :] <= qpos[:, None]
        s = jnp.where(mask, s, -jnp.inf)
        p = jax.nn.softmax(s, axis=-1)
        return jnp.einsum('bhqk,bkhd->bqhd', p.astype(v.dtype), v)

    o_meta = attend((q[:, :N_META], cum_t[:, :, :N_META], jnp.arange(N_META)))
    q_r = q[:, N_META:].reshape(B, n_blk, Q_BLOCK, N_HEADS, HEAD_DIM).transpose(1, 0, 2, 3, 4)
    c_r = cum_t[:, :, N_META:].reshape(B, N_HEADS, n_blk, Q_BLOCK).transpose(2, 0, 1, 3)
    pos_r = (N_META + jnp.arange(n_real)).reshape(n_blk, Q_BLOCK)
    o_r = lax.map(attend, (q_r, c_r, pos_r))
    o_r = o_r.transpose(1, 0, 2, 3, 4).reshape(B, n_real, N_HEADS, HEAD_DIM)
    return jnp.concatenate([o_meta, o_r], axis=1)


def hybrid_mixer(h, w_in, b_f, conv_w, w_o_attn, w_o_conv, w_o):
    B, L, _ = h.shape
    proj = h @ w_in
    q, k, v, f_logit, gb, gc, u, gates = jnp.split(proj, SPLIT_POINTS, axis=-1)
    log_f = jax.nn.log_sigmoid((f_logit + b_f).astype(jnp.float32))
    cum_logf = jnp.cumsum(log_f, axis=1)
    att = forgetting_attention(q.reshape(B, L, N_HEADS, HEAD_DIM),
                               k.reshape(B, L, N_HEADS, HEAD_DIM),
                               v.reshape(B, L, N_HEADS, HEAD_DIM), cum_logf)
    y_att = att.reshape(B, L, D_ATTN) @ w_o_attn
    y_conv = (gb * causal_dwconv(gc * u, conv_w)) @ w_o_conv
    g_att, g_conv = jnp.split(jax.nn.sigmoid(gates), N_BRANCH, axis=-1)
    return (g_att * y_att + g_conv * y_conv) @ w_o


def conv_glu(h, w_ffn_in, ffn_conv_w, w_ffn_out):
    a, val = jnp.split(h @ w_ffn_in, 2, axis=-1)
    return (jax.nn.silu(causal_dwconv(a, ffn_conv_w)) * val) @ w_ffn_out


def setup_inputs(seed: int = 0) -> dict:
    key = jax.random.key(seed)
    ks = jax.random.split(key, 16)
    nrm = lambda k, shape, fan_in: jax.random.normal(k, shape, jnp.float32) * (fan_in ** -0.5)
    return {
        "x": jax.random.normal(ks[0], (BATCH, SEQ, D_MODEL), jnp.float32),
        "meta_tokens": jax.random.normal(ks[1], (N_META, D_MODEL), jnp.float32),
        "g_mix": 1.0 + 0.02 * jax.random.normal(ks[2], (DEPTH, D_MODEL), jnp.float32),
        "w_in": nrm(ks[3], (DEPTH, D_MODEL, D_IN), D_MODEL),
        "b_f": jax.random.uniform(ks[4], (DEPTH, N_HEADS), jnp.float32, 1.0, 6.0),
        "conv_w": nrm(ks[5], (DEPTH, CONV_WIDTH, D_CONV), CONV_WIDTH),
        "w_o_attn": nrm(ks[6], (DEPTH, D_ATTN, D_MODEL), D_ATTN),
        "w_o_conv": nrm(ks[7], (DEPTH, D_CONV, D_MODEL), D_CONV),
        "w_o": nrm(ks[8], (DEPTH, D_MODEL, D_MODEL), D_MODEL),
        "g_ffn": 1.0 + 0.02 * jax.random.normal(ks[9], (DEPTH, D_MODEL), jnp.float32),
        "w_ffn_in": nrm(ks[10], (DEPTH, D_MODEL, 2 * D_FF), D_MODEL),
        "ffn_conv_w": nrm(ks[11], (DEPTH, CONV_WIDTH, D_FF), CONV_WIDTH),
        "w_ffn_out": nrm(ks[12], (DEPTH, D_FF, D_MODEL), D_FF),
        "g_final": 1.0 + 0.02 * jax.random.normal(ks[13], (D_MODEL,), jnp.float32),
    }


def reference(x, meta_tokens, g_mix, w_in, b_f, conv_w, w_o_attn, w_o_conv, w_o,
              g_ffn, w_ffn_in, ffn_conv_w, w_ffn_out, g_final):
    B = x.shape[0]
    meta = jnp.broadcast_to(meta_tokens.astype(x.dtype)[None], (B, N_META, x.shape[-1]))
    z = jnp.concatenate([meta, x], axis=1)
    for l in range(DEPTH):
        z = z + hybrid_mixer(rms_norm(z, g_mix[l]), w_in[l], b_f[l], conv_w[l],
                             w_o_attn[l], w_o_conv[l], w_o[l])
        z = z + conv_glu(rms_norm(z, g_ffn[l]), w_ffn_in[l], ffn_conv_w[l], w_ffn_out[l])
    return rms_norm(z, g_final)[:, N_META:]
```

```python
import contextlib
import numpy as np
import concourse.bass as bass
import concourse.mybir as mybir
from concourse.bass_utils import run_bass_kernel_spmd

F32 = mybir.dt.float32
BF16 = mybir.dt.bfloat16
AF = mybir.ActivationFunctionType
ALU = mybir.AluOpType

D = 1024
SEQ = 4096
NMETA = 16
L = SEQ + NMETA
NT = 33
NPOS = NT * 128
DFF = 2816
NFC = DFF // 128
DIN = 8200
KOFF, VOFF, FOFF, BOFF, COFF, UOFF, GAOFF, GCOFF = 1024, 2048, 3072, 3080, 4104, 5128, 6152, 7176
SCALE = float(128 ** -0.5)
EPS = 1e-6
NEG = -30000.0

ENGS = ("pe", "act", "dve", "pool", "sp")


class Prog:
    def __init__(self, nc, same_engine_sync=True):
        self.nc = nc
        self.ops = {e: [] for e in ENGS}
        self.count = {}
        self.waited = {e: {} for e in ENGS}
        self.last_write = {}
        self.readers = {}
        self.same_engine_sync = same_engine_sync
        self.dma_keys = []

    def _deps(self, eng, reads, writes):
        evs = []
        for r in reads:
            e = self.last_write.get(r)
            if e is not None:
                evs.append(e)
        for r in writes:
            e = self.last_write.get(r)
            if e is not None:
                evs.append(e)
            evs.extend(self.readers.get(r, ()))
        best = {}
        w = self.waited[eng]
        for (k, v) in evs:
            if k == ("eng", eng) and (eng == "pe" or not self.same_engine_sync):
                continue
            if w.get(k, 0) >= v:
                continue
            best[k] = max(best.get(k, 0), v)
        for k, v in best.items():
            w[k] = v
        return list(best.items())

    def _commit(self, ev, reads, writes):
        for r in reads:
            self.readers.setdefault(r, []).append(ev)
        for r in writes:
            self.last_write[r] = ev
            self.readers[r] = []

    def op(self, eng, fn, reads=(), writes=()):
        waits = self._deps(eng, reads, writes)
        k = ("eng", eng)
        self.count[k] = self.count.get(k, 0) + 1
        ev = (k, self.count[k])
        self.ops[eng].append((fn, waits, k, 1))
        self._commit(ev, reads, writes)
        return ev

    def dma(self, eng, out, in_, key, reads=(), writes=()):
        waits = self._deps(eng, reads, writes)
        k = ("dma", key)
        if k not in self.count:
            self.dma_keys.append(k)
        self.count[k] = self.count.get(k, 0) + 16
        ev = (k, self.count[k])
        fn = lambda e, out=out, in_=in_: e.dma_start(out=out, in_=in_)
        self.ops[eng].append((fn, waits, k, 16))
        self._commit(ev, reads, writes)
        return ev

    def wait_events(self, eng, events):
        waits = []
        for (k, v) in events:
            if self.waited[eng].get(k, 0) < v:
                self.waited[eng][k] = v
                waits.append((k, v))
        if waits:
            self.ops[eng].append((None, waits, None, 0))

    def barrier(self):
        evs = [(k, v) for k, v in self.count.items() if v > 0]
        for e in ENGS:
            self.wait_events(e, [(k, v) for (k, v) in evs if not (k == ("eng", e) and e in ("pe", "sp"))])
        self.last_write = {}
        self.readers = {}

    def emit(self):
        nc = self.nc
        with contextlib.ExitStack() as st:
            sems = {}
            keys = [("eng", e) for e in ENGS] + self.dma_keys
            for i, k in enumerate(keys):
                if self.count.get(k, 0) == 0:
                    continue
                sems[k] = st.enter_context(nc.semaphore("s%d" % i))
            block = st.enter_context(nc.Block())

            def run(eng_name):
                def body(e):
                    for (fn, waits, k, n) in self.ops[eng_name]:
                        for (wk, wv) in waits:
                            e.wait_ge(sems[wk], wv)
                        if fn is None:
                            continue
                        fn(e).then_inc(sems[k], n)
                return body

            block.tensor(run("pe"))
            block.scalar(run("act"))
            block.vector(run("dve"))
            block.gpsimd(run("pool"))
            block.sync(run("sp"))


def build_nc(debug=False, p2_blocks=8):
    nc = bass.Bass("TRN2", target_bir_lowering=False)
    dt_in = lambda name, shape: nc.dram_tensor(name, shape, F32, kind="ExternalInput").ap()
    x = dt_in("x", [SEQ, D])
    meta = dt_in("meta", [NMETA, D])
    w_in = dt_in("w_in", [D, DIN])
    w_oa = dt_in("w_o_attn", [D, D])
    w_oc = dt_in("w_o_conv", [D, D])
    w_o = dt_in("w_o", [D, D])
    w_fi = dt_in("w_ffn_in", [D, 2 * DFF])
    w_fo = dt_in("w_ffn_out", [DFF, D])
    g3_d = dt_in("g3", [128, 3, D])
    bf_d = dt_in("bfrep", [128, NT * 8])
    cw_d = dt_in("cw", [128, 8, 3])
    fcw_d = dt_in("fcw", [128, NFC, 3])
    cst_d = dt_in("cst", [128, 4, 128])
    out = nc.dram_tensor("out", [SEQ, D], F32, kind="ExternalOutput").ap()
    if debug:
        dbg_att = nc.dram_tensor("dbg_att", [128, 8, L], F32, kind="ExternalOutput").ap()
        dbg_c = nc.dram_tensor("dbg_c", [128, NT * 8], F32, kind="ExternalOutput").ap()
        dbg_B = nc.dram_tensor("dbg_B", [128, 8, 512], BF16, kind="ExternalOutput").ap()
        dbg_C = nc.dram_tensor("dbg_C", [128, 8, 512], BF16, kind="ExternalOutput").ap()
        dbg_A = nc.dram_tensor("dbg_A", [128, 8, 512], BF16, kind="ExternalOutput").ap()
        dbg_G = nc.dram_tensor("dbg_G", [128, NFC, 512], BF16, kind="ExternalOutput").ap()
        dbg_Z = nc.dram_tensor("dbg_Z", [128, 4, D], F32, kind="ExternalOutput").ap()

    kview = lambda w: w.rearrange("(kc p) c -> p kc c", p=128)
    w_in_v, w_oa_v, w_oc_v, w_o_v, w_fi_v, w_fo_v = map(kview, (w_in, w_oa, w_oc, w_o, w_fi, w_fo))

    CH = []
    for cc in range(8):
        CH.append((8, 384, [(i * 128, w_in_v[:, :, o_ + cc * 128:o_ + (cc + 1) * 128], 128)
                            for i, o_ in enumerate((BOFF, COFF, UOFF))]))
    for oc in range(8):
        cs_ = slice(oc * 128, (oc + 1) * 128)
        CH.append((8, 512, [(0, w_oa_v[:, :, cs_], 128), (128, w_oc_v[:, :, cs_], 128),
                            (256, w_in_v[:, :, GAOFF + oc * 128:GAOFF + (oc + 1) * 128], 128),
                            (384, w_in_v[:, :, GCOFF + oc * 128:GCOFF + (oc + 1) * 128], 128)]))
    for half in range(2):
        CH.append((8, 512, [(0, w_o_v[:, :, half * 512:(half + 1) * 512], 512)]))
    for p_ in range(NFC // 2):
        CH.append((8, 512, [(0, w_fi_v[:, :, p_ * 256:(p_ + 1) * 256], 256),
                            (256, w_fi_v[:, :, DFF + p_ * 256:DFF + (p_ + 1) * 256], 256)]))
    for half in range(2):
        for kg in range(3):
            k0_ = kg * 8
            nk_ = min(8, NFC - k0_)
            CH.append((nk_, 512, [(0, w_fo_v[:, k0_:k0_ + nk_, half * 512:(half + 1) * 512], 512)]))
    NCHUNK = len(CH)
    assert NCHUNK == 35
    CI_A, CI_B, CI_C, CI_D, CI_E = 0, 8, 16, 18, 29
    wscr = nc.dram_tensor("wscr", [NCHUNK, 128, 8, 512], BF16, kind="Internal").ap()
    conv_state = {"i": 0, "n": 0}

    P = Prog(nc)

    def convert_chunks(n):
        for _ in range(n):
            ci = conv_state["i"]
            if ci >= NCHUNK:
                return
            conv_state["i"] += 1
            nk, ncols, pieces = CH[ci]
            for (c0, src, w) in pieces:
                conv_state["n"] += 1
                P.dma("pool", wscr[ci][:, 0:nk, c0:c0 + w], src, "cv%d" % (conv_state["n"] % 4), writes=["scr%d" % conv_state["n"]])

    with contextlib.ExitStack() as st0:
        T0 = lambda name, shape, dt: st0.enter_context(nc.sbuf_tensor("sb_" + name, shape, dt))
        attT = T0("attT", [128, 8, L], BF16)
        cstb = T0("cstb", [128, 4, 128], BF16)
        g3 = T0("g3", [128, 3, D], F32)
        ss = T0("ss", [128, 8], F32)
        rs = T0("rs", [128, 8], F32)
        xs = [T0("xs%d" % i, [128, D], BF16) for i in range(2)]
        junk = T0("junk", [128, D], BF16)
        ident, negmask, uincl, ones = (cstb[:, i, :] for i in range(4))
        TR = {"banks": None}
        BK = {"banks": None}

        with nc.sbuf_tensor("sb_cstf", [128, 4, 128], F32) as cstf:
            P.dma("sp", cstf[:], cst_d, "cst", writes=["cstf"])
            P.op("dve", lambda e: e.tensor_copy(cstb[:], cstf[:]), reads=["cstf"], writes=["cst"])
            P.dma("sp", g3[:], g3_d, "g3", writes=["g3"])
            P.barrier()

        ring_state = {"i": 0}

        def psalloc(ring):
            i = ring[ring_state["i"] % len(ring)]
            ring_state["i"] += 1
            return i

        evac_state = {"i": 0}

        def copy_any(out_ap, in_ap, reads, writes):
            evac_state["i"] += 1
            mode = evac_state.get("mode", "both")
            if mode == "act" or (mode == "both" and evac_state["i"] % 2):
                P.op("act", lambda e: e.activation(out_ap, in_ap, AF.Copy), reads=reads, writes=writes)
            else:
                P.op("dve", lambda e: e.tensor_copy(out_ap, in_ap), reads=reads, writes=writes)

        def mm_group(out_ap, pairs, reads, writes, first_start=True):
            def fn(e):
                n = len(pairs)
                ins = None
                for i, (l, r) in enumerate(pairs):
                    ins = e.matmul(out_ap, lhsT=l, rhs=r, start=(first_start and i == 0), stop=(i == n - 1))
                return ins
            P.op("pe", fn, reads=reads, writes=writes)

        norm_ctr = {"i": 0}

        def norm_stats(src, rows, src_res):
            i = norm_ctr["i"]
            norm_ctr["i"] += 1
            c = i % 8
            P.op("dve", lambda e: e.memset(ss[:, c:c + 1], 0.0), writes=["ss%d" % c])
            P.op("act", lambda e: e.activation(junk[0:rows, :], src, AF.Square, accum_out=ss[0:rows, c:c + 1]),
                 reads=[src_res], writes=["ss%d" % c, "junk"])
            P.op("act", lambda e: e.activation(rs[:, c:c + 1], ss[:, c:c + 1], AF.Ln, bias=EPS, scale=1.0 / D),
                 reads=["ss%d" % c], writes=["rs%d" % c])
            P.op("act", lambda e: e.activation(rs[:, c:c + 1], rs[:, c:c + 1], AF.Exp, scale=-0.5),
                 reads=["rs%d" % c], writes=["rs%d" % c])
            return c

        fin_ctr = {"i": 0}

        def norm_finish(c, src, rows, dst, grow, src_res, dst_res):
            i = fin_ctr["i"]
            fin_ctr["i"] += 1
            xi = i % 2
            xb = xs[xi]
            P.op("dve", lambda e: e.scalar_tensor_tensor(
                xb[0:rows, :], src, rs[0:rows, c:c + 1], grow[0:rows, :], ALU.mult, ALU.mult),
                reads=[src_res, "rs%d" % c, "g3"], writes=["xs%d" % xi])
            nb_ = len(TR["banks"])
            ti = i % nb_
            trb = TR["banks"][ti]

            def tr(e):
                ins = None
                for kc in range(8):
                    ins = e.transpose(trb[:, kc, 0:rows], xb[0:rows, kc * 128:(kc + 1) * 128], ident[0:rows, 0:rows])
                return ins
            P.op("pe", tr, reads=["xs%d" % xi, "cst"], writes=["tr%d" % ti])
            copy_any(dst, trb[:, :, 0:rows], reads=["tr%d" % ti], writes=[dst_res])

        def norm_tile(src, rows, dst, grow, src_res, dst_res):
            c = norm_stats(src, rows, src_res)
            norm_finish(c, src, rows, dst, grow, src_res, dst_res)

        with contextlib.ExitStack() as st1:
            T1 = lambda name, shape, dt: st1.enter_context(nc.sbuf_tensor("sb_" + name, shape, dt))
            hT = T1("hT", [128, 8, NPOS], BF16)
            cpos = T1("cpos", [128, NT * 8], F32)
            off = T1("off", [128, (NT + 1) * 8], F32)
            TR["banks"] = [st1.enter_context(nc.psum_tensor("tr_ps", [128, 8, 128], BF16))]
            banks = [st1.enter_context(nc.psum_tensor("bank%d" % i, [128, 512], F32)) for i in range(7)]
            ring1 = [0, 1, 2, 3, 4]
            O_b, L_b = banks[5], banks[6]

            with contextlib.ExitStack() as st:
                Ts = lambda name, shape, dt: st.enter_context(nc.sbuf_tensor("sb_" + name, shape, dt))
                xt = [Ts("xt%d" % i, [128, 4, D], F32) for i in range(2)]
                P.op("pool", lambda e: e.memset(hT[:, :, L:NPOS], 0.0), writes=["hT"])
                P.dma("sp", xt[0][0:NMETA, 0, :], meta, "xt0", writes=["xt0"])
                norm_tile(xt[0][0:NMETA, 0, :], NMETA, hT[:, :, 0:NMETA], g3[:, 0, :], "xt0", "hT")
                for b in range(8):
                    xb_ = xt[(b + 1) % 2]
                    res = "xt%d" % ((b + 1) % 2)
                    P.dma("sp", xb_[:], x[b * 512:(b + 1) * 512, :].rearrange("(t p) d -> p t d", p=128),
                          res, writes=[res])
                    for t in range(4):
                        p0 = NMETA + b * 512 + t * 128
                        norm_tile(xb_[:, t, :], 128, hT[:, :, p0:p0 + 128], g3[:, 0, :], res, "hT")
            P.barrier()

            with contextlib.ExitStack() as st:
                Ts = lambda name, shape, dt: st.enter_context(nc.sbuf_tensor("sb_" + name, shape, dt))
                wf = Ts("wf", [128, 8, 8], BF16)
                bfr = Ts("bfr", [128, NT * 8], F32)
                fb = Ts("fb", [128, NT * 8], F32)
                r1 = Ts("r1", [128, NT * 8], F32)
                parts = [Ts("part%d" % i, [128, NT * 8], BF16) for i in range(3)]
                tot = Ts("tot", [128, NT * 8], F32)
                P.dma("pool", wf[:], w_in_v[:, :, FOFF:FOFF + 8], "wf", writes=["wf"])
                P.dma("sp", bfr[:], bf_d, "bfr", writes=["bfr"])
                psF, psC, psT = banks[0], banks[1], banks[2]

                def fmm(e):
                    ins = None
                    for j in range(NT):
                        for kc in range(8):
                            ins = e.matmul(psF[:, j * 8:(j + 1) * 8], lhsT=hT[:, kc, j * 128:(j + 1) * 128],
                                           rhs=wf[:, kc, :], start=(kc == 0), stop=(kc == 7))
                    return ins
                P.op("pe", fmm, reads=["hT", "wf"], writes=["bank0"])
                NF = NT * 8
                P.op("dve", lambda e: e.tensor_tensor(fb[:], psF[:, 0:NF], bfr[:], ALU.add),
                     reads=["bank0", "bfr"], writes=["fb"])
                P.op("act", lambda e: e.activation(fb[:], fb[:], AF.Exp, scale=-1.0), reads=["fb"], writes=["fb"])
                P.op("act", lambda e: e.activation(fb[:], fb[:], AF.Ln, bias=1.0), reads=["fb"], writes=["fb"])
                P.op("dve", lambda e: e.tensor_copy(parts[0][:], fb[:]), reads=["fb"], writes=["p0"])
                P.op("dve", lambda e: e.tensor_tensor(r1[:], fb[:], parts[0][:], ALU.subtract),
                     reads=["fb", "p0"], writes=["r1"])
                P.op("dve", lambda e: e.tensor_copy(parts[1][:], r1[:]), reads=["r1"], writes=["p1"])
                P.op("dve", lambda e: e.tensor_tensor(r1[:], r1[:], parts[1][:], ALU.subtract),
                     reads=["r1", "p1"], writes=["r1"])
                P.op("dve", lambda e: e.tensor_copy(parts[2][:], r1[:]), reads=["r1"], writes=["p2"])
                mm_group(psC[:, 0:NF], [(uincl, parts[i][:]) for i in range(3)],
                         reads=["cst", "p0", "p1", "p2"], writes=["bank1"])
                mm_group(psT[:, 0:NF], [(ones, parts[i][:]) for i in range(3)],
                         reads=["cst", "p0", "p1", "p2"], writes=["bank2"])
                P.op("dve", lambda e: e.tensor_copy(tot[:], psT[:, 0:NF]), reads=["bank2"], writes=["tot"])
                P.op("dve", lambda e: e.memset(off[:, 0:8], 0.0), writes=["off"])
                for j in range(1, NT + 1):
                    P.op("dve", lambda e, j=j: e.tensor_tensor(off[:, j * 8:(j + 1) * 8], off[:, (j - 1) * 8:j * 8],
                                                               tot[:, (j - 1) * 8:j * 8], ALU.add),
                         reads=["off", "tot"], writes=["off"])
                P.op("dve", lambda e: e.tensor_tensor(cpos[:], psC[:, 0:NF], off[:, 0:NF], ALU.add),
                     reads=["bank1", "off"], writes=["cpos"])
                if debug:
                    P.dma("sp", dbg_c, cpos[:], "dbgc", reads=["cpos"])
            P.barrier()

            with contextlib.ExitStack() as st:
                Ts = lambda name, shape, dt: st.enter_context(nc.sbuf_tensor("sb_" + name, shape, dt))
                qT = Ts("qT", [128, NPOS], BF16)
                kT = Ts("kT", [128, NPOS], BF16)
                Vt = Ts("Vt", [128, NPOS], BF16)
                wqkv = [Ts("wqkv%d" % i, [128, 8, 3, 128], BF16) for i in range(2)]
                Bh = [Ts("Bh%d" % i, [128, 17, 34], F32) for i in range(2)]
                NPT = 6
                PT = [Ts("PT%d" % i, [128, 512], BF16) for i in range(NPT)]
                rl = [Ts("rl%d" % i, [128, 512], F32) for i in range(2)]
                cpos3 = cpos[:].rearrange("p (j h) -> p j h", h=8)
                evac_state["mode"] = "dve"
                pt_i = 0
                rl_i = 0
                for h in range(8):
                    wq = wqkv[h % 2]
                    wres = "wqkv%d" % (h % 2)
                    for t in range(3):
                        P.dma("pool", wq[:, :, t, :], w_in_v[:, :, t * 1024 + h * 128:t * 1024 + (h + 1) * 128],
                              wres + "_%d" % t, writes=[wres])
                    convert_chunks(6)
                    bh = Bh[h % 2]
                    bres = "Bh%d" % (h % 2)
                    for m in range(17):
                        jn = min(2 * m + 2, NT)
                        P.op("dve", lambda e, m=m, jn=jn, bh=bh, h=h: e.tensor_scalar(
                            bh[:, m, 0:jn], cpos3[:, 0:jn, h], off[:, (2 * m + 1) * 8 + h:(2 * m + 1) * 8 + h + 1],
                            None, ALU.subtract), reads=["cpos", "off"], writes=[bres])
                    for (t, dst, dres) in ((0, qT, "qT"), (1, kT, "kT")):
                        for n in range(9):
                            c0 = n * 512
                            w = min(512, NPOS - c0)
                            b = psalloc(ring1)
                            mm_group(banks[b][:, 0:w], [(wq[:, kc, t, :], hT[:, kc, c0:c0 + w]) for kc in range(8)],
                                     reads=[wres, "hT"], writes=["bank%d" % b])
                            copy_any(dst[:, c0:c0 + w], banks[b][:, 0:w], reads=["bank%d" % b], writes=[dres])
                    for j4 in range(0, NT, 4):
                        nj = min(4, NT - j4)
                        b = psalloc(ring1)

                        def vmm(e, j4=j4, nj=nj, bk=banks[b], wq=wq):
                            ins = None
                            for jj in range(nj):
                                j = j4 + jj
                                for kc in range(8):
                                    ins = e.matmul(bk[:, jj * 128:(jj + 1) * 128],
                                                   lhsT=hT[:, kc, j * 128:(j + 1) * 128], rhs=wq[:, kc, 2, :],
                                                   start=(kc == 0), stop=(kc == 7))
                            return ins
                        P.op("pe", vmm, reads=[wres, "hT"], writes=["bank%d" % b])
                        copy_any(Vt[:, j4 * 128:(j4 + nj) * 128], banks[b][:, 0:nj * 128],
                                 reads=["bank%d" % b], writes=["Vt"])
                    steps = []
                    for qb in range(9):
                        q0 = qb * 512
                        qend = min(q0 + 512, L)
                        jlast = (qend - 1) // 128
                        for j in range(jlast + 1):
                            steps.append((qb, q0, qend, jlast, j))
                    LA = 3
                    infl = {}

                    def issue_qk(i):
                        nonlocal pt_i
                        (qb, q0, qend, jlast, j) = steps[i]
                        c0 = max(q0, 128 * j)
                        ncols = qend - c0
                        diag = (128 * j >= q0)
                        b = psalloc(ring1)
                        S = banks[b]

                        def smm(e, j=j, c0=c0, ncols=ncols, diag=diag, S=S):
                            kt = kT[:, j * 128:(j + 1) * 128]
                            if not diag:
                                return e.matmul(S[:, 0:ncols], lhsT=kt, rhs=qT[:, c0:c0 + ncols],
                                                start=True, stop=True)
                            wd = min(128, ncols)
                            e.matmul(S[:, 0:wd], lhsT=ident, rhs=negmask[:, 0:wd], start=True, stop=False,
                                     skip_group_check=True)
                            ins = e.matmul(S[:, 0:wd], lhsT=kt, rhs=qT[:, c0:c0 + wd], start=False, stop=True,
                                           skip_group_check=True)
                            if ncols > wd:
                                ins = e.matmul(S[:, wd:ncols], lhsT=kt, rhs=qT[:, c0 + wd:c0 + ncols],
                                               start=False, stop=True, skip_group_check=True)
                            return ins
                        P.op("pe", smm, reads=["qT", "kT", "cst"], writes=["bank%d" % b])
                        pt = PT[pt_i % NPT]
                        pres = "PT%d" % (pt_i % NPT)
                        pt_i += 1
                        m_lo, m_hi = c0 // 256, (qend - 1) // 256
                        for m in range(m_lo, m_hi + 1):
                            a = max(c0, 256 * m) - c0
                            bnd = min(qend, 256 * m + 256) - c0
                            P.op("act", lambda e, pt=pt, S=S, a=a, bnd=bnd, bh=bh, m=m, j=j: e.activation(
                                pt[:, a:bnd], S[:, a:bnd], AF.Exp, bias=bh[:, m, j:j + 1], scale=SCALE),
                                reads=["bank%d" % b, bres], writes=[pres])
                        infl[i] = (pt, pres, c0, ncols)

                    def issue_pv(i):
                        nonlocal rl_i
                        (qb, q0, qend, jlast, j) = steps[i]
                        (pt, pres, c0, ncols) = infl.pop(i)
                        o0 = c0 - q0
                        nq = qend - q0

                        def pvmm(e, j=j, o0=o0, ncols=ncols, pt=pt, jlast=jlast):
                            e.matmul(O_b[:, o0:o0 + ncols], lhsT=Vt[:, j * 128:(j + 1) * 128], rhs=pt[:, 0:ncols],
                                     start=(j == 0), stop=(j == jlast), skip_group_check=True)
                            return e.matmul(L_b[:, o0:o0 + ncols], lhsT=ones, rhs=pt[:, 0:ncols],
                                            start=(j == 0), stop=(j == jlast), skip_group_check=True)
                        P.op("pe", pvmm, reads=[pres, "Vt", "cst"], writes=["bank5", "bank6"])
                        if j == jlast:
                            r = rl[rl_i % 2]
                            rres = "rl%d" % (rl_i % 2)
                            rl_i += 1
                            P.op("dve", lambda e, r=r, nq=nq: e.reciprocal(r[:, 0:nq], L_b[:, 0:nq]),
                                 reads=["bank6"], writes=[rres])
                            P.op("dve", lambda e, r=r, nq=nq, h=h, q0=q0: e.tensor_tensor(
                                attT[:, h, q0:q0 + nq], O_b[:, 0:nq], r[:, 0:nq], ALU.mult),
                                reads=["bank5", rres], writes=["attT"])

                    for i in range(len(steps) + LA):
                        if i < len(steps):
                            issue_qk(i)
                        if i >= LA:
                            issue_pv(i - LA)
            P.barrier()

        if debug:
            with contextlib.ExitStack() as st:
                dbt = st.enter_context(nc.sbuf_tensor("dbt", [128, 4, L], F32))
                for hh in range(2):
                    P.op("dve", lambda e, hh=hh: e.tensor_copy(dbt[:], attT[:, hh * 4:(hh + 1) * 4, :]),
                         reads=["attT"], writes=["dbt"])
                    P.dma("sp", dbg_att[:, hh * 4:(hh + 1) * 4, :], dbt[:], "dbga", reads=["dbt"])
                P.barrier()

        with contextlib.ExitStack() as st:
            Ts = lambda name, shape, dt: st.enter_context(nc.sbuf_tensor("sb_" + name, shape, dt))
            TR["banks"] = [st.enter_context(nc.psum_tensor("tr2_%d" % i, [128, 8, 128], BF16)) for i in range(2)]
            banks = [st.enter_context(nc.psum_tensor("bk2_%d" % i, [128, 512], F32)) for i in range(6)]
            ring2 = [0, 1, 2, 3, 4, 5]
            evac_state["mode"] = "both"
            Z = Ts("Z", [128, 4, D], F32)
            bufA = Ts("bufA", [128, 8, 512], BF16)
            bufB = Ts("bufB", [128, 8, 512], BF16)
            bufC = Ts("bufC", [128, 8, 512], BF16)
            gT = Ts("gT", [128, NFC, 512], BF16)
            NB = 4
            wr = [Ts("wr%d" % i, [128, 8, 512], BF16) for i in range(NB)]
            cub = [Ts("cub%d" % i, [128, 514], F32) for i in range(2)]
            usb = [Ts("usb%d" % i, [128, 512], F32) for i in range(2)]
            tmp = [Ts("tmp%d" % i, [128, 512], F32) for i in range(2)]
            sg = [Ts("sg%d" % i, [128, 512], F32) for i in range(2)]
            NSTG = 3
            stg = [Ts("stg%d" % i, [128, D], F32) for i in range(NSTG)]
            cw = Ts("cw", [128, 8, 3], F32)
            fcw = Ts("fcw", [128, NFC, 3], F32)
            carry_cu = Ts("carry_cu", [128, 8, 2], F32)
            carry_a = Ts("carry_a", [128, NFC, 2], F32)
            P.dma("pool", cw[:], cw_d, "cw", writes=["cw"])
            P.dma("pool", fcw[:], fcw_d, "fcw", writes=["fcw"])
            P.op("dve", lambda e: e.memset(carry_cu[:], 0.0), writes=["carry_cu"])
            P.op("dve", lambda e: e.memset(carry_a[:], 0.0), writes=["carry_a"])

            wstate = {"i": 0}

            def load_w(ci):
                s_ = wstate["i"] % NB
                wstate["i"] += 1
                nk, ncols, _ = CH[ci]
                P.dma("sp", wr[s_][:, 0:nk, 0:ncols], wscr[ci][:, 0:nk, 0:ncols], "wr%d" % s_, writes=["wr%d" % s_])
                return wr[s_], "wr%d" % s_

            stg_state = {"i": 0}

            def stg_next():
                i = stg_state["i"] % NSTG
                stg_state["i"] += 1
                return stg[i], "stg%d" % i

            out_events = []

            def conv_taps(tm, tres, src, sres, wcol, N):
                P.op("dve", lambda e: e.scalar_tensor_tensor(tm[:, 0:N], src[:, 1:1 + N], wcol[:, 1:2],
                                                             tm[:, 0:N], ALU.mult, ALU.add),
                     reads=list(sres) + [tres, "cw", "fcw"], writes=[tres])
                P.op("dve", lambda e: e.scalar_tensor_tensor(tm[:, 0:N], src[:, 0:N], wcol[:, 0:1],
                                                             tm[:, 0:N], ALU.mult, ALU.add),
                     reads=list(sres) + [tres, "cw", "fcw"], writes=[tres])

            def do_norm1(k):
                if k < 0:
                    sb, sres = stg_next()
                    P.dma("pool", sb[0:NMETA, :], meta, sres, writes=[sres])
                    norm_tile(sb[0:NMETA, :], NMETA, bufA[:, :, 0:NMETA], g3[:, 0, :], sres, "bufA")
                    return
                pend = []
                for t in range(4):
                    if t == NSTG:
                        norm_finish(*pend.pop(0))
                    sb, sres = stg_next()
                    r0 = k * 512 + t * 128
                    P.dma("pool", sb[:], x[r0:r0 + 128, :], sres, writes=[sres])
                    c = norm_stats(sb[:], 128, sres)
                    pend.append((c, sb[:], 128, bufA[:, :, t * 128:(t + 1) * 128], g3[:, 0, :], sres, "bufA"))
                for a_ in pend:
                    norm_finish(*a_)

            def block(bi, nxt):
                is_meta = bi < 0
                N = NMETA if is_meta else 512
                ntt = 1 if is_meta else 4
                rows = NMETA if is_meta else 128
                pos0 = 0 if is_meta else NMETA + bi * 512
                if is_meta:
                    P.dma("pool", Z[0:NMETA, 0, :], meta, "Z0", writes=["Zt0"])
                else:
                    for t in range(4):
                        r0 = bi * 512 + t * 128
                        P.dma("pool", Z[:, t, :], x[r0:r0 + 128, :], "Z%d" % t, writes=["Zt%d" % t])
                for cc in range(8):
                    wA, rA = load_w(CI_A + cc)
                    pU, pC, pB = psalloc(ring2), psalloc(ring2), psalloc(ring2)
                    for (pb_, c0) in ((pU, 256), (pC, 128), (pB, 0)):
                        mm_group(banks[pb_][:, 0:N], [(wA[:, kc, c0:c0 + 128], bufA[:, kc, 0:N]) for kc in range(8)],
                                 reads=[rA, "bufA"], writes=["bk%d" % pb_])
                    k = cc % 2
                    cu, us, tm = cub[k], usb[k], tmp[k]
                    P.op("act", lambda e, us=us, pU=pU: e.activation(us[:, 0:N], banks[pU][:, 0:N], AF.Copy),
                         reads=["bk%d" % pU], writes=["usb%d" % k])
                    P.op("pool", lambda e, cu=cu, cc=cc: e.tensor_copy(cu[:, 0:2], carry_cu[:, cc, :]),
                         reads=["carry_cu"], writes=["cubc%d" % k])
                    P.op("dve", lambda e, cu=cu, us=us, pC=pC: e.tensor_tensor(
                        cu[:, 2:2 + N], banks[pC][:, 0:N], us[:, 0:N], ALU.mult),
                        reads=["bk%d" % pC, "usb%d" % k], writes=["cub%d" % k])
                    P.op("act", lambda e, cu=cu, tm=tm, cc=cc: e.activation(
                        tm[:, 0:N], cu[:, 2:2 + N], AF.Copy, scale=cw[:, cc, 2:3]),
                        reads=["cub%d" % k, "cw"], writes=["tmp%d" % k])
                    conv_taps(tm, "tmp%d" % k, cu, ["cub%d" % k, "cubc%d" % k], cw[:, cc, :], N)
                    P.op("pool", lambda e, cu=cu, cc=cc: e.tensor_copy(carry_cu[:, cc, :], cu[:, N:N + 2]),
                         reads=["cub%d" % k, "cubc%d" % k], writes=["carry_cu"])
                    P.op("dve", lambda e, tm=tm, pB=pB, cc=cc: e.tensor_tensor(
                        bufB[:, cc, 0:N], banks[pB][:, 0:N], tm[:, 0:N], ALU.mult),
                        reads=["bk%d" % pB, "tmp%d" % k], writes=["bufB"])
                for oc in range(8):
                    wB, rB = load_w(CI_B + oc)
                    pGA, pGC, pYA, pYC = (psalloc(ring2) for _ in range(4))
                    mm_group(banks[pGA][:, 0:N], [(wB[:, kc, 256:384], bufA[:, kc, 0:N]) for kc in range(8)],
                             reads=[rB, "bufA"], writes=["bk%d" % pGA])
                    mm_group(banks[pGC][:, 0:N], [(wB[:, kc, 384:512], bufA[:, kc, 0:N]) for kc in range(8)],
                             reads=[rB, "bufA"], writes=["bk%d" % pGC])
                    mm_group(banks[pYA][:, 0:N], [(wB[:, kc, 0:128], attT[:, kc, pos0:pos0 + N]) for kc in range(8)],
                             reads=[rB, "attT"], writes=["bk%d" % pYA])
                    mm_group(banks[pYC][:, 0:N], [(wB[:, kc, 128:256], bufB[:, kc, 0:N]) for kc in range(8)],
                             reads=[rB, "bufB"], writes=["bk%d" % pYC])
                    k = oc % 2
                    s1, s2, tm = sg[k], usb[k], tmp[k]
                    P.op("act", lambda e, s1=s1, pGA=pGA: e.activation(s1[:, 0:N], banks[pGA][:, 0:N], AF.Sigmoid),
                         reads=["bk%d" % pGA], writes=["sg%d" % k])
                    P.op("act", lambda e, s2=s2, pGC=pGC: e.activation(s2[:, 0:N], banks[pGC][:, 0:N], AF.Sigmoid),
                         reads=["bk%d" % pGC], writes=["usb%d" % k])
                    P.op("dve", lambda e, s1=s1, pYA=pYA, tm=tm: e.tensor_tensor(
                        tm[:, 0:N], banks[pYA][:, 0:N], s1[:, 0:N], ALU.mult),
                        reads=["bk%d" % pYA, "sg%d" % k], writes=["tmp%d" % k])
                    P.op("dve", lambda e, s2=s2, pYC=pYC: e.tensor_tensor(
                        s2[:, 0:N], banks[pYC][:, 0:N], s2[:, 0:N], ALU.mult),
                        reads=["bk%d" % pYC, "usb%d" % k], writes=["usb%d" % k])
                    P.op("dve", lambda e, s2=s2, tm=tm, oc=oc: e.tensor_tensor(
                        bufC[:, oc, 0:N], tm[:, 0:N], s2[:, 0:N], ALU.add),
                        reads=["tmp%d" % k, "usb%d" % k], writes=["bufC"])
                if debug and bi == 0:
                    P.dma("pool", dbg_A, bufA[:], "dbgA", reads=["bufA"])
                    P.dma("pool", dbg_B, bufB[:], "dbgB", reads=["bufB"])
                    P.dma("pool", dbg_C, bufC[:], "dbgC", reads=["bufC"])
                wO = [load_w(CI_C + half) for half in range(2)]
                cst_ = {}
                for t in range(ntt + 1):
                    if t < ntt:
                        for half in range(2):
                            hs = slice(half * 512, (half + 1) * 512)
                            pb_ = psalloc(ring2)
                            mm_group(banks[pb_][0:rows, :],
                                     [(bufC[:, kc, t * rows:(t + 1) * rows], wO[half][0][:, kc, :]) for kc in range(8)],
                                     reads=[wO[half][1], "bufC"], writes=["bk%d" % pb_])
                            P.op("dve", lambda e, pb_=pb_, t=t, hs=hs: e.tensor_tensor(
                                Z[0:rows, t, hs], banks[pb_][0:rows, :], Z[0:rows, t, hs], ALU.add),
                                reads=["bk%d" % pb_, "Zt%d" % t], writes=["Zt%d" % t])
                    if t < ntt:
                        cst_[t] = norm_stats(Z[0:rows, t, :], rows, "Zt%d" % t)
                    if t > 0:
                        tp_ = t - 1
                        norm_finish(cst_[tp_], Z[0:rows, tp_, :], rows, bufA[:, :, tp_ * rows:(tp_ + 1) * rows],
                                    g3[:, 1, :], "Zt%d" % tp_, "bufA")
                if debug and bi == 0:
                    P.dma("pool", dbg_Z, Z[:], "dbgZ", reads=["Zt0", "Zt1", "Zt2", "Zt3"])
                for p_ in range(NFC // 2):
                    wD, rD = load_w(CI_D + p_)
                    fcs = (2 * p_, 2 * p_ + 1)
                    pA = [psalloc(ring2), psalloc(ring2)]
                    for i in range(2):
                        mm_group(banks[pA[i]][:, 0:N], [(wD[:, kc, i * 128:(i + 1) * 128], bufA[:, kc, 0:N])
                                                         for kc in range(8)],
                                 reads=[rD, "bufA"], writes=["bk%d" % pA[i]])
                    if not is_meta:
                        pV = [psalloc(ring2), psalloc(ring2)]
                        for i in range(2):
                            mm_group(banks[pV[i]][:, 0:N], [(wD[:, kc, 256 + i * 128:256 + (i + 1) * 128], bufA[:, kc, 0:N])
                                                             for kc in range(8)],
                                     reads=[rD, "bufA"], writes=["bk%d" % pV[i]])
                    for i in range(2):
                        fc = fcs[i]
                        ab, tm = cub[i], tmp[i]
                        P.op("pool", lambda e, ab=ab, fc=fc: e.tensor_copy(ab[:, 0:2], carry_a[:, fc, :]),
                             reads=["carry_a"], writes=["cubc%d" % i])
                        P.op("act", lambda e, ab=ab, pa=pA[i]: e.activation(ab[:, 2:2 + N], banks[pa][:, 0:N], AF.Copy),
                             reads=["bk%d" % pA[i]], writes=["cub%d" % i])
                        if not is_meta:
                            P.op("act", lambda e, tm=tm, pa=pA[i], fc=fc: e.activation(
                                tm[:, 0:N], banks[pa][:, 0:N], AF.Copy, scale=fcw[:, fc, 2:3]),
                                reads=["bk%d" % pA[i], "fcw"], writes=["tmp%d" % i])
                    if not is_meta:
                        for i in range(2):
                            fc = fcs[i]
                            ab, tm = cub[i], tmp[i]
                            P.op("dve", lambda e, tm=tm, ab=ab, fc=fc: e.scalar_tensor_tensor(
                                tm[:, 0:N], ab[:, 1:1 + N], fcw[:, fc, 1:2], tm[:, 0:N], ALU.mult, ALU.add),
                                reads=["cub%d" % i, "cubc%d" % i, "tmp%d" % i, "fcw"], writes=["tmp%d" % i])
                        for i in range(2):
                            fc = fcs[i]
                            ab, tm = cub[i], tmp[i]
                            P.op("dve", lambda e, tm=tm, ab=ab, fc=fc: e.scalar_tensor_tensor(
                                tm[:, 0:N], ab[:, 0:N], fcw[:, fc, 0:1], tm[:, 0:N], ALU.mult, ALU.add),
                                reads=["cub%d" % i, "cubc%d" % i, "tmp%d" % i, "fcw"], writes=["tmp%d" % i])
                    for i in range(2):
                        fc = fcs[i]
                        ab = cub[i]
                        P.op("pool", lambda e, ab=ab, fc=fc: e.tensor_copy(carry_a[:, fc, :], ab[:, N:N + 2]),
                             reads=["cub%d" % i, "cubc%d" % i], writes=["carry_a"])
                    if is_meta:
                        continue
                    for i in range(2):
                        tm, s1 = tmp[i], sg[i]
                        P.op("act", lambda e, s1=s1, tm=tm: e.activation(s1[:, 0:N], tm[:, 0:N], AF.Silu),
                             reads=["tmp%d" % i], writes=["sg%d" % i])
                    for i in range(2):
                        fc = fcs[i]
                        s1 = sg[i]
                        P.op("dve", lambda e, s1=s1, pv=pV[i], fc=fc: e.tensor_tensor(
                            gT[:, fc, 0:N], banks[pv][:, 0:N], s1[:, 0:N], ALU.mult),
                            reads=["bk%d" % pV[i], "sg%d" % i], writes=["gT"])
                if is_meta:
                    if nxt is not None:
                        do_norm1(nxt)
                    return
                if debug and bi == 0:
                    P.dma("pool", dbg_G, gT[:], "dbgG", reads=["gT"])
                for half in range(2):
                    hs = slice(half * 512, (half + 1) * 512)
                    pbs = [psalloc(ring2) for _ in range(4)]
                    for kg in range(3):
                        k0 = kg * 8
                        nk = min(8, NFC - k0)
                        wE, rE = load_w(CI_E + half * 3 + kg)
                        for t in range(4):
                            def dmm(e, t=t, k0=k0, nk=nk, wE=wE, pb_=pbs[t]):
                                ins = None
                                for kk in range(nk):
                                    ins = e.matmul(banks[pb_][:, :], lhsT=gT[:, k0 + kk, t * 128:(t + 1) * 128],
                                                   rhs=wE[:, kk, :], start=(k0 + kk == 0), stop=(k0 + kk == NFC - 1))
                                return ins
                            P.op("pe", dmm, reads=[rE, "gT"], writes=["bk%d" % pbs[t]])
                    for t in range(4):
                        P.op("dve", lambda e, t=t, hs=hs, pb_=pbs[t]: e.tensor_tensor(
                            Z[:, t, hs], banks[pb_][:, :], Z[:, t, hs], ALU.add),
                            reads=["bk%d" % pbs[t], "Zt%d" % t], writes=["Zt%d" % t])
                    if half == 0 and nxt is not None:
                        do_norm1(nxt)
                for t in range(4):
                    c = norm_stats(Z[:, t, :], 128, "Zt%d" % t)
                    o, ores = stg_next()
                    P.op("dve", lambda e, o=o, t=t, c=c: e.scalar_tensor_tensor(
                        o[:], Z[:, t, :], rs[:, c:c + 1], g3[:, 2, :], ALU.mult, ALU.mult),
                        reads=["Zt%d" % t, "rs%d" % c, "g3"], writes=[ores])
                    r0 = bi * 512 + t * 128
                    out_events.append(P.dma("pool", out[r0:r0 + 128, :], o[:], ores, reads=[ores]))

            seq = [-1] + list(range(p2_blocks))
            do_norm1(seq[0])
            for i, bi in enumerate(seq):
                block(bi, seq[i + 1] if i + 1 < len(seq) else None)
            P.barrier()
            P.wait_events("sp", out_events)
        P.emit()
    return nc


_NC_CACHE = {}


def _consts():
    s = np.arange(128)[:, None]
    t = np.arange(128)[None, :]
    c = np.zeros((128, 4, 128), np.float32)
    c[:, 0, :] = np.eye(128, dtype=np.float32)
    c[:, 1, :] = np.where(s > t, NEG, 0.0)
    c[:, 2, :] = (s <= t).astype(np.float32)
    c[:, 3, :] = 1.0
    return c


def make_in_maps(x, meta_tokens, g_mix, w_in, b_f, conv_w, w_o_attn, w_o_conv, w_o,
                 g_ffn, w_ffn_in, ffn_conv_w, w_ffn_out, g_final):
    f = lambda a: np.ascontiguousarray(np.asarray(a, dtype=np.float32))
    g3 = np.stack([f(g_mix)[0], f(g_ffn)[0], f(g_final)], 0)
    g3 = np.ascontiguousarray(np.broadcast_to(g3[None], (128, 3, D)))
    bfrep = np.ascontiguousarray(np.broadcast_to(f(b_f)[0][None, None, :], (128, NT, 8)).reshape(128, NT * 8))
    cw = np.ascontiguousarray(f(conv_w)[0].T.reshape(8, 128, 3).transpose(1, 0, 2))
    fcw = np.ascontiguousarray(f(ffn_conv_w)[0].T.reshape(NFC, 128, 3).transpose(1, 0, 2))
    shared = {
        "meta": f(meta_tokens), "w_in": f(w_in)[0], "w_o_attn": f(w_o_attn)[0], "w_o_conv": f(w_o_conv)[0],
        "w_o": f(w_o)[0], "w_ffn_in": f(w_ffn_in)[0], "w_ffn_out": f(w_ffn_out)[0],
        "g3": g3, "bfrep": bfrep, "cw": cw, "fcw": fcw, "cst": _consts(),
    }
    xs_ = f(x)
    return [dict(shared, x=xs_[b]) for b in range(xs_.shape[0])]


def kernel(**inputs):
    in_maps = make_in_maps(**inputs)
    if "nc" not in _NC_CACHE:
        _NC_CACHE["nc"] = build_nc()
    res = run_bass_kernel_spmd(_NC_CACHE["nc"], in_maps, core_ids=list(range(8)))
    return np.stack([r["out"] for r in res.results], 0).astype(np.float32)
```

```python
import contextlib
import numpy as np
import concourse.bass as bass
import concourse.mybir as mybir
from concourse.bass_utils import run_bass_kernel_spmd

F32 = mybir.dt.float32
BF16 = mybir.dt.bfloat16
AF = mybir.ActivationFunctionType
ALU = mybir.AluOpType

D = 1024
SEQ = 4096
NMETA = 16
L = SEQ + NMETA
NT = 33
NPOS = NT * 128
DFF = 2816
NFC = DFF // 128
DIN = 8200
KOFF, VOFF, FOFF, BOFF, COFF, UOFF, GAOFF, GCOFF = 1024, 2048, 3072, 3080, 4104, 5128, 6152, 7176
SCALE = float(128 ** -0.5)
EPS = 1e-6
NEG = -30000.0

ENGS = ("pe", "act", "dve", "pool", "sp")


class Prog:
    def __init__(self, nc, same_engine_sync=True):
        self.nc = nc
        self.ops = {e: [] for e in ENGS}
        self.count = {}
        self.waited = {e: {} for e in ENGS}
        self.last_write = {}
        self.readers = {}
        self.same_engine_sync = same_engine_sync
        self.dma_keys = []

    def _deps(self, eng, reads, writes):
        evs = []
        for r in reads:
            e = self.last_write.get(r)
            if e is not None:
                evs.append(e)
        for r in writes:
            e = self.last_write.get(r)
            if e is not None:
                evs.append(e)
            evs.extend(self.readers.get(r, ()))
        best = {}
        w = self.waited[eng]
        for (k, v) in evs:
            if k == ("eng", eng) and (eng == "pe" or not self.same_engine_sync):
                continue
            if w.get(k, 0) >= v:
                continue
            best[k] = max(best.get(k, 0), v)
        for k, v in best.items():
            w[k] = v
        return list(best.items())

    def _commit(self, ev, reads, writes):
        for r in reads:
            self.readers.setdefault(r, []).append(ev)
        for r in writes:
            self.last_write[r] = ev
            self.readers[r] = []

    def op(self, eng, fn, reads=(), writes=()):
        waits = self._deps(eng, reads, writes)
        k = ("eng", eng)
        self.count[k] = self.count.get(k, 0) + 1
        ev = (k, self.count[k])
        self.ops[eng].append((fn, waits, k, 1))
        self._commit(ev, reads, writes)
        return ev

    def dma(self, eng, out, in_, key, reads=(), writes=()):
        waits = self._deps(eng, reads, writes)
        k = ("dma", key)
        if k not in self.count:
            self.dma_keys.append(k)
        self.count[k] = self.count.get(k, 0) + 16
        ev = (k, self.count[k])
        fn = lambda e, out=out, in_=in_: e.dma_start(out=out, in_=in_)
        self.ops[eng].append((fn, waits, k, 16))
        self._commit(ev, reads, writes)
        return ev

    def wait_events(self, eng, events):
        waits = []
        for (k, v) in events:
            if self.waited[eng].get(k, 0) < v:
                self.waited[eng][k] = v
                waits.append((k, v))
        if waits:
            self.ops[eng].append((None, waits, None, 0))

    def barrier(self):
        evs = [(k, v) for k, v in self.count.items() if v > 0]
        for e in ENGS:
            self.wait_events(e, [(k, v) for (k, v) in evs if not (k == ("eng", e) and e in ("pe", "sp"))])
        self.last_write = {}
        self.readers = {}

    def emit(self):
        nc = self.nc
        with contextlib.ExitStack() as st:
            sems = {}
            keys = [("eng", e) for e in ENGS] + self.dma_keys
            for i, k in enumerate(keys):
                if self.count.get(k, 0) == 0:
                    continue
                sems[k] = st.enter_context(nc.semaphore("s%d" % i))
            block = st.enter_context(nc.Block())

            def run(eng_name):
                def body(e):
                    for (fn, waits, k, n) in self.ops[eng_name]:
                        for (wk, wv) in waits:
                            e.wait_ge(sems[wk], wv)
                        if fn is None:
                            continue
                        fn(e).then_inc(sems[k], n)
                return body

            block.tensor(run("pe"))
            block.scalar(run("act"))
            block.vector(run("dve"))
            block.gpsimd(run("pool"))
            block.sync(run("sp"))


def build_nc(debug=False, p2_blocks=8):
    nc = bass.Bass("TRN2", target_bir_lowering=False)
    dt_in = lambda name, shape: nc.dram_tensor(name, shape, F32, kind="ExternalInput").ap()
    x = dt_in("x", [SEQ, D])
    meta = dt_in("meta", [NMETA, D])
    w_in = dt_in("w_in", [D, DIN])
    w_oa = dt_in("w_o_attn", [D, D])
    w_oc = dt_in("w_o_conv", [D, D])
    w_o = dt_in("w_o", [D, D])
    w_fi = dt_in("w_ffn_in", [D, 2 * DFF])
    w_fo = dt_in("w_ffn_out", [DFF, D])
    g3_d = dt_in("g3", [128, 3, D])
    bf_d = dt_in("bfrep", [128, NT * 8])
    cw_d = dt_in("cw", [128, 8, 3])
    fcw_d = dt_in("fcw", [128, NFC, 3])
    cst_d = dt_in("cst", [128, 4, 128])
    out = nc.dram_tensor("out", [SEQ, D], F32, kind="ExternalOutput").ap()
    if debug:
        dbg_att = nc.dram_tensor("dbg_att", [128, 8, L], F32, kind="ExternalOutput").ap()
        dbg_c = nc.dram_tensor("dbg_c", [128, NT * 8], F32, kind="ExternalOutput").ap()
        dbg_B = nc.dram_tensor("dbg_B", [128, 8, 512], BF16, kind="ExternalOutput").ap()
        dbg_C = nc.dram_tensor("dbg_C", [128, 8, 512], BF16, kind="ExternalOutput").ap()
        dbg_A = nc.dram_tensor("dbg_A", [128, 8, 512], BF16, kind="ExternalOutput").ap()
        dbg_G = nc.dram_tensor("dbg_G", [128, NFC, 512], BF16, kind="ExternalOutput").ap()
        dbg_Z = nc.dram_tensor("dbg_Z", [128, 4, D], F32, kind="ExternalOutput").ap()

    kview = lambda w: w.rearrange("(kc p) c -> p kc c", p=128)
    w_in_v, w_oa_v, w_oc_v, w_o_v, w_fi_v, w_fo_v = map(kview, (w_in, w_oa, w_oc, w_o, w_fi, w_fo))

    CH = []
    for cc in range(8):
        CH.append((8, 384, [(i * 128, w_in_v[:, :, o_ + cc * 128:o_ + (cc + 1) * 128], 128)
                            for i, o_ in enumerate((BOFF, COFF, UOFF))]))
    for oc in range(8):
        cs_ = slice(oc * 128, (oc + 1) * 128)
        CH.append((8, 512, [(0, w_oa_v[:, :, cs_], 128), (128, w_oc_v[:, :, cs_], 128),
                            (256, w_in_v[:, :, GAOFF + oc * 128:GAOFF + (oc + 1) * 128], 128),
                            (384, w_in_v[:, :, GCOFF + oc * 128:GCOFF + (oc + 1) * 128], 128)]))
    for half in range(2):
        CH.append((8, 512, [(0, w_o_v[:, :, half * 512:(half + 1) * 512], 512)]))
    for p_ in range(NFC // 2):
        CH.append((8, 512, [(0, w_fi_v[:, :, p_ * 256:(p_ + 1) * 256], 256),
                            (256, w_fi_v[:, :, DFF + p_ * 256:DFF + (p_ + 1) * 256], 256)]))
    for half in range(2):
        for kg in range(3):
            k0_ = kg * 8
            nk_ = min(8, NFC - k0_)
            CH.append((nk_, 512, [(0, w_fo_v[:, k0_:k0_ + nk_, half * 512:(half + 1) * 512], 512)]))
    NCHUNK = len(CH)
    assert NCHUNK == 35
    CI_A, CI_B, CI_C, CI_D, CI_E = 0, 8, 16, 18, 29
    wscr = nc.dram_tensor("wscr", [NCHUNK, 128, 8, 512], BF16, kind="Internal").ap()
    conv_state = {"i": 0, "n": 0}

    P = Prog(nc)

    def convert_chunks(n):
        for _ in range(n):
            ci = conv_state["i"]
            if ci >= NCHUNK:
                return
            conv_state["i"] += 1
            nk, ncols, pieces = CH[ci]
            for (c0, src, w) in pieces:
                conv_state["n"] += 1
                P.dma("pool", wscr[ci][:, 0:nk, c0:c0 + w], src, "cv%d" % (conv_state["n"] % 4), writes=["scr%d" % conv_state["n"]])

    with contextlib.ExitStack() as st0:
        T0 = lambda name, shape, dt: st0.enter_context(nc.sbuf_tensor("sb_" + name, shape, dt))
        attT = T0("attT", [128, 8, L], BF16)
        cstb = T0("cstb", [128, 4, 128], BF16)
        g3 = T0("g3", [128, 3, D], F32)
        ss = T0("ss", [128, 8], F32)
        rs = T0("rs", [128, 8], F32)
        xs = [T0("xs%d" % i, [128, D], BF16) for i in range(2)]
        junk = T0("junk", [128, D], BF16)
        ident, negmask, uincl, ones = (cstb[:, i, :] for i in range(4))
        TR = {"banks": None}
        BK = {"banks": None}

        with nc.sbuf_tensor("sb_cstf", [128, 4, 128], F32) as cstf:
            P.dma("sp", cstf[:], cst_d, "cst", writes=["cstf"])
            P.op("dve", lambda e: e.tensor_copy(cstb[:], cstf[:]), reads=["cstf"], writes=["cst"])
            P.dma("sp", g3[:], g3_d, "g3", writes=["g3"])
            P.barrier()

        ring_state = {"i": 0}

        def psalloc(ring):
            i = ring[ring_state["i"] % len(ring)]
            ring_state["i"] += 1
            return i

        evac_state = {"i": 0}

        def copy_any(out_ap, in_ap, reads, writes):
            evac_state["i"] += 1
            mode = evac_state.get("mode", "both")
            if mode == "act" or (mode == "both" and evac_state["i"] % 2):
                P.op("act", lambda e: e.activation(out_ap, in_ap, AF.Copy), reads=reads, writes=writes)
            else:
                P.op("dve", lambda e: e.tensor_copy(out_ap, in_ap), reads=reads, writes=writes)

        def mm_group(out_ap, pairs, reads, writes, first_start=True):
            def fn(e):
                n = len(pairs)
                ins = None
                for i, (l, r) in enumerate(pairs):
                    ins = e.matmul(out_ap, lhsT=l, rhs=r, start=(first_start and i == 0), stop=(i == n - 1))
                return ins
            P.op("pe", fn, reads=reads, writes=writes)

        norm_ctr = {"i": 0}

        def norm_stats(src, rows, src_res):
            i = norm_ctr["i"]
            norm_ctr["i"] += 1
            c = i % 8
            P.op("dve", lambda e: e.memset(ss[:, c:c + 1], 0.0), writes=["ss%d" % c])
            P.op("act", lambda e: e.activation(junk[0:rows, :], src, AF.Square, accum_out=ss[0:rows, c:c + 1]),
                 reads=[src_res], writes=["ss%d" % c, "junk"])
            P.op("act", lambda e: e.activation(rs[:, c:c + 1], ss[:, c:c + 1], AF.Ln, bias=EPS, scale=1.0 / D),
                 reads=["ss%d" % c], writes=["rs%d" % c])
            P.op("act", lambda e: e.activation(rs[:, c:c + 1], rs[:, c:c + 1], AF.Exp, scale=-0.5),
                 reads=["rs%d" % c], writes=["rs%d" % c])
            return c

        fin_ctr = {"i": 0}

        def norm_finish(c, src, rows, dst, grow, src_res, dst_res):
            i = fin_ctr["i"]
            fin_ctr["i"] += 1
            xi = i % 2
            xb = xs[xi]
            P.op("dve", lambda e: e.scalar_tensor_tensor(
                xb[0:rows, :], src, rs[0:rows, c:c + 1], grow[0:rows, :], ALU.mult, ALU.mult),
                reads=[src_res, "rs%d" % c, "g3"], writes=["xs%d" % xi])
            nb_ = len(TR["banks"])
            ti = i % nb_
            trb = TR["banks"][ti]

            def tr(e):
                ins = None
                for kc in range(8):
                    ins = e.transpose(trb[:, kc, 0:rows], xb[0:rows, kc * 128:(kc + 1) * 128], ident[0:rows, 0:rows])
                return ins
            P.op("pe", tr, reads=["xs%d" % xi, "cst"], writes=["tr%d" % ti])
            copy_any(dst, trb[:, :, 0:rows], reads=["tr%d" % ti], writes=[dst_res])

        def norm_tile(src, rows, dst, grow, src_res, dst_res):
            c = norm_stats(src, rows, src_res)
            norm_finish(c, src, rows, dst, grow, src_res, dst_res)

        with contextlib.ExitStack() as st1:
            T1 = lambda name, shape, dt: st1.enter_context(nc.sbuf_tensor("sb_" + name, shape, dt))
            hT = T1("hT", [128, 8, NPOS], BF16)
            cpos = T1("cpos", [128, NT * 8], F32)
            off = T1("off", [128, (NT + 1) * 8], F32)
            TR["banks"] = [st1.enter_context(nc.psum_tensor("tr_ps", [128, 8, 128], BF16))]
            banks = [st1.enter_context(nc.psum_tensor("bank%d" % i, [128, 512], F32)) for i in range(7)]
            ring1 = [0, 1, 2, 3]
            O_bs, L_b = [banks[4], banks[5]], banks[6]

            with contextlib.ExitStack() as st:
                Ts = lambda name, shape, dt: st.enter_context(nc.sbuf_tensor("sb_" + name, shape, dt))
                xt = [Ts("xt%d" % i, [128, 4, D], F32) for i in range(2)]
                P.op("pool", lambda e: e.memset(hT[:, :, L:NPOS], 0.0), writes=["hT"])
                P.dma("sp", xt[0][0:NMETA, 0, :], meta, "xt0", writes=["xt0"])
                norm_tile(xt[0][0:NMETA, 0, :], NMETA, hT[:, :, 0:NMETA], g3[:, 0, :], "xt0", "hT")
                for b in range(8):
                    xb_ = xt[(b + 1) % 2]
                    res = "xt%d" % ((b + 1) % 2)
                    P.dma("sp", xb_[:], x[b * 512:(b + 1) * 512, :].rearrange("(t p) d -> p t d", p=128),
                          res, writes=[res])
                    cs4 = [norm_stats(xb_[:, t, :], 128, res) for t in range(4)]
                    for t in range(4):
                        p0 = NMETA + b * 512 + t * 128
                        norm_finish(cs4[t], xb_[:, t, :], 128, hT[:, :, p0:p0 + 128], g3[:, 0, :], res, "hT")
            P.barrier()

            with contextlib.ExitStack() as st:
                Ts = lambda name, shape, dt: st.enter_context(nc.sbuf_tensor("sb_" + name, shape, dt))
                wf = Ts("wf", [128, 8, 8], BF16)
                bfr = Ts("bfr", [128, NT * 8], F32)
                fb = Ts("fb", [128, NT * 8], F32)
                r1 = Ts("r1", [128, NT * 8], F32)
                parts = [Ts("part%d" % i, [128, NT * 8], BF16) for i in range(3)]
                tot = Ts("tot", [128, NT * 8], F32)
                P.dma("pool", wf[:], w_in_v[:, :, FOFF:FOFF + 8], "wf", writes=["wf"])
                P.dma("sp", bfr[:], bf_d, "bfr", writes=["bfr"])
                psF, psC, psT = banks[0], banks[1], banks[2]

                def fmm(e):
                    ins = None
                    for j in range(NT):
                        for kc in range(8):
                            ins = e.matmul(psF[:, j * 8:(j + 1) * 8], lhsT=hT[:, kc, j * 128:(j + 1) * 128],
                                           rhs=wf[:, kc, :], start=(kc == 0), stop=(kc == 7))
                    return ins
                P.op("pe", fmm, reads=["hT", "wf"], writes=["bank0"])
                NF = NT * 8
                P.op("dve", lambda e: e.tensor_tensor(fb[:], psF[:, 0:NF], bfr[:], ALU.add),
                     reads=["bank0", "bfr"], writes=["fb"])
                P.op("act", lambda e: e.activation(fb[:], fb[:], AF.Exp, scale=-1.0), reads=["fb"], writes=["fb"])
                P.op("act", lambda e: e.activation(fb[:], fb[:], AF.Ln, bias=1.0), reads=["fb"], writes=["fb"])
                P.op("dve", lambda e: e.tensor_copy(parts[0][:], fb[:]), reads=["fb"], writes=["p0"])
                P.op("dve", lambda e: e.tensor_tensor(r1[:], fb[:], parts[0][:], ALU.subtract),
                     reads=["fb", "p0"], writes=["r1"])
                P.op("dve", lambda e: e.tensor_copy(parts[1][:], r1[:]), reads=["r1"], writes=["p1"])
                P.op("dve", lambda e: e.tensor_tensor(r1[:], r1[:], parts[1][:], ALU.subtract),
                     reads=["r1", "p1"], writes=["r1"])
                P.op("dve", lambda e: e.tensor_copy(parts[2][:], r1[:]), reads=["r1"], writes=["p2"])
                mm_group(psC[:, 0:NF], [(uincl, parts[i][:]) for i in range(3)],
                         reads=["cst", "p0", "p1", "p2"], writes=["bank1"])
                mm_group(psT[:, 0:NF], [(ones, parts[i][:]) for i in range(3)],
                         reads=["cst", "p0", "p1", "p2"], writes=["bank2"])
                P.op("dve", lambda e: e.tensor_copy(tot[:], psT[:, 0:NF]), reads=["bank2"], writes=["tot"])
                P.op("dve", lambda e: e.memset(off[:, 0:8], 0.0), writes=["off"])
                for j in range(1, NT + 1):
                    P.op("dve", lambda e, j=j: e.tensor_tensor(off[:, j * 8:(j + 1) * 8], off[:, (j - 1) * 8:j * 8],
                                                               tot[:, (j - 1) * 8:j * 8], ALU.add),
                         reads=["off", "tot"], writes=["off"])
                P.op("dve", lambda e: e.tensor_tensor(cpos[:], psC[:, 0:NF], off[:, 0:NF], ALU.add),
                     reads=["bank1", "off"], writes=["cpos"])
                if debug:
                    P.dma("sp", dbg_c, cpos[:], "dbgc", reads=["cpos"])
            P.barrier()

            with contextlib.ExitStack() as st:
                Ts = lambda name, shape, dt: st.enter_context(nc.sbuf_tensor("sb_" + name, shape, dt))
                qT = Ts("qT", [128, NPOS], BF16)
                kT = Ts("kT", [128, NPOS], BF16)
                Vt = Ts("Vt", [128, NPOS], BF16)
                wqkv = [Ts("wqkv%d" % i, [128, 8, 3, 128], BF16) for i in range(2)]
                Bh = [Ts("Bh%d" % i, [128, 17, 34], F32) for i in range(2)]
                NPT = 5
                PT = [Ts("PT%d" % i, [128, 512], BF16) for i in range(NPT)]
                rl = [Ts("rl%d" % i, [128, 512], F32) for i in range(2)]
                acc = [Ts("acc%d" % i, [128, 512], F32) for i in range(2)]
                ahi = Ts("ahi", [128, 512], BF16)
                alo = Ts("alo", [128, 512], BF16)
                cpos3 = cpos[:].rearrange("p (j h) -> p j h", h=8)
                evac_state["mode"] = "dve"
                pt_i = 0
                rl_i = 0
                for h in range(8):
                    wq = wqkv[h % 2]
                    wres = "wqkv%d" % (h % 2)
                    for t in range(3):
                        P.dma("pool", wq[:, :, t, :], w_in_v[:, :, t * 1024 + h * 128:t * 1024 + (h + 1) * 128],
                              wres + "_%d" % t, writes=[wres])
                    convert_chunks(6)
                    bh = Bh[h % 2]
                    bres = "Bh%d" % (h % 2)
                    for m in range(17):
                        jn = min(2 * m + 2, NT)
                        P.op("dve", lambda e, m=m, jn=jn, bh=bh, h=h: e.tensor_scalar(
                            bh[:, m, 0:jn], cpos3[:, 0:jn, h], off[:, (2 * m + 1) * 8 + h:(2 * m + 1) * 8 + h + 1],
                            None, ALU.subtract), reads=["cpos", "off"], writes=[bres])
                    for (t, dst, dres) in ((0, qT, "qT"), (1, kT, "kT")):
                        for n in range(9):
                            c0 = n * 512
                            w = min(512, NPOS - c0)
                            b = psalloc(ring1)
                            mm_group(banks[b][:, 0:w], [(wq[:, kc, t, :], hT[:, kc, c0:c0 + w]) for kc in range(8)],
                                     reads=[wres, "hT"], writes=["bank%d" % b])
                            copy_any(dst[:, c0:c0 + w], banks[b][:, 0:w], reads=["bank%d" % b], writes=[dres])
                    for j4 in range(0, NT, 4):
                        nj = min(4, NT - j4)
                        b = psalloc(ring1)

                        def vmm(e, j4=j4, nj=nj, bk=banks[b], wq=wq):
                            ins = None
                            for jj in range(nj):
                                j = j4 + jj
                                for kc in range(8):
                                    ins = e.matmul(bk[:, jj * 128:(jj + 1) * 128],
                                                   lhsT=hT[:, kc, j * 128:(j + 1) * 128], rhs=wq[:, kc, 2, :],
                                                   start=(kc == 0), stop=(kc == 7))
                            return ins
                        P.op("pe", vmm, reads=[wres, "hT"], writes=["bank%d" % b])
                        copy_any(Vt[:, j4 * 128:(j4 + nj) * 128], banks[b][:, 0:nj * 128],
                                 reads=["bank%d" % b], writes=["Vt"])
                    steps = []
                    for qb in range(9):
                        q0 = qb * 512
                        qend = min(q0 + 512, L)
                        jlast = (qend - 1) // 128
                        for j in range(jlast + 1):
                            steps.append((qb, q0, qend, jlast, j))
                    LA = 3
                    infl = {}

                    def issue_qk(i):
                        nonlocal pt_i
                        (qb, q0, qend, jlast, j) = steps[i]
                        c0 = max(q0, 128 * j)
                        ncols = qend - c0
                        diag = (128 * j >= q0)
                        b = psalloc(ring1)
                        S = banks[b]

                        def smm(e, j=j, c0=c0, ncols=ncols, diag=diag, S=S):
                            kt = kT[:, j * 128:(j + 1) * 128]
                            if not diag:
                                return e.matmul(S[:, 0:ncols], lhsT=kt, rhs=qT[:, c0:c0 + ncols],
                                                start=True, stop=True)
                            wd = min(128, ncols)
                            e.matmul(S[:, 0:wd], lhsT=ident, rhs=negmask[:, 0:wd], start=True, stop=False,
                                     skip_group_check=True)
                            ins = e.matmul(S[:, 0:wd], lhsT=kt, rhs=qT[:, c0:c0 + wd], start=False, stop=True,
                                           skip_group_check=True)
                            if ncols > wd:
                                ins = e.matmul(S[:, wd:ncols], lhsT=kt, rhs=qT[:, c0 + wd:c0 + ncols],
                                               start=False, stop=True, skip_group_check=True)
                            return ins
                        P.op("pe", smm, reads=["qT", "kT", "cst"], writes=["bank%d" % b])
                        pt = PT[pt_i % NPT]
                        pres = "PT%d" % (pt_i % NPT)
                        pt_i += 1
                        m_lo, m_hi = c0 // 256, (qend - 1) // 256
                        for m in range(m_lo, m_hi + 1):
                            a = max(c0, 256 * m) - c0
                            bnd = min(qend, 256 * m + 256) - c0
                            P.op("act", lambda e, pt=pt, S=S, a=a, bnd=bnd, bh=bh, m=m, j=j: e.activation(
                                pt[:, a:bnd], S[:, a:bnd], AF.Exp, bias=bh[:, m, j:j + 1], scale=SCALE),
                                reads=["bank%d" % b, bres], writes=[pres])
                        ac = acc[qb % 2]
                        ares = "acc%d" % (qb % 2)
                        o0_ = c0 - q0
                        if j == 0:
                            P.op("dve", lambda e, ac=ac, pt=pt, ncols=ncols: e.tensor_copy(ac[:, 0:ncols], pt[:, 0:ncols]),
                                 reads=[pres], writes=[ares])
                        else:
                            P.op("dve", lambda e, ac=ac, pt=pt, ncols=ncols, o0_=o0_: e.tensor_tensor(
                                ac[:, o0_:o0_ + ncols], ac[:, o0_:o0_ + ncols], pt[:, 0:ncols], ALU.add),
                                reads=[pres, ares], writes=[ares])
                        infl[i] = (pt, pres, c0, ncols)

                    def issue_pv(i):
                        nonlocal rl_i
                        (qb, q0, qend, jlast, j) = steps[i]
                        (pt, pres, c0, ncols) = infl.pop(i)
                        o0 = c0 - q0
                        nq = qend - q0

                        O_b = O_bs[qb % 2]
                        ores_ = "bank%d" % (4 + qb % 2)

                        def pvmm(e, j=j, o0=o0, ncols=ncols, pt=pt, jlast=jlast, O_b=O_b):
                            return e.matmul(O_b[:, o0:o0 + ncols], lhsT=Vt[:, j * 128:(j + 1) * 128], rhs=pt[:, 0:ncols],
                                            start=(j == 0), stop=(j == jlast), skip_group_check=True)
                        P.op("pe", pvmm, reads=[pres, "Vt"], writes=[ores_])
                        if j == jlast:
                            ac = acc[qb % 2]
                            ares = "acc%d" % (qb % 2)
                            P.op("dve", lambda e, ac=ac, nq=nq: e.tensor_copy(ahi[:, 0:nq], ac[:, 0:nq]),
                                 reads=[ares], writes=["ahi"])
                            P.op("dve", lambda e, ac=ac, nq=nq: e.tensor_tensor(
                                alo[:, 0:nq], ac[:, 0:nq], ahi[:, 0:nq], ALU.subtract),
                                reads=[ares, "ahi"], writes=["alo"])
                            mm_group(L_b[:, 0:nq], [(ones, ahi[:, 0:nq]), (ones, alo[:, 0:nq])],
                                     reads=["cst", "ahi", "alo"], writes=["bank6"])
                            r = rl[rl_i % 2]
                            rres = "rl%d" % (rl_i % 2)
                            rl_i += 1
                            P.op("dve", lambda e, r=r, nq=nq: e.reciprocal(r[:, 0:nq], L_b[:, 0:nq]),
                                 reads=["bank6"], writes=[rres])
                            P.op("dve", lambda e, r=r, nq=nq, h=h, q0=q0, O_b=O_b: e.tensor_tensor(
                                attT[:, h, q0:q0 + nq], O_b[:, 0:nq], r[:, 0:nq], ALU.mult),
                                reads=[ores_, rres], writes=["attT"])

                    for i in range(len(steps) + LA):
                        if i < len(steps):
                            issue_qk(i)
                        if i >= LA:
                            issue_pv(i - LA)
            P.barrier()

        if debug:
            with contextlib.ExitStack() as st:
                dbt = st.enter_context(nc.sbuf_tensor("dbt", [128, 4, L], F32))
                for hh in range(2):
                    P.op("dve", lambda e, hh=hh: e.tensor_copy(dbt[:], attT[:, hh * 4:(hh + 1) * 4, :]),
                         reads=["attT"], writes=["dbt"])
                    P.dma("sp", dbg_att[:, hh * 4:(hh + 1) * 4, :], dbt[:], "dbga", reads=["dbt"])
                P.barrier()

        with contextlib.ExitStack() as st:
            Ts = lambda name, shape, dt: st.enter_context(nc.sbuf_tensor("sb_" + name, shape, dt))
            TR["banks"] = [st.enter_context(nc.psum_tensor("tr2_%d" % i, [128, 8, 128], BF16)) for i in range(2)]
            banks = [st.enter_context(nc.psum_tensor("bk2_%d" % i, [128, 512], F32)) for i in range(6)]
            ring2 = [0, 1, 2, 3, 4, 5]
            evac_state["mode"] = "both"
            Z = Ts("Z", [128, 4, D], F32)
            bufA = Ts("bufA", [128, 8, 512], BF16)
            bufB = Ts("bufB", [128, 8, 512], BF16)
            bufC = Ts("bufC", [128, 8, 512], BF16)
            gT = Ts("gT", [128, NFC, 512], BF16)
            NB = 4
            wr = [Ts("wr%d" % i, [128, 8, 512], BF16) for i in range(NB)]
            cub = [Ts("cub%d" % i, [128, 514], F32) for i in range(2)]
            usb = [Ts("usb%d" % i, [128, 512], F32) for i in range(2)]
            tmp = [Ts("tmp%d" % i, [128, 512], F32) for i in range(2)]
            sg = [Ts("sg%d" % i, [128, 512], F32) for i in range(2)]
            NSTG = 3
            stg = [Ts("stg%d" % i, [128, D], F32) for i in range(NSTG)]
            cw = Ts("cw", [128, 8, 3], F32)
            fcw = Ts("fcw", [128, NFC, 3], F32)
            carry_cu = Ts("carry_cu", [128, 8, 2], F32)
            carry_a = Ts("carry_a", [128, NFC, 2], F32)
            P.dma("pool", cw[:], cw_d, "cw", writes=["cw"])
            P.dma("pool", fcw[:], fcw_d, "fcw", writes=["fcw"])
            P.op("dve", lambda e: e.memset(carry_cu[:], 0.0), writes=["carry_cu"])
            P.op("dve", lambda e: e.memset(carry_a[:], 0.0), writes=["carry_a"])

            wstate = {"i": 0}

            def load_w(ci):
                s_ = wstate["i"] % NB
                wstate["i"] += 1
                nk, ncols, _ = CH[ci]
                P.dma("sp", wr[s_][:, 0:nk, 0:ncols], wscr[ci][:, 0:nk, 0:ncols], "wr%d" % s_, writes=["wr%d" % s_])
                return wr[s_], "wr%d" % s_

            stg_state = {"i": 0}

            def stg_next():
                i = stg_state["i"] % NSTG
                stg_state["i"] += 1
                return stg[i], "stg%d" % i

            out_events = []

            def conv_taps(tm, tres, src, sres, wcol, N):
                P.op("dve", lambda e: e.scalar_tensor_tensor(tm[:, 0:N], src[:, 1:1 + N], wcol[:, 1:2],
                                                             tm[:, 0:N], ALU.mult, ALU.add),
                     reads=list(sres) + [tres, "cw", "fcw"], writes=[tres])
                P.op("dve", lambda e: e.scalar_tensor_tensor(tm[:, 0:N], src[:, 0:N], wcol[:, 0:1],
                                                             tm[:, 0:N], ALU.mult, ALU.add),
                     reads=list(sres) + [tres, "cw", "fcw"], writes=[tres])

            def do_norm1(k):
                if k < 0:
                    sb, sres = stg_next()
                    P.dma("pool", sb[0:NMETA, :], meta, sres, writes=[sres])
                    norm_tile(sb[0:NMETA, :], NMETA, bufA[:, :, 0:NMETA], g3[:, 0, :], sres, "bufA")
                    return
                pend = []
                for t in range(4):
                    if t == NSTG:
                        norm_finish(*pend.pop(0))
                    sb, sres = stg_next()
                    r0 = k * 512 + t * 128
                    P.dma("pool", sb[:], x[r0:r0 + 128, :], sres, writes=[sres])
                    c = norm_stats(sb[:], 128, sres)
                    pend.append((c, sb[:], 128, bufA[:, :, t * 128:(t + 1) * 128], g3[:, 0, :], sres, "bufA"))
                for a_ in pend:
                    norm_finish(*a_)

            def block(bi, nxt):
                is_meta = bi < 0
                N = NMETA if is_meta else 512
                ntt = 1 if is_meta else 4
                rows = NMETA if is_meta else 128
                pos0 = 0 if is_meta else NMETA + bi * 512
                if is_meta:
                    P.dma("pool", Z[0:NMETA, 0, :], meta, "Z0", writes=["Zt0"])
                else:
                    for t in range(4):
                        r0 = bi * 512 + t * 128
                        P.dma("pool", Z[:, t, :], x[r0:r0 + 128, :], "Z%d" % t, writes=["Zt%d" % t])
                for cc in range(8):
                    wA, rA = load_w(CI_A + cc)
                    pU, pC, pB = psalloc(ring2), psalloc(ring2), psalloc(ring2)
                    for (pb_, c0) in ((pU, 256), (pC, 128), (pB, 0)):
                        mm_group(banks[pb_][:, 0:N], [(wA[:, kc, c0:c0 + 128], bufA[:, kc, 0:N]) for kc in range(8)],
                                 reads=[rA, "bufA"], writes=["bk%d" % pb_])
                    k = cc % 2
                    cu, us, tm = cub[k], usb[k], tmp[k]
                    P.op("act", lambda e, us=us, pU=pU: e.activation(us[:, 0:N], banks[pU][:, 0:N], AF.Copy),
                         reads=["bk%d" % pU], writes=["usb%d" % k])
                    P.op("pool", lambda e, cu=cu, cc=cc: e.tensor_copy(cu[:, 0:2], carry_cu[:, cc, :]),
                         reads=["carry_cu"], writes=["cubc%d" % k])
                    P.op("dve", lambda e, cu=cu, us=us, pC=pC: e.tensor_tensor(
                        cu[:, 2:2 + N], banks[pC][:, 0:N], us[:, 0:N], ALU.mult),
                        reads=["bk%d" % pC, "usb%d" % k], writes=["cub%d" % k])
                    P.op("act", lambda e, cu=cu, tm=tm, cc=cc: e.activation(
                        tm[:, 0:N], cu[:, 2:2 + N], AF.Copy, scale=cw[:, cc, 2:3]),
                        reads=["cub%d" % k, "cw"], writes=["tmp%d" % k])
                    conv_taps(tm, "tmp%d" % k, cu, ["cub%d" % k, "cubc%d" % k], cw[:, cc, :], N)
                    P.op("pool", lambda e, cu=cu, cc=cc: e.tensor_copy(carry_cu[:, cc, :], cu[:, N:N + 2]),
                         reads=["cub%d" % k, "cubc%d" % k], writes=["carry_cu"])
                    P.op("dve", lambda e, tm=tm, pB=pB, cc=cc: e.tensor_tensor(
                        bufB[:, cc, 0:N], banks[pB][:, 0:N], tm[:, 0:N], ALU.mult),
                        reads=["bk%d" % pB, "tmp%d" % k], writes=["bufB"])
                for oc in range(8):
                    wB, rB = load_w(CI_B + oc)
                    pGA, pGC, pYA, pYC = (psalloc(ring2) for _ in range(4))
                    mm_group(banks[pGA][:, 0:N], [(wB[:, kc, 256:384], bufA[:, kc, 0:N]) for kc in range(8)],
                             reads=[rB, "bufA"], writes=["bk%d" % pGA])
                    mm_group(banks[pGC][:, 0:N], [(wB[:, kc, 384:512], bufA[:, kc, 0:N]) for kc in range(8)],
                             reads=[rB, "bufA"], writes=["bk%d" % pGC])
                    mm_group(banks[pYA][:, 0:N], [(wB[:, kc, 0:128], attT[:, kc, pos0:pos0 + N]) for kc in range(8)],
                             reads=[rB, "attT"], writes=["bk%d" % pYA])
                    mm_group(banks[pYC][:, 0:N], [(wB[:, kc, 128:256], bufB[:, kc, 0:N]) for kc in range(8)],
                             reads=[rB, "bufB"], writes=["bk%d" % pYC])
                    k = oc % 2
                    s1, s2, tm = sg[k], usb[k], tmp[k]
                    P.op("act", lambda e, s1=s1, pGA=pGA: e.activation(s1[:, 0:N], banks[pGA][:, 0:N], AF.Sigmoid),
                         reads=["bk%d" % pGA], writes=["sg%d" % k])
                    P.op("act", lambda e, s2=s2, pGC=pGC: e.activation(s2[:, 0:N], banks[pGC][:, 0:N], AF.Sigmoid),
                         reads=["bk%d" % pGC], writes=["usb%d" % k])
                    P.op("dve", lambda e, s1=s1, pYA=pYA, tm=tm: e.tensor_tensor(
                        tm[:, 0:N], banks[pYA][:, 0:N], s1[:, 0:N], ALU.mult),
                        reads=["bk%d" % pYA, "sg%d" % k], writes=["tmp%d" % k])
                    P.op("dve", lambda e, s2=s2, pYC=pYC: e.tensor_tensor(
                        s2[:, 0:N], banks[pYC][:, 0:N], s2[:, 0:N], ALU.mult),
                        reads=["bk%d" % pYC, "usb%d" % k], writes=["usb%d" % k])
                    P.op("dve", lambda e, s2=s2, tm=tm, oc=oc: e.tensor_tensor(
                        bufC[:, oc, 0:N], tm[:, 0:N], s2[:, 0:N], ALU.add),
                        reads=["tmp%d" % k, "usb%d" % k], writes=["bufC"])
                if debug and bi == 0:
                    P.dma("pool", dbg_A, bufA[:], "dbgA", reads=["bufA"])
                    P.dma("pool", dbg_B, bufB[:], "dbgB", reads=["bufB"])
                    P.dma("pool", dbg_C, bufC[:], "dbgC", reads=["bufC"])
                wO = [load_w(CI_C + half) for half in range(2)]
                cst_ = {}
                for t in range(ntt + 1):
                    if t < ntt:
                        for half in range(2):
                            hs = slice(half * 512, (half + 1) * 512)
                            pb_ = psalloc(ring2)
                            mm_group(banks[pb_][0:rows, :],
                                     [(bufC[:, kc, t * rows:(t + 1) * rows], wO[half][0][:, kc, :]) for kc in range(8)],
                                     reads=[wO[half][1], "bufC"], writes=["bk%d" % pb_])
                            P.op("dve", lambda e, pb_=pb_, t=t, hs=hs: e.tensor_tensor(
                                Z[0:rows, t, hs], banks[pb_][0:rows, :], Z[0:rows, t, hs], ALU.add),
                                reads=["bk%d" % pb_, "Zt%d" % t], writes=["Zt%d" % t])
                    if t < ntt:
                        cst_[t] = norm_stats(Z[0:rows, t, :], rows, "Zt%d" % t)
                    if t > 0:
                        tp_ = t - 1
                        norm_finish(cst_[tp_], Z[0:rows, tp_, :], rows, bufA[:, :, tp_ * rows:(tp_ + 1) * rows],
                                    g3[:, 1, :], "Zt%d" % tp_, "bufA")
                if debug and bi == 0:
                    P.dma("pool", dbg_Z, Z[:], "dbgZ", reads=["Zt0", "Zt1", "Zt2", "Zt3"])
                for p_ in range(NFC // 2):
                    wD, rD = load_w(CI_D + p_)
                    fcs = (2 * p_, 2 * p_ + 1)
                    pA = [psalloc(ring2), psalloc(ring2)]
                    for i in range(2):
                        mm_group(banks[pA[i]][:, 0:N], [(wD[:, kc, i * 128:(i + 1) * 128], bufA[:, kc, 0:N])
                                                         for kc in range(8)],
                                 reads=[rD, "bufA"], writes=["bk%d" % pA[i]])
                    if not is_meta:
                        pV = [psalloc(ring2), psalloc(ring2)]
                        for i in range(2):
                            mm_group(banks[pV[i]][:, 0:N], [(wD[:, kc, 256 + i * 128:256 + (i + 1) * 128], bufA[:, kc, 0:N])
                                                             for kc in range(8)],
                                     reads=[rD, "bufA"], writes=["bk%d" % pV[i]])
                    for i in range(2):
                        fc = fcs[i]
                        ab, tm = cub[i], tmp[i]
                        P.op("pool", lambda e, ab=ab, fc=fc: e.tensor_copy(ab[:, 0:2], carry_a[:, fc, :]),
                             reads=["carry_a"], writes=["cubc%d" % i])
                        P.op("act", lambda e, ab=ab, pa=pA[i]: e.activation(ab[:, 2:2 + N], banks[pa][:, 0:N], AF.Copy),
                             reads=["bk%d" % pA[i]], writes=["cub%d" % i])
                        if not is_meta:
                            P.op("act", lambda e, tm=tm, pa=pA[i], fc=fc: e.activation(
                                tm[:, 0:N], banks[pa][:, 0:N], AF.Copy, scale=fcw[:, fc, 2:3]),
                                reads=["bk%d" % pA[i], "fcw"], writes=["tmp%d" % i])
                    if not is_meta:
                        for i in range(2):
                            fc = fcs[i]
                            ab, tm = cub[i], tmp[i]
                            P.op("dve", lambda e, tm=tm, ab=ab, fc=fc: e.scalar_tensor_tensor(
                                tm[:, 0:N], ab[:, 1:1 + N], fcw[:, fc, 1:2], tm[:, 0:N], ALU.mult, ALU.add),
                                reads=["cub%d" % i, "cubc%d" % i, "tmp%d" % i, "fcw"], writes=["tmp%d" % i])
                        for i in range(2):
                            fc = fcs[i]
                            ab, tm = cub[i], tmp[i]
                            P.op("dve", lambda e, tm=tm, ab=ab, fc=fc: e.scalar_tensor_tensor(
                                tm[:, 0:N], ab[:, 0:N], fcw[:, fc, 0:1], tm[:, 0:N], ALU.mult, ALU.add),
                                reads=["cub%d" % i, "cubc%d" % i, "tmp%d" % i, "fcw"], writes=["tmp%d" % i])
                    for i in range(2):
                        fc = fcs[i]
                        ab = cub[i]
                        P.op("pool", lambda e, ab=ab, fc=fc: e.tensor_copy(carry_a[:, fc, :], ab[:, N:N + 2]),
                             reads=["cub%d" % i, "cubc%d" % i], writes=["carry_a"])
                    if is_meta:
                        continue
                    for i in range(2):
                        tm, s1 = tmp[i], sg[i]
                        P.op("act", lambda e, s1=s1, tm=tm: e.activation(s1[:, 0:N], tm[:, 0:N], AF.Silu),
                             reads=["tmp%d" % i], writes=["sg%d" % i])
                    for i in range(2):
                        fc = fcs[i]
                        s1 = sg[i]
                        P.op("dve", lambda e, s1=s1, pv=pV[i], fc=fc: e.tensor_tensor(
                            gT[:, fc, 0:N], banks[pv][:, 0:N], s1[:, 0:N], ALU.mult),
                            reads=["bk%d" % pV[i], "sg%d" % i], writes=["gT"])
                if is_meta:
                    if nxt is not None:
                        do_norm1(nxt)
                    return
                if debug and bi == 0:
                    P.dma("pool", dbg_G, gT[:], "dbgG", reads=["gT"])
                for half in range(2):
                    hs = slice(half * 512, (half + 1) * 512)
                    pbs = [psalloc(ring2) for _ in range(4)]
                    for kg in range(3):
                        k0 = kg * 8
                        nk = min(8, NFC - k0)
                        wE, rE = load_w(CI_E + half * 3 + kg)
                        for t in range(4):
                            def dmm(e, t=t, k0=k0, nk=nk, wE=wE, pb_=pbs[t]):
                                ins = None
                                for kk in range(nk):
                                    ins = e.matmul(banks[pb_][:, :], lhsT=gT[:, k0 + kk, t * 128:(t + 1) * 128],
                                                   rhs=wE[:, kk, :], start=(k0 + kk == 0), stop=(k0 + kk == NFC - 1))
                                return ins
                            P.op("pe", dmm, reads=[rE, "gT"], writes=["bk%d" % pbs[t]])
                    for t in range(4):
                        P.op("dve", lambda e, t=t, hs=hs, pb_=pbs[t]: e.tensor_tensor(
                            Z[:, t, hs], banks[pb_][:, :], Z[:, t, hs], ALU.add),
                            reads=["bk%d" % pbs[t], "Zt%d" % t], writes=["Zt%d" % t])
                    if half == 0 and nxt is not None:
                        do_norm1(nxt)
                for t in range(4):
                    c = norm_stats(Z[:, t, :], 128, "Zt%d" % t)
                    o, ores = stg_next()
                    P.op("dve", lambda e, o=o, t=t, c=c: e.scalar_tensor_tensor(
                        o[:], Z[:, t, :], rs[:, c:c + 1], g3[:, 2, :], ALU.mult, ALU.mult),
                        reads=["Zt%d" % t, "rs%d" % c, "g3"], writes=[ores])
                    r0 = bi * 512 + t * 128
                    out_events.append(P.dma("pool", out[r0:r0 + 128, :], o[:], ores, reads=[ores]))

            seq = [-1] + list(range(p2_blocks))
            do_norm1(seq[0])
            for i, bi in enumerate(seq):
                block(bi, seq[i + 1] if i + 1 < len(seq) else None)
            P.barrier()
            P.wait_events("sp", out_events)
        P.emit()
    return nc


_NC_CACHE = {}


def _consts():
    s = np.arange(128)[:, None]
    t = np.arange(128)[None, :]
    c = np.zeros((128, 4, 128), np.float32)
    c[:, 0, :] = np.eye(128, dtype=np.float32)
    c[:, 1, :] = np.where(s > t, NEG, 0.0)
    c[:, 2, :] = (s <= t).astype(np.float32)
    c[:, 3, :] = 1.0
    return c


def make_in_maps(x, meta_tokens, g_mix, w_in, b_f, conv_w, w_o_attn, w_o_conv, w_o,
                 g_ffn, w_ffn_in, ffn_conv_w, w_ffn_out, g_final):
    f = lambda a: np.ascontiguousarray(np.asarray(a, dtype=np.float32))
    g3 = np.stack([f(g_mix)[0], f(g_ffn)[0], f(g_final)], 0)
    g3 = np.ascontiguousarray(np.broadcast_to(g3[None], (128, 3, D)))
    bfrep = np.ascontiguousarray(np.broadcast_to(f(b_f)[0][None, None, :], (128, NT, 8)).reshape(128, NT * 8))
    cw = np.ascontiguousarray(f(conv_w)[0].T.reshape(8, 128, 3).transpose(1, 0, 2))
    fcw = np.ascontiguousarray(f(ffn_conv_w)[0].T.reshape(NFC, 128, 3).transpose(1, 0, 2))
    shared = {
        "meta": f(meta_tokens), "w_in": f(w_in)[0], "w_o_attn": f(w_o_attn)[0], "w_o_conv": f(w_o_conv)[0],
        "w_o": f(w_o)[0], "w_ffn_in": f(w_ffn_in)[0], "w_ffn_out": f(w_ffn_out)[0],
        "g3": g3, "bfrep": bfrep, "cw": cw, "fcw": fcw, "cst": _consts(),
    }
    xs_ = f(x)
    return [dict(shared, x=xs_[b]) for b in range(xs_.shape[0])]


def kernel(**inputs):
    in_maps = make_in_maps(**inputs)
    if "nc" not in _NC_CACHE:
        _NC_CACHE["nc"] = build_nc()
    res = run_bass_kernel_spmd(_NC_CACHE["nc"], in_maps, core_ids=list(range(8)))
    return np.stack([r["out"] for r in res.results], 0).astype(np.float32)
```

```python
import contextlib
import numpy as np
import concourse.bass as bass
import concourse.mybir as mybir
from concourse.bass_utils import run_bass_kernel_spmd

F32 = mybir.dt.float32
BF16 = mybir.dt.bfloat16
AF = mybir.ActivationFunctionType
ALU = mybir.AluOpType

D = 1024
SEQ = 4096
NMETA = 16
L = SEQ + NMETA
NT = 33
NPOS = NT * 128
DFF = 2816
NFC = DFF // 128
DIN = 8200
KOFF, VOFF, FOFF, BOFF, COFF, UOFF, GAOFF, GCOFF = 1024, 2048, 3072, 3080, 4104, 5128, 6152, 7176
SCALE = float(128 ** -0.5)
EPS = 1e-6
NEG = -30000.0

ENGS = ("pe", "act", "dve", "pool", "sp")


class Prog:
    def __init__(self, nc, same_engine_sync=True):
        self.nc = nc
        self.ops = {e: [] for e in ENGS}
        self.count = {}
        self.waited = {e: {} for e in ENGS}
        self.last_write = {}
        self.readers = {}
        self.same_engine_sync = same_engine_sync
        self.dma_keys = []

    def _deps(self, eng, reads, writes):
        evs = []
        for r in reads:
            e = self.last_write.get(r)
            if e is not None:
                evs.append(e)
        for r in writes:
            e = self.last_write.get(r)
            if e is not None:
                evs.append(e)
            evs.extend(self.readers.get(r, ()))
        best = {}
        w = self.waited[eng]
        for (k, v) in evs:
            if k == ("eng", eng) and (eng == "pe" or not self.same_engine_sync):
                continue
            if w.get(k, 0) >= v:
                continue
            best[k] = max(best.get(k, 0), v)
        for k, v in best.items():
            w[k] = v
        return list(best.items())

    def _commit(self, ev, reads, writes):
        for r in reads:
            self.readers.setdefault(r, []).append(ev)
        for r in writes:
            self.last_write[r] = ev
            self.readers[r] = []

    def op(self, eng, fn, reads=(), writes=()):
        waits = self._deps(eng, reads, writes)
        k = ("eng", eng)
        self.count[k] = self.count.get(k, 0) + 1
        ev = (k, self.count[k])
        self.ops[eng].append((fn, waits, k, 1))
        self._commit(ev, reads, writes)
        return ev

    def dma(self, eng, out, in_, key, reads=(), writes=()):
        waits = self._deps(eng, reads, writes)
        k = ("dma", key)
        if k not in self.count:
            self.dma_keys.append(k)
        self.count[k] = self.count.get(k, 0) + 16
        ev = (k, self.count[k])
        fn = lambda e, out=out, in_=in_: e.dma_start(out=out, in_=in_)
        self.ops[eng].append((fn, waits, k, 16))
        self._commit(ev, reads, writes)
        return ev

    def wait_events(self, eng, events):
        waits = []
        for (k, v) in events:
            if self.waited[eng].get(k, 0) < v:
                self.waited[eng][k] = v
                waits.append((k, v))
        if waits:
            self.ops[eng].append((None, waits, None, 0))

    def barrier(self):
        evs = [(k, v) for k, v in self.count.items() if v > 0]
        for e in ENGS:
            self.wait_events(e, [(k, v) for (k, v) in evs if not (k == ("eng", e) and e in ("pe", "sp"))])
        self.last_write = {}
        self.readers = {}

    def emit(self):
        nc = self.nc
        with contextlib.ExitStack() as st:
            sems = {}
            keys = [("eng", e) for e in ENGS] + self.dma_keys
            for i, k in enumerate(keys):
                if self.count.get(k, 0) == 0:
                    continue
                sems[k] = st.enter_context(nc.semaphore("s%d" % i))
            block = st.enter_context(nc.Block())

            def run(eng_name):
                def body(e):
                    for (fn, waits, k, n) in self.ops[eng_name]:
                        for (wk, wv) in waits:
                            e.wait_ge(sems[wk], wv)
                        if fn is None:
                            continue
                        fn(e).then_inc(sems[k], n)
                return body

            block.tensor(run("pe"))
            block.scalar(run("act"))
            block.vector(run("dve"))
            block.gpsimd(run("pool"))
            block.sync(run("sp"))


def build_nc(debug=False, p2_blocks=8):
    nc = bass.Bass("TRN2", target_bir_lowering=False)
    dt_in = lambda name, shape: nc.dram_tensor(name, shape, F32, kind="ExternalInput").ap()
    x = dt_in("x", [SEQ, D])
    meta = dt_in("meta", [NMETA, D])
    w_in = dt_in("w_in", [D, DIN])
    w_oa = dt_in("w_o_attn", [D, D])
    w_oc = dt_in("w_o_conv", [D, D])
    w_o = dt_in("w_o", [D, D])
    w_fi = dt_in("w_ffn_in", [D, 2 * DFF])
    w_fo = dt_in("w_ffn_out", [DFF, D])
    g3_d = dt_in("g3", [128, 3, D])
    bf_d = dt_in("bfrep", [128, NT * 8])
    cw_d = dt_in("cw", [128, 8, 3])
    fcw_d = dt_in("fcw", [128, NFC, 3])
    cst_d = dt_in("cst", [128, 4, 128])
    out = nc.dram_tensor("out", [SEQ, D], F32, kind="ExternalOutput").ap()
    if debug:
        dbg_att = nc.dram_tensor("dbg_att", [128, 8, L], F32, kind="ExternalOutput").ap()
        dbg_c = nc.dram_tensor("dbg_c", [128, NT * 8], F32, kind="ExternalOutput").ap()
        dbg_B = nc.dram_tensor("dbg_B", [128, 8, 512], BF16, kind="ExternalOutput").ap()
        dbg_C = nc.dram_tensor("dbg_C", [128, 8, 512], BF16, kind="ExternalOutput").ap()
        dbg_A = nc.dram_tensor("dbg_A", [128, 8, 512], BF16, kind="ExternalOutput").ap()
        dbg_G = nc.dram_tensor("dbg_G", [128, NFC, 512], BF16, kind="ExternalOutput").ap()
        dbg_Z = nc.dram_tensor("dbg_Z", [128, 4, D], F32, kind="ExternalOutput").ap()

    kview = lambda w: w.rearrange("(kc p) c -> p kc c", p=128)
    w_in_v, w_oa_v, w_oc_v, w_o_v, w_fi_v, w_fo_v = map(kview, (w_in, w_oa, w_oc, w_o, w_fi, w_fo))

    CH = []
    for cc in range(8):
        CH.append((8, 384, [(i * 128, w_in_v[:, :, o_ + cc * 128:o_ + (cc + 1) * 128], 128)
                            for i, o_ in enumerate((BOFF, COFF, UOFF))]))
    for oc in range(8):
        cs_ = slice(oc * 128, (oc + 1) * 128)
        CH.append((8, 512, [(0, w_oa_v[:, :, cs_], 128), (128, w_oc_v[:, :, cs_], 128),
                            (256, w_in_v[:, :, GAOFF + oc * 128:GAOFF + (oc + 1) * 128], 128),
                            (384, w_in_v[:, :, GCOFF + oc * 128:GCOFF + (oc + 1) * 128], 128)]))
    for half in range(2):
        CH.append((8, 512, [(0, w_o_v[:, :, half * 512:(half + 1) * 512], 512)]))
    for p_ in range(NFC // 2):
        CH.append((8, 512, [(0, w_fi_v[:, :, p_ * 256:(p_ + 1) * 256], 256),
                            (256, w_fi_v[:, :, DFF + p_ * 256:DFF + (p_ + 1) * 256], 256)]))
    for half in range(2):
        for kg in range(3):
            k0_ = kg * 8
            nk_ = min(8, NFC - k0_)
            CH.append((nk_, 512, [(0, w_fo_v[:, k0_:k0_ + nk_, half * 512:(half + 1) * 512], 512)]))
    NCHUNK = len(CH)
    assert NCHUNK == 35
    CI_A, CI_B, CI_C, CI_D, CI_E = 0, 8, 16, 18, 29
    wscr = nc.dram_tensor("wscr", [NCHUNK, 128, 8, 512], BF16, kind="Internal").ap()
    conv_state = {"i": 0, "n": 0}

    P = Prog(nc)

    def convert_chunks(n):
        for _ in range(n):
            ci = conv_state["i"]
            if ci >= NCHUNK:
                return
            conv_state["i"] += 1
            nk, ncols, pieces = CH[ci]
            for (c0, src, w) in pieces:
                conv_state["n"] += 1
                P.dma("pool", wscr[ci][:, 0:nk, c0:c0 + w], src, "cv%d" % (conv_state["n"] % 4), writes=["scr%d" % conv_state["n"]])

    with contextlib.ExitStack() as st0:
        T0 = lambda name, shape, dt: st0.enter_context(nc.sbuf_tensor("sb_" + name, shape, dt))
        attT = T0("attT", [128, 8, L], BF16)
        cstb = T0("cstb", [128, 4, 128], BF16)
        ss = T0("ss", [128, 8], F32)
        rs = T0("rs", [128, 8], F32)
        NORM = {"xs": None, "junk": None}
        ident, negmask, uincl, ones = (cstb[:, i, :] for i in range(4))
        TR = {"banks": None}
        BK = {"banks": None}

        with nc.sbuf_tensor("sb_cstf", [128, 4, 128], F32) as cstf:
            P.dma("sp", cstf[:], cst_d, "cst", writes=["cstf"])
            P.op("dve", lambda e: e.tensor_copy(cstb[:], cstf[:]), reads=["cstf"], writes=["cst"])
            P.barrier()

        ring_state = {"i": 0}

        def psalloc(ring):
            i = ring[ring_state["i"] % len(ring)]
            ring_state["i"] += 1
            return i

        evac_state = {"i": 0}

        def copy_any(out_ap, in_ap, reads, writes):
            evac_state["i"] += 1
            mode = evac_state.get("mode", "both")
            if mode == "act" or (mode == "both" and evac_state["i"] % 2):
                P.op("act", lambda e: e.activation(out_ap, in_ap, AF.Copy), reads=reads, writes=writes)
            else:
                P.op("dve", lambda e: e.tensor_copy(out_ap, in_ap), reads=reads, writes=writes)

        def mm_group(out_ap, pairs, reads, writes, first_start=True):
            def fn(e):
                n = len(pairs)
                ins = None
                for i, (l, r) in enumerate(pairs):
                    ins = e.matmul(out_ap, lhsT=l, rhs=r, start=(first_start and i == 0), stop=(i == n - 1))
                return ins
            P.op("pe", fn, reads=reads, writes=writes)

        norm_ctr = {"i": 0}

        def norm_stats(src, rows, src_res):
            i = norm_ctr["i"]
            norm_ctr["i"] += 1
            c = i % 8
            junk = NORM["junk"]
            P.op("dve", lambda e: e.memset(ss[:, c:c + 1], 0.0), writes=["ss%d" % c])
            P.op("act", lambda e: e.activation(junk[0:rows, :], src, AF.Square, accum_out=ss[0:rows, c:c + 1]),
                 reads=[src_res], writes=["ss%d" % c, "junk"])
            P.op("act", lambda e: e.activation(rs[:, c:c + 1], ss[:, c:c + 1], AF.Ln, bias=EPS, scale=1.0 / D),
                 reads=["ss%d" % c], writes=["rs%d" % c])
            P.op("act", lambda e: e.activation(rs[:, c:c + 1], rs[:, c:c + 1], AF.Exp, scale=-0.5),
                 reads=["rs%d" % c], writes=["rs%d" % c])
            return c

        fin_ctr = {"i": 0}

        def norm_finish(c, src, rows, dst, grow, src_res, dst_res):
            i = fin_ctr["i"]
            fin_ctr["i"] += 1
            xi = i % 2
            xb = NORM["xs"][xi]
            P.op("dve", lambda e: e.scalar_tensor_tensor(
                xb[0:rows, :], src, rs[0:rows, c:c + 1], grow[0:rows, :], ALU.mult, ALU.mult),
                reads=[src_res, "rs%d" % c, "g3"], writes=["xs%d" % xi])
            nb_ = len(TR["banks"])
            ti = i % nb_
            trb = TR["banks"][ti]

            def tr(e):
                ins = None
                for kc in range(8):
                    ins = e.transpose(trb[:, kc, 0:rows], xb[0:rows, kc * 128:(kc + 1) * 128], ident[0:rows, 0:rows])
                return ins
            P.op("pe", tr, reads=["xs%d" % xi, "cst"], writes=["tr%d" % ti])
            copy_any(dst, trb[:, :, 0:rows], reads=["tr%d" % ti], writes=[dst_res])

        def norm_tile(src, rows, dst, grow, src_res, dst_res):
            c = norm_stats(src, rows, src_res)
            norm_finish(c, src, rows, dst, grow, src_res, dst_res)

        with contextlib.ExitStack() as st1:
            T1 = lambda name, shape, dt: st1.enter_context(nc.sbuf_tensor("sb_" + name, shape, dt))
            hT = T1("hT", [128, 8, NPOS], BF16)
            cpos = T1("cpos", [128, NT * 8], F32)
            off = T1("off", [128, (NT + 1) * 8], F32)
            TR["banks"] = [st1.enter_context(nc.psum_tensor("tr_ps", [128, 8, 128], BF16))]
            banks = [st1.enter_context(nc.psum_tensor("bank%d" % i, [128, 512], F32)) for i in range(7)]
            ring1 = [0, 1, 2, 3]
            O_bs, L_b = [banks[4], banks[5]], banks[6]

            with contextlib.ExitStack() as st:
                Ts = lambda name, shape, dt: st.enter_context(nc.sbuf_tensor("sb_" + name, shape, dt))
                xt = [Ts("xt%d" % i, [128, 4, D], F32) for i in range(2)]
                g1 = Ts("g1", [128, D], F32)
                NORM["xs"] = [Ts("xs1_%d" % i, [128, D], BF16) for i in range(2)]
                NORM["junk"] = Ts("junk1", [128, D], BF16)
                P.dma("sp", g1[:], g3_d[:, 0, :], "g3", writes=["g3"])
                P.op("pool", lambda e: e.memset(hT[:, :, L:NPOS], 0.0), writes=["hT"])
                P.dma("sp", xt[0][0:NMETA, 0, :], meta, "xt0", writes=["xt0"])
                norm_tile(xt[0][0:NMETA, 0, :], NMETA, hT[:, :, 0:NMETA], g1[:], "xt0", "hT")
                for b in range(8):
                    xb_ = xt[(b + 1) % 2]
                    res = "xt%d" % ((b + 1) % 2)
                    P.dma("sp", xb_[:], x[b * 512:(b + 1) * 512, :].rearrange("(t p) d -> p t d", p=128),
                          res, writes=[res])
                    cs4 = [norm_stats(xb_[:, t, :], 128, res) for t in range(4)]
                    for t in range(4):
                        p0 = NMETA + b * 512 + t * 128
                        norm_finish(cs4[t], xb_[:, t, :], 128, hT[:, :, p0:p0 + 128], g1[:], res, "hT")
            P.barrier()

            with contextlib.ExitStack() as st:
                Ts = lambda name, shape, dt: st.enter_context(nc.sbuf_tensor("sb_" + name, shape, dt))
                wf = Ts("wf", [128, 8, 8], BF16)
                bfr = Ts("bfr", [128, NT * 8], F32)
                fb = Ts("fb", [128, NT * 8], F32)
                r1 = Ts("r1", [128, NT * 8], F32)
                parts = [Ts("part%d" % i, [128, NT * 8], BF16) for i in range(3)]
                tot = Ts("tot", [128, NT * 8], F32)
                P.dma("pool", wf[:], w_in_v[:, :, FOFF:FOFF + 8], "wf", writes=["wf"])
                P.dma("sp", bfr[:], bf_d, "bfr", writes=["bfr"])
                psF, psC, psT = banks[0], banks[1], banks[2]

                def fmm(e):
                    ins = None
                    for j in range(NT):
                        for kc in range(8):
                            ins = e.matmul(psF[:, j * 8:(j + 1) * 8], lhsT=hT[:, kc, j * 128:(j + 1) * 128],
                                           rhs=wf[:, kc, :], start=(kc == 0), stop=(kc == 7))
                    return ins
                P.op("pe", fmm, reads=["hT", "wf"], writes=["bank0"])
                NF = NT * 8
                P.op("dve", lambda e: e.tensor_tensor(fb[:], psF[:, 0:NF], bfr[:], ALU.add),
                     reads=["bank0", "bfr"], writes=["fb"])
                P.op("act", lambda e: e.activation(fb[:], fb[:], AF.Exp, scale=-1.0), reads=["fb"], writes=["fb"])
                P.op("act", lambda e: e.activation(fb[:], fb[:], AF.Ln, bias=1.0), reads=["fb"], writes=["fb"])
                P.op("dve", lambda e: e.tensor_copy(parts[0][:], fb[:]), reads=["fb"], writes=["p0"])
                P.op("dve", lambda e: e.tensor_tensor(r1[:], fb[:], parts[0][:], ALU.subtract),
                     reads=["fb", "p0"], writes=["r1"])
                P.op("dve", lambda e: e.tensor_copy(parts[1][:], r1[:]), reads=["r1"], writes=["p1"])
                P.op("dve", lambda e: e.tensor_tensor(r1[:], r1[:], parts[1][:], ALU.subtract),
                     reads=["r1", "p1"], writes=["r1"])
                P.op("dve", lambda e: e.tensor_copy(parts[2][:], r1[:]), reads=["r1"], writes=["p2"])
                mm_group(psC[:, 0:NF], [(uincl, parts[i][:]) for i in range(3)],
                         reads=["cst", "p0", "p1", "p2"], writes=["bank1"])
                mm_group(psT[:, 0:NF], [(ones, parts[i][:]) for i in range(3)],
                         reads=["cst", "p0", "p1", "p2"], writes=["bank2"])
                P.op("dve", lambda e: e.tensor_copy(tot[:], psT[:, 0:NF]), reads=["bank2"], writes=["tot"])
                P.op("dve", lambda e: e.memset(off[:, 0:8], 0.0), writes=["off"])
                for j in range(1, NT + 1):
                    P.op("dve", lambda e, j=j: e.tensor_tensor(off[:, j * 8:(j + 1) * 8], off[:, (j - 1) * 8:j * 8],
                                                               tot[:, (j - 1) * 8:j * 8], ALU.add),
                         reads=["off", "tot"], writes=["off"])
                P.op("dve", lambda e: e.tensor_tensor(cpos[:], psC[:, 0:NF], off[:, 0:NF], ALU.add),
                     reads=["bank1", "off"], writes=["cpos"])
                if debug:
                    P.dma("sp", dbg_c, cpos[:], "dbgc", reads=["cpos"])
            P.barrier()

            with contextlib.ExitStack() as st:
                Ts = lambda name, shape, dt: st.enter_context(nc.sbuf_tensor("sb_" + name, shape, dt))
                qk = [(Ts("qT%d" % i, [128, NPOS], BF16), Ts("kT%d" % i, [128, NPOS], BF16)) for i in range(2)]
                Vt = Ts("Vt", [128, NPOS], BF16)
                wqkv = [Ts("wqkv%d" % i, [128, 8, 3, 128], BF16) for i in range(2)]
                Bh = [Ts("Bh%d" % i, [128, 17, 34], F32) for i in range(2)]
                NPT = 5
                PT = [Ts("PT%d" % i, [128, 512], BF16) for i in range(NPT)]
                rl = [Ts("rl%d" % i, [128, 512], F32) for i in range(2)]
                acc = [Ts("acc%d" % i, [128, 512], F32) for i in range(2)]
                ahi = Ts("ahi", [128, 512], BF16)
                alo = Ts("alo", [128, 512], BF16)
                cpos3 = cpos[:].rearrange("p (j h) -> p j h", h=8)
                evac_state["mode"] = "dve"
                pt_i = 0
                rl_i = 0

                def load_head_w(h):
                    wq = wqkv[h % 2]
                    wres = "wqkv%d" % (h % 2)
                    for t in range(3):
                        P.dma("pool", wq[:, :, t, :], w_in_v[:, :, t * 1024 + h * 128:t * 1024 + (h + 1) * 128],
                              wres + "_%d" % t, writes=[wres])

                def proj_groups(h):
                    wq = wqkv[h % 2]
                    wres = "wqkv%d" % (h % 2)
                    out_ = []
                    for (t, dst, dres) in ((0, qk[h % 2][0], "qT%d" % (h % 2)), (1, qk[h % 2][1], "kT%d" % (h % 2))):
                        for n in range(9):
                            def g(t=t, dst=dst, dres=dres, n=n):
                                c0 = n * 512
                                w = min(512, NPOS - c0)
                                b = psalloc(ring1)
                                mm_group(banks[b][:, 0:w], [(wq[:, kc, t, :], hT[:, kc, c0:c0 + w]) for kc in range(8)],
                                         reads=[wres, "hT"], writes=["bank%d" % b])
                                copy_any(dst[:, c0:c0 + w], banks[b][:, 0:w], reads=["bank%d" % b], writes=[dres])
                            out_.append(g)
                    return out_

                load_head_w(0)
                for g_ in proj_groups(0):
                    g_()
                for h in range(8):
                    wq = wqkv[h % 2]
                    wres = "wqkv%d" % (h % 2)
                    qT, kT = qk[h % 2]
                    qres, kres = "qT%d" % (h % 2), "kT%d" % (h % 2)
                    if h + 1 < 8:
                        load_head_w(h + 1)
                    convert_chunks(6)
                    bh = Bh[h % 2]
                    bres = "Bh%d" % (h % 2)
                    for m in range(17):
                        jn = min(2 * m + 2, NT)
                        P.op("dve", lambda e, m=m, jn=jn, bh=bh, h=h: e.tensor_scalar(
                            bh[:, m, 0:jn], cpos3[:, 0:jn, h], off[:, (2 * m + 1) * 8 + h:(2 * m + 1) * 8 + h + 1],
                            None, ALU.subtract), reads=["cpos", "off"], writes=[bres])
                    for j4 in range(0, NT, 4):
                        nj = min(4, NT - j4)
                        b = psalloc(ring1)

                        def vmm(e, j4=j4, nj=nj, bk=banks[b], wq=wq):
                            ins = None
                            for jj in range(nj):
                                j = j4 + jj
                                for kc in range(8):
                                    ins = e.matmul(bk[:, jj * 128:(jj + 1) * 128],
                                                   lhsT=hT[:, kc, j * 128:(j + 1) * 128], rhs=wq[:, kc, 2, :],
                                                   start=(kc == 0), stop=(kc == 7))
                            return ins
                        P.op("pe", vmm, reads=[wres, "hT"], writes=["bank%d" % b])
                        copy_any(Vt[:, j4 * 128:(j4 + nj) * 128], banks[b][:, 0:nj * 128],
                                 reads=["bank%d" % b], writes=["Vt"])
                    pending = proj_groups(h + 1) if h + 1 < 8 else []
                    steps = []
                    for qb in range(9):
                        q0 = qb * 512
                        qend = min(q0 + 512, L)
                        jlast = (qend - 1) // 128
                        for j in range(jlast + 1):
                            steps.append((qb, q0, qend, jlast, j))
                    LA = 3
                    infl = {}

                    def issue_qk(i):
                        nonlocal pt_i
                        (qb, q0, qend, jlast, j) = steps[i]
                        c0 = max(q0, 128 * j)
                        ncols = qend - c0
                        diag = (128 * j >= q0)
                        b = psalloc(ring1)
                        S = banks[b]

                        def smm(e, j=j, c0=c0, ncols=ncols, diag=diag, S=S, qT=qT, kT=kT):
                            kt = kT[:, j * 128:(j + 1) * 128]
                            if not diag:
                                return e.matmul(S[:, 0:ncols], lhsT=kt, rhs=qT[:, c0:c0 + ncols],
                                                start=True, stop=True)
                            wd = min(128, ncols)
                            e.matmul(S[:, 0:wd], lhsT=ident, rhs=negmask[:, 0:wd], start=True, stop=False,
                                     skip_group_check=True)
                            ins = e.matmul(S[:, 0:wd], lhsT=kt, rhs=qT[:, c0:c0 + wd], start=False, stop=True,
                                           skip_group_check=True)
                            if ncols > wd:
                                ins = e.matmul(S[:, wd:ncols], lhsT=kt, rhs=qT[:, c0 + wd:c0 + ncols],
                                               start=False, stop=True, skip_group_check=True)
                            return ins
                        P.op("pe", smm, reads=[qres, kres, "cst"], writes=["bank%d" % b])
                        pt = PT[pt_i % NPT]
                        pres = "PT%d" % (pt_i % NPT)
                        pt_i += 1
                        m_lo, m_hi = c0 // 256, (qend - 1) // 256
                        for m in range(m_lo, m_hi + 1):
                            a = max(c0, 256 * m) - c0
                            bnd = min(qend, 256 * m + 256) - c0
                            P.op("act", lambda e, pt=pt, S=S, a=a, bnd=bnd, bh=bh, m=m, j=j: e.activation(
                                pt[:, a:bnd], S[:, a:bnd], AF.Exp, bias=bh[:, m, j:j + 1], scale=SCALE),
                                reads=["bank%d" % b, bres], writes=[pres])
                        ac = acc[qb % 2]
                        ares = "acc%d" % (qb % 2)
                        o0_ = c0 - q0
                        if j == 0:
                            P.op("dve", lambda e, ac=ac, pt=pt, ncols=ncols: e.tensor_copy(ac[:, 0:ncols], pt[:, 0:ncols]),
                                 reads=[pres], writes=[ares])
                        else:
                            P.op("dve", lambda e, ac=ac, pt=pt, ncols=ncols, o0_=o0_: e.tensor_tensor(
                                ac[:, o0_:o0_ + ncols], ac[:, o0_:o0_ + ncols], pt[:, 0:ncols], ALU.add),
                                reads=[pres, ares], writes=[ares])
                        infl[i] = (pt, pres, c0, ncols)

                    def issue_pv(i):
                        nonlocal rl_i
                        (qb, q0, qend, jlast, j) = steps[i]
                        (pt, pres, c0, ncols) = infl.pop(i)
                        o0 = c0 - q0
                        nq = qend - q0

                        O_b = O_bs[qb % 2]
                        ores_ = "bank%d" % (4 + qb % 2)

                        def pvmm(e, j=j, o0=o0, ncols=ncols, pt=pt, jlast=jlast, O_b=O_b):
                            return e.matmul(O_b[:, o0:o0 + ncols], lhsT=Vt[:, j * 128:(j + 1) * 128], rhs=pt[:, 0:ncols],
                                            start=(j == 0), stop=(j == jlast), skip_group_check=True)
                        P.op("pe", pvmm, reads=[pres, "Vt"], writes=[ores_])
                        if j == jlast:
                            ac = acc[qb % 2]
                            ares = "acc%d" % (qb % 2)
                            P.op("dve", lambda e, ac=ac, nq=nq: e.tensor_copy(ahi[:, 0:nq], ac[:, 0:nq]),
                                 reads=[ares], writes=["ahi"])
                            P.op("dve", lambda e, ac=ac, nq=nq: e.tensor_tensor(
                                alo[:, 0:nq], ac[:, 0:nq], ahi[:, 0:nq], ALU.subtract),
                                reads=[ares, "ahi"], writes=["alo"])
                            mm_group(L_b[:, 0:nq], [(ones, ahi[:, 0:nq]), (ones, alo[:, 0:nq])],
                                     reads=["cst", "ahi", "alo"], writes=["bank6"])
                            r = rl[rl_i % 2]
                            rres = "rl%d" % (rl_i % 2)
                            rl_i += 1
                            P.op("dve", lambda e, r=r, nq=nq: e.reciprocal(r[:, 0:nq], L_b[:, 0:nq]),
                                 reads=["bank6"], writes=[rres])
                            P.op("dve", lambda e, r=r, nq=nq, h=h, q0=q0, O_b=O_b: e.tensor_tensor(
                                attT[:, h, q0:q0 + nq], O_b[:, 0:nq], r[:, 0:nq], ALU.mult),
                                reads=[ores_, rres], writes=["attT"])

                    every = max(1, len(steps) // (len(pending) + 1)) if pending else 0
                    for i in range(len(steps) + LA):
                        if i < len(steps):
                            issue_qk(i)
                        if i >= LA:
                            issue_pv(i - LA)
                        if pending and i % every == every - 1:
                            pending.pop(0)()
                    while pending:
                        pending.pop(0)()
            P.barrier()

        if debug:
            with contextlib.ExitStack() as st:
                dbt = st.enter_context(nc.sbuf_tensor("dbt", [128, 4, L], F32))
                for hh in range(2):
                    P.op("dve", lambda e, hh=hh: e.tensor_copy(dbt[:], attT[:, hh * 4:(hh + 1) * 4, :]),
                         reads=["attT"], writes=["dbt"])
                    P.dma("sp", dbg_att[:, hh * 4:(hh + 1) * 4, :], dbt[:], "dbga", reads=["dbt"])
                P.barrier()

        with contextlib.ExitStack() as st:
            Ts = lambda name, shape, dt: st.enter_context(nc.sbuf_tensor("sb_" + name, shape, dt))
            TR["banks"] = [st.enter_context(nc.psum_tensor("tr2_%d" % i, [128, 8, 128], BF16)) for i in range(2)]
            banks = [st.enter_context(nc.psum_tensor("bk2_%d" % i, [128, 512], F32)) for i in range(6)]
            ring2 = [0, 1, 2, 3, 4, 5]
            evac_state["mode"] = "both"
            g3 = Ts("g3", [128, 3, D], F32)
            NORM["xs"] = [Ts("xs2_%d" % i, [128, D], BF16) for i in range(2)]
            NORM["junk"] = Ts("junk2", [128, D], BF16)
            P.dma("pool", g3[:], g3_d, "g3", writes=["g3"])
            Z = Ts("Z", [128, 4, D], F32)
            bufA = Ts("bufA", [128, 8, 512], BF16)
            bufB = Ts("bufB", [128, 8, 512], BF16)
            bufC = Ts("bufC", [128, 8, 512], BF16)
            gT = Ts("gT", [128, NFC, 512], BF16)
            NB = 4
            wr = [Ts("wr%d" % i, [128, 8, 512], BF16) for i in range(NB)]
            cub = [Ts("cub%d" % i, [128, 514], F32) for i in range(2)]
            usb = [Ts("usb%d" % i, [128, 512], F32) for i in range(2)]
            tmp = [Ts("tmp%d" % i, [128, 512], F32) for i in range(2)]
            sg = [Ts("sg%d" % i, [128, 512], F32) for i in range(2)]
            NSTG = 3
            stg = [Ts("stg%d" % i, [128, D], F32) for i in range(NSTG)]
            cw = Ts("cw", [128, 8, 3], F32)
            fcw = Ts("fcw", [128, NFC, 3], F32)
            carry_cu = Ts("carry_cu", [128, 8, 2], F32)
            carry_a = Ts("carry_a", [128, NFC, 2], F32)
            P.dma("pool", cw[:], cw_d, "cw", writes=["cw"])
            P.dma("pool", fcw[:], fcw_d, "fcw", writes=["fcw"])
            P.op("dve", lambda e: e.memset(carry_cu[:], 0.0), writes=["carry_cu"])
            P.op("dve", lambda e: e.memset(carry_a[:], 0.0), writes=["carry_a"])

            wstate = {"i": 0}

            def load_w(ci):
                s_ = wstate["i"] % NB
                wstate["i"] += 1
                nk, ncols, _ = CH[ci]
                P.dma("sp", wr[s_][:, 0:nk, 0:ncols], wscr[ci][:, 0:nk, 0:ncols], "wr%d" % s_, writes=["wr%d" % s_])
                return wr[s_], "wr%d" % s_

            stg_state = {"i": 0}

            def stg_next():
                i = stg_state["i"] % NSTG
                stg_state["i"] += 1
                return stg[i], "stg%d" % i

            out_events = []

            def conv_taps(tm, tres, src, sres, wcol, N):
                P.op("dve", lambda e: e.scalar_tensor_tensor(tm[:, 0:N], src[:, 1:1 + N], wcol[:, 1:2],
                                                             tm[:, 0:N], ALU.mult, ALU.add),
                     reads=list(sres) + [tres, "cw", "fcw"], writes=[tres])
                P.op("dve", lambda e: e.scalar_tensor_tensor(tm[:, 0:N], src[:, 0:N], wcol[:, 0:1],
                                                             tm[:, 0:N], ALU.mult, ALU.add),
                     reads=list(sres) + [tres, "cw", "fcw"], writes=[tres])

            def do_norm1(k):
                if k < 0:
                    sb, sres = stg_next()
                    P.dma("pool", sb[0:NMETA, :], meta, sres, writes=[sres])
                    norm_tile(sb[0:NMETA, :], NMETA, bufA[:, :, 0:NMETA], g3[:, 0, :], sres, "bufA")
                    return
                pend = []
                for t in range(4):
                    if t == NSTG:
                        norm_finish(*pend.pop(0))
                    sb, sres = stg_next()
                    r0 = k * 512 + t * 128
                    P.dma("pool", sb[:], x[r0:r0 + 128, :], sres, writes=[sres])
                    c = norm_stats(sb[:], 128, sres)
                    pend.append((c, sb[:], 128, bufA[:, :, t * 128:(t + 1) * 128], g3[:, 0, :], sres, "bufA"))
                for a_ in pend:
                    norm_finish(*a_)

            def block(bi, nxt):
                is_meta = bi < 0
                N = NMETA if is_meta else 512
                ntt = 1 if is_meta else 4
                rows = NMETA if is_meta else 128
                pos0 = 0 if is_meta else NMETA + bi * 512
                if is_meta:
                    P.dma("pool", Z[0:NMETA, 0, :], meta, "Z0", writes=["Zt0"])
                else:
                    for t in range(4):
                        r0 = bi * 512 + t * 128
                        P.dma("pool", Z[:, t, :], x[r0:r0 + 128, :], "Z%d" % t, writes=["Zt%d" % t])
                for cc in range(8):
                    wA, rA = load_w(CI_A + cc)
                    pU, pC, pB = psalloc(ring2), psalloc(ring2), psalloc(ring2)
                    for (pb_, c0) in ((pU, 256), (pC, 128), (pB, 0)):
                        mm_group(banks[pb_][:, 0:N], [(wA[:, kc, c0:c0 + 128], bufA[:, kc, 0:N]) for kc in range(8)],
                                 reads=[rA, "bufA"], writes=["bk%d" % pb_])
                    k = cc % 2
                    cu, us, tm = cub[k], usb[k], tmp[k]
                    P.op("act", lambda e, us=us, pU=pU: e.activation(us[:, 0:N], banks[pU][:, 0:N], AF.Copy),
                         reads=["bk%d" % pU], writes=["usb%d" % k])
                    P.op("pool", lambda e, cu=cu, cc=cc: e.tensor_copy(cu[:, 0:2], carry_cu[:, cc, :]),
                         reads=["carry_cu"], writes=["cubc%d" % k])
                    P.op("dve", lambda e, cu=cu, us=us, pC=pC: e.tensor_tensor(
                        cu[:, 2:2 + N], banks[pC][:, 0:N], us[:, 0:N], ALU.mult),
                        reads=["bk%d" % pC, "usb%d" % k], writes=["cub%d" % k])
                    P.op("act", lambda e, cu=cu, tm=tm, cc=cc: e.activation(
                        tm[:, 0:N], cu[:, 2:2 + N], AF.Copy, scale=cw[:, cc, 2:3]),
                        reads=["cub%d" % k, "cw"], writes=["tmp%d" % k])
                    conv_taps(tm, "tmp%d" % k, cu, ["cub%d" % k, "cubc%d" % k], cw[:, cc, :], N)
                    P.op("pool", lambda e, cu=cu, cc=cc: e.tensor_copy(carry_cu[:, cc, :], cu[:, N:N + 2]),
                         reads=["cub%d" % k, "cubc%d" % k], writes=["carry_cu"])
                    P.op("dve", lambda e, tm=tm, pB=pB, cc=cc: e.tensor_tensor(
                        bufB[:, cc, 0:N], banks[pB][:, 0:N], tm[:, 0:N], ALU.mult),
                        reads=["bk%d" % pB, "tmp%d" % k], writes=["bufB"])
                for oc in range(8):
                    wB, rB = load_w(CI_B + oc)
                    pGA, pGC, pYA, pYC = (psalloc(ring2) for _ in range(4))
                    mm_group(banks[pGA][:, 0:N], [(wB[:, kc, 256:384], bufA[:, kc, 0:N]) for kc in range(8)],
                             reads=[rB, "bufA"], writes=["bk%d" % pGA])
                    mm_group(banks[pGC][:, 0:N], [(wB[:, kc, 384:512], bufA[:, kc, 0:N]) for kc in range(8)],
                             reads=[rB, "bufA"], writes=["bk%d" % pGC])
                    mm_group(banks[pYA][:, 0:N], [(wB[:, kc, 0:128], attT[:, kc, pos0:pos0 + N]) for kc in range(8)],
                             reads=[rB, "attT"], writes=["bk%d" % pYA])
                    mm_group(banks[pYC][:, 0:N], [(wB[:, kc, 128:256], bufB[:, kc, 0:N]) for kc in range(8)],
                             reads=[rB, "bufB"], writes=["bk%d" % pYC])
                    k = oc % 2
                    s1, s2, tm = sg[k], usb[k], tmp[k]
                    P.op("act", lambda e, s1=s1, pGA=pGA: e.activation(s1[:, 0:N], banks[pGA][:, 0:N], AF.Sigmoid),
                         reads=["bk%d" % pGA], writes=["sg%d" % k])
                    P.op("act", lambda e, s2=s2, pGC=pGC: e.activation(s2[:, 0:N], banks[pGC][:, 0:N], AF.Sigmoid),
                         reads=["bk%d" % pGC], writes=["usb%d" % k])
                    P.op("dve", lambda e, s1=s1, pYA=pYA, tm=tm: e.tensor_tensor(
                        tm[:, 0:N], banks[pYA][:, 0:N], s1[:, 0:N], ALU.mult),
                        reads=["bk%d" % pYA, "sg%d" % k], writes=["tmp%d" % k])
                    P.op("dve", lambda e, s2=s2, pYC=pYC: e.tensor_tensor(
                        s2[:, 0:N], banks[pYC][:, 0:N], s2[:, 0:N], ALU.mult),
                        reads=["bk%d" % pYC, "usb%d" % k], writes=["usb%d" % k])
                    P.op("dve", lambda e, s2=s2, tm=tm, oc=oc: e.tensor_tensor(
                        bufC[:, oc, 0:N], tm[:, 0:N], s2[:, 0:N], ALU.add),
                        reads=["tmp%d" % k, "usb%d" % k], writes=["bufC"])
                if debug and bi == 0:
                    P.dma("pool", dbg_A, bufA[:], "dbgA", reads=["bufA"])
                    P.dma("pool", dbg_B, bufB[:], "dbgB", reads=["bufB"])
                    P.dma("pool", dbg_C, bufC[:], "dbgC", reads=["bufC"])
                wO = [load_w(CI_C + half) for half in range(2)]
                cst_ = {}
                for t in range(ntt + 1):
                    if t < ntt:
                        for half in range(2):
                            hs = slice(half * 512, (half + 1) * 512)
                            pb_ = psalloc(ring2)
                            mm_group(banks[pb_][0:rows, :],
                                     [(bufC[:, kc, t * rows:(t + 1) * rows], wO[half][0][:, kc, :]) for kc in range(8)],
                                     reads=[wO[half][1], "bufC"], writes=["bk%d" % pb_])
                            P.op("dve", lambda e, pb_=pb_, t=t, hs=hs: e.tensor_tensor(
                                Z[0:rows, t, hs], banks[pb_][0:rows, :], Z[0:rows, t, hs], ALU.add),
                                reads=["bk%d" % pb_, "Zt%d" % t], writes=["Zt%d" % t])
                    if t < ntt:
                        cst_[t] = norm_stats(Z[0:rows, t, :], rows, "Zt%d" % t)
                    if t > 0:
                        tp_ = t - 1
                        norm_finish(cst_[tp_], Z[0:rows, tp_, :], rows, bufA[:, :, tp_ * rows:(tp_ + 1) * rows],
                                    g3[:, 1, :], "Zt%d" % tp_, "bufA")
                if debug and bi == 0:
                    P.dma("pool", dbg_Z, Z[:], "dbgZ", reads=["Zt0", "Zt1", "Zt2", "Zt3"])
                for p_ in range(NFC // 2):
                    wD, rD = load_w(CI_D + p_)
                    fcs = (2 * p_, 2 * p_ + 1)
                    pA = [psalloc(ring2), psalloc(ring2)]
                    for i in range(2):
                        mm_group(banks[pA[i]][:, 0:N], [(wD[:, kc, i * 128:(i + 1) * 128], bufA[:, kc, 0:N])
                                                         for kc in range(8)],
                                 reads=[rD, "bufA"], writes=["bk%d" % pA[i]])
                    if not is_meta:
                        pV = [psalloc(ring2), psalloc(ring2)]
                        for i in range(2):
                            mm_group(banks[pV[i]][:, 0:N], [(wD[:, kc, 256 + i * 128:256 + (i + 1) * 128], bufA[:, kc, 0:N])
                                                             for kc in range(8)],
                                     reads=[rD, "bufA"], writes=["bk%d" % pV[i]])
                    for i in range(2):
                        fc = fcs[i]
                        ab, tm = cub[i], tmp[i]
                        P.op("pool", lambda e, ab=ab, fc=fc: e.tensor_copy(ab[:, 0:2], carry_a[:, fc, :]),
                             reads=["carry_a"], writes=["cubc%d" % i])
                        P.op("act", lambda e, ab=ab, pa=pA[i]: e.activation(ab[:, 2:2 + N], banks[pa][:, 0:N], AF.Copy),
                             reads=["bk%d" % pA[i]], writes=["cub%d" % i])
                        if not is_meta:
                            P.op("act", lambda e, tm=tm, pa=pA[i], fc=fc: e.activation(
                                tm[:, 0:N], banks[pa][:, 0:N], AF.Copy, scale=fcw[:, fc, 2:3]),
                                reads=["bk%d" % pA[i], "fcw"], writes=["tmp%d" % i])
                    if not is_meta:
                        for i in range(2):
                            fc = fcs[i]
                            ab, tm = cub[i], tmp[i]
                            P.op("dve", lambda e, tm=tm, ab=ab, fc=fc: e.scalar_tensor_tensor(
                                tm[:, 0:N], ab[:, 1:1 + N], fcw[:, fc, 1:2], tm[:, 0:N], ALU.mult, ALU.add),
                                reads=["cub%d" % i, "cubc%d" % i, "tmp%d" % i, "fcw"], writes=["tmp%d" % i])
                        for i in range(2):
                            fc = fcs[i]
                            ab, tm = cub[i], tmp[i]
                            P.op("dve", lambda e, tm=tm, ab=ab, fc=fc: e.scalar_tensor_tensor(
                                tm[:, 0:N], ab[:, 0:N], fcw[:, fc, 0:1], tm[:, 0:N], ALU.mult, ALU.add),
                                reads=["cub%d" % i, "cubc%d" % i, "tmp%d" % i, "fcw"], writes=["tmp%d" % i])
                    for i in range(2):
                        fc = fcs[i]
                        ab = cub[i]
                        P.op("pool", lambda e, ab=ab, fc=fc: e.tensor_copy(carry_a[:, fc, :], ab[:, N:N + 2]),
                             reads=["cub%d" % i, "cubc%d" % i], writes=["carry_a"])
                    if is_meta:
                        continue
                    for i in range(2):
                        tm, s1 = tmp[i], sg[i]
                        P.op("act", lambda e, s1=s1, tm=tm: e.activation(s1[:, 0:N], tm[:, 0:N], AF.Silu),
                             reads=["tmp%d" % i], writes=["sg%d" % i])
                    for i in range(2):
                        fc = fcs[i]
                        s1 = sg[i]
                        P.op("dve", lambda e, s1=s1, pv=pV[i], fc=fc: e.tensor_tensor(
                            gT[:, fc, 0:N], banks[pv][:, 0:N], s1[:, 0:N], ALU.mult),
                            reads=["bk%d" % pV[i], "sg%d" % i], writes=["gT"])
                if is_meta:
                    if nxt is not None:
                        do_norm1(nxt)
                    return
                if debug and bi == 0:
                    P.dma("pool", dbg_G, gT[:], "dbgG", reads=["gT"])
                for half in range(2):
                    hs = slice(half * 512, (half + 1) * 512)
                    pbs = [psalloc(ring2) for _ in range(4)]
                    for kg in range(3):
                        k0 = kg * 8
                        nk = min(8, NFC - k0)
                        wE, rE = load_w(CI_E + half * 3 + kg)
                        for t in range(4):
                            def dmm(e, t=t, k0=k0, nk=nk, wE=wE, pb_=pbs[t]):
                                ins = None
                                for kk in range(nk):
                                    ins = e.matmul(banks[pb_][:, :], lhsT=gT[:, k0 + kk, t * 128:(t + 1) * 128],
                                                   rhs=wE[:, kk, :], start=(k0 + kk == 0), stop=(k0 + kk == NFC - 1))
                                return ins
                            P.op("pe", dmm, reads=[rE, "gT"], writes=["bk%d" % pbs[t]])
                    for t in range(4):
                        P.op("dve", lambda e, t=t, hs=hs, pb_=pbs[t]: e.tensor_tensor(
                            Z[:, t, hs], banks[pb_][:, :], Z[:, t, hs], ALU.add),
                            reads=["bk%d" % pbs[t], "Zt%d" % t], writes=["Zt%d" % t])
                    if half == 0 and nxt is not None:
                        do_norm1(nxt)
                for t in range(4):
                    c = norm_stats(Z[:, t, :], 128, "Zt%d" % t)
                    o, ores = stg_next()
                    P.op("dve", lambda e, o=o, t=t, c=c: e.scalar_tensor_tensor(
                        o[:], Z[:, t, :], rs[:, c:c + 1], g3[:, 2, :], ALU.mult, ALU.mult),
                        reads=["Zt%d" % t, "rs%d" % c, "g3"], writes=[ores])
                    r0 = bi * 512 + t * 128
                    out_events.append(P.dma("pool", out[r0:r0 + 128, :], o[:], ores, reads=[ores]))

            seq = [-1] + list(range(p2_blocks))
            do_norm1(seq[0])
            for i, bi in enumerate(seq):
                block(bi, seq[i + 1] if i + 1 < len(seq) else None)
            P.barrier()
            P.wait_events("sp", out_events)
        P.emit()
    return nc


_NC_CACHE = {}


def _consts():
    s = np.arange(128)[:, None]
    t = np.arange(128)[None, :]
    c = np.zeros((128, 4, 128), np.float32)
    c[:, 0, :] = np.eye(128, dtype=np.float32)
    c[:, 1, :] = np.where(s > t, NEG, 0.0)
    c[:, 2, :] = (s <= t).astype(np.float32)
    c[:, 3, :] = 1.0
    return c


def make_in_maps(x, meta_tokens, g_mix, w_in, b_f, conv_w, w_o_attn, w_o_conv, w_o,
                 g_ffn, w_ffn_in, ffn_conv_w, w_ffn_out, g_final):
    f = lambda a: np.ascontiguousarray(np.asarray(a, dtype=np.float32))
    g3 = np.stack([f(g_mix)[0], f(g_ffn)[0], f(g_final)], 0)
    g3 = np.ascontiguousarray(np.broadcast_to(g3[None], (128, 3, D)))
    bfrep = np.ascontiguousarray(np.broadcast_to(f(b_f)[0][None, None, :], (128, NT, 8)).reshape(128, NT * 8))
    cw = np.ascontiguousarray(f(conv_w)[0].T.reshape(8, 128, 3).transpose(1, 0, 2))
    fcw = np.ascontiguousarray(f(ffn_conv_w)[0].T.reshape(NFC, 128, 3).transpose(1, 0, 2))
    shared = {
        "meta": f(meta_tokens), "w_in": f(w_in)[0], "w_o_attn": f(w_o_attn)[0], "w_o_conv": f(w_o_conv)[0],
        "w_o": f(w_o)[0], "w_ffn_in": f(w_ffn_in)[0], "w_ffn_out": f(w_ffn_out)[0],
        "g3": g3, "bfrep": bfrep, "cw": cw, "fcw": fcw, "cst": _consts(),
    }
    xs_ = f(x)
    return [dict(shared, x=xs_[b]) for b in range(xs_.shape[0])]


def kernel(**inputs):
    in_maps = make_in_maps(**inputs)
    if "nc" not in _NC_CACHE:
        _NC_CACHE["nc"] = build_nc()
    res = run_bass_kernel_spmd(_NC_CACHE["nc"], in_maps, core_ids=list(range(8)))
    return np.stack([r["out"] for r in res.results], 0).astype(np.float32)
```

```python
import contextlib
import numpy as np
import concourse.bass as bass
import concourse.mybir as mybir
from concourse.bass_utils import run_bass_kernel_spmd

F32 = mybir.dt.float32
BF16 = mybir.dt.bfloat16
AF = mybir.ActivationFunctionType
ALU = mybir.AluOpType

D = 1024
SEQ = 4096
NMETA = 16
L = SEQ + NMETA
NT = 33
NPOS = NT * 128
DFF = 2816
NFC = DFF // 128
DIN = 8200
KOFF, VOFF, FOFF, BOFF, COFF, UOFF, GAOFF, GCOFF = 1024, 2048, 3072, 3080, 4104, 5128, 6152, 7176
SCALE = float(128 ** -0.5)
EPS = 1e-6
NEG = -30000.0

ENGS = ("pe", "act", "dve", "pool", "sp")


class Prog:
    def __init__(self, nc, same_engine_sync=True):
        self.nc = nc
        self.ops = {e: [] for e in ENGS}
        self.count = {}
        self.waited = {e: {} for e in ENGS}
        self.last_write = {}
        self.readers = {}
        self.same_engine_sync = same_engine_sync
        self.dma_keys = []

    def _deps(self, eng, reads, writes):
        evs = []
        for r in reads:
            e = self.last_write.get(r)
            if e is not None:
                evs.append(e)
        for r in writes:
            e = self.last_write.get(r)
            if e is not None:
                evs.append(e)
            evs.extend(self.readers.get(r, ()))
        best = {}
        w = self.waited[eng]
        for (k, v) in evs:
            if k == ("eng", eng) and (eng == "pe" or not self.same_engine_sync):
                continue
            if w.get(k, 0) >= v:
                continue
            best[k] = max(best.get(k, 0), v)
        for k, v in best.items():
            w[k] = v
        return list(best.items())

    def _commit(self, ev, reads, writes):
        for r in reads:
            self.readers.setdefault(r, []).append(ev)
        for r in writes:
            self.last_write[r] = ev
            self.readers[r] = []

    def op(self, eng, fn, reads=(), writes=()):
        waits = self._deps(eng, reads, writes)
        k = ("eng", eng)
        self.count[k] = self.count.get(k, 0) + 1
        ev = (k, self.count[k])
        self.ops[eng].append((fn, waits, k, 1))
        self._commit(ev, reads, writes)
        return ev

    def dma(self, eng, out, in_, key, reads=(), writes=()):
        waits = self._deps(eng, reads, writes)
        k = ("dma", key)
        if k not in self.count:
            self.dma_keys.append(k)
        self.count[k] = self.count.get(k, 0) + 16
        ev = (k, self.count[k])
        fn = lambda e, out=out, in_=in_: e.dma_start(out=out, in_=in_)
        self.ops[eng].append((fn, waits, k, 16))
        self._commit(ev, reads, writes)
        return ev

    def wait_events(self, eng, events):
        waits = []
        for (k, v) in events:
            if self.waited[eng].get(k, 0) < v:
                self.waited[eng][k] = v
                waits.append((k, v))
        if waits:
            self.ops[eng].append((None, waits, None, 0))

    def barrier(self):
        evs = [(k, v) for k, v in self.count.items() if v > 0]
        for e in ENGS:
            self.wait_events(e, [(k, v) for (k, v) in evs if not (k == ("eng", e) and e in ("pe", "sp"))])
        self.last_write = {}
        self.readers = {}

    def emit(self):
        nc = self.nc
        with contextlib.ExitStack() as st:
            sems = {}
            keys = [("eng", e) for e in ENGS] + self.dma_keys
            for i, k in enumerate(keys):
                if self.count.get(k, 0) == 0:
                    continue
                sems[k] = st.enter_context(nc.semaphore("s%d" % i))
            block = st.enter_context(nc.Block())

            def run(eng_name):
                def body(e):
                    for (fn, waits, k, n) in self.ops[eng_name]:
                        for (wk, wv) in waits:
                            e.wait_ge(sems[wk], wv)
                        if fn is None:
                            continue
                        fn(e).then_inc(sems[k], n)
                return body

            block.tensor(run("pe"))
            block.scalar(run("act"))
            block.vector(run("dve"))
            block.gpsimd(run("pool"))
            block.sync(run("sp"))


def build_nc(debug=False, p2_blocks=8):
    nc = bass.Bass("TRN2", target_bir_lowering=False)
    dt_in = lambda name, shape: nc.dram_tensor(name, shape, F32, kind="ExternalInput").ap()
    x = dt_in("x", [SEQ, D])
    meta = dt_in("meta", [NMETA, D])
    w_in = dt_in("w_in", [D, DIN])
    w_oa = dt_in("w_o_attn", [D, D])
    w_oc = dt_in("w_o_conv", [D, D])
    w_o = dt_in("w_o", [D, D])
    w_fi = dt_in("w_ffn_in", [D, 2 * DFF])
    w_fo = dt_in("w_ffn_out", [DFF, D])
    g3_d = dt_in("g3", [128, 3, D])
    bf_d = dt_in("bfrep", [128, NT * 8])
    cw_d = dt_in("cw", [128, 8, 3])
    fcw_d = dt_in("fcw", [128, NFC, 3])
    cst_d = dt_in("cst", [128, 4, 128])
    out = nc.dram_tensor("out", [SEQ, D], F32, kind="ExternalOutput").ap()
    if debug:
        dbg_att = nc.dram_tensor("dbg_att", [128, 8, L], F32, kind="ExternalOutput").ap()
        dbg_c = nc.dram_tensor("dbg_c", [128, NT * 8], F32, kind="ExternalOutput").ap()
        dbg_B = nc.dram_tensor("dbg_B", [128, 8, 512], BF16, kind="ExternalOutput").ap()
        dbg_C = nc.dram_tensor("dbg_C", [128, 8, 512], BF16, kind="ExternalOutput").ap()
        dbg_A = nc.dram_tensor("dbg_A", [128, 8, 512], BF16, kind="ExternalOutput").ap()
        dbg_G = nc.dram_tensor("dbg_G", [128, NFC, 512], BF16, kind="ExternalOutput").ap()
        dbg_Z = nc.dram_tensor("dbg_Z", [128, 4, D], F32, kind="ExternalOutput").ap()

    kview = lambda w: w.rearrange("(kc p) c -> p kc c", p=128)
    w_in_v, w_oa_v, w_oc_v, w_o_v, w_fi_v, w_fo_v = map(kview, (w_in, w_oa, w_oc, w_o, w_fi, w_fo))

    CH = []
    for cc in range(8):
        CH.append((8, 384, [(i * 128, w_in_v[:, :, o_ + cc * 128:o_ + (cc + 1) * 128], 128)
                            for i, o_ in enumerate((BOFF, COFF, UOFF))]))
    for oc in range(8):
        cs_ = slice(oc * 128, (oc + 1) * 128)
        CH.append((8, 512, [(0, w_oa_v[:, :, cs_], 128), (128, w_oc_v[:, :, cs_], 128),
                            (256, w_in_v[:, :, GAOFF + oc * 128:GAOFF + (oc + 1) * 128], 128),
                            (384, w_in_v[:, :, GCOFF + oc * 128:GCOFF + (oc + 1) * 128], 128)]))
    for half in range(2):
        CH.append((8, 512, [(0, w_o_v[:, :, half * 512:(half + 1) * 512], 512)]))
    for p_ in range(NFC // 2):
        CH.append((8, 512, [(0, w_fi_v[:, :, p_ * 256:(p_ + 1) * 256], 256),
                            (256, w_fi_v[:, :, DFF + p_ * 256:DFF + (p_ + 1) * 256], 256)]))
    for half in range(2):
        for kg in range(3):
            k0_ = kg * 8
            nk_ = min(8, NFC - k0_)
            CH.append((nk_, 512, [(0, w_fo_v[:, k0_:k0_ + nk_, half * 512:(half + 1) * 512], 512)]))
    NCHUNK = len(CH)
    assert NCHUNK == 35
    CI_A, CI_B, CI_C, CI_D, CI_E = 0, 8, 16, 18, 29
    wscr = nc.dram_tensor("wscr", [NCHUNK, 128, 8, 512], BF16, kind="Internal").ap()
    conv_state = {"i": 0, "n": 0}

    P = Prog(nc)

    def convert_chunks(n):
        for _ in range(n):
            ci = conv_state["i"]
            if ci >= NCHUNK:
                return
            conv_state["i"] += 1
            nk, ncols, pieces = CH[ci]
            for (c0, src, w) in pieces:
                conv_state["n"] += 1
                P.dma("pool", wscr[ci][:, 0:nk, c0:c0 + w], src, "cv%d" % (conv_state["n"] % 4), writes=["scr%d" % conv_state["n"]])

    with contextlib.ExitStack() as st0:
        T0 = lambda name, shape, dt: st0.enter_context(nc.sbuf_tensor("sb_" + name, shape, dt))
        attT = T0("attT", [128, 8, L], BF16)
        cstb = T0("cstb", [128, 4, 128], BF16)
        ss = T0("ss", [128, 8], F32)
        rs = T0("rs", [128, 8], F32)
        NORM = {"xs": None, "junk": None}
        ident, negmask, uincl, ones = (cstb[:, i, :] for i in range(4))
        TR = {"banks": None}
        BK = {"banks": None}

        with nc.sbuf_tensor("sb_cstf", [128, 4, 128], F32) as cstf:
            P.dma("sp", cstf[:], cst_d, "cst", writes=["cstf"])
            P.op("dve", lambda e: e.tensor_copy(cstb[:], cstf[:]), reads=["cstf"], writes=["cst"])
            P.barrier()

        ring_state = {"i": 0}

        def psalloc(ring):
            i = ring[ring_state["i"] % len(ring)]
            ring_state["i"] += 1
            return i

        evac_state = {"i": 0}

        def copy_any(out_ap, in_ap, reads, writes):
            evac_state["i"] += 1
            mode = evac_state.get("mode", "both")
            if mode == "act" or (mode == "both" and evac_state["i"] % 2):
                P.op("act", lambda e: e.activation(out_ap, in_ap, AF.Copy), reads=reads, writes=writes)
            else:
                P.op("dve", lambda e: e.tensor_copy(out_ap, in_ap), reads=reads, writes=writes)

        def mm_group(out_ap, pairs, reads, writes, first_start=True):
            def fn(e):
                n = len(pairs)
                ins = None
                for i, (l, r) in enumerate(pairs):
                    ins = e.matmul(out_ap, lhsT=l, rhs=r, start=(first_start and i == 0), stop=(i == n - 1))
                return ins
            P.op("pe", fn, reads=reads, writes=writes)

        norm_ctr = {"i": 0}

        def norm_stats(src, rows, src_res):
            i = norm_ctr["i"]
            norm_ctr["i"] += 1
            c = i % 8
            junk = NORM["junk"]
            P.op("dve", lambda e: e.memset(ss[:, c:c + 1], 0.0), writes=["ss%d" % c])
            P.op("act", lambda e: e.activation(junk[0:rows, :], src, AF.Square, accum_out=ss[0:rows, c:c + 1]),
                 reads=[src_res], writes=["ss%d" % c, "junk"])
            P.op("act", lambda e: e.activation(rs[:, c:c + 1], ss[:, c:c + 1], AF.Ln, bias=EPS, scale=1.0 / D),
                 reads=["ss%d" % c], writes=["rs%d" % c])
            P.op("act", lambda e: e.activation(rs[:, c:c + 1], rs[:, c:c + 1], AF.Exp, scale=-0.5),
                 reads=["rs%d" % c], writes=["rs%d" % c])
            return c

        fin_ctr = {"i": 0}

        def norm_finish(c, src, rows, dst, grow, src_res, dst_res):
            i = fin_ctr["i"]
            fin_ctr["i"] += 1
            xi = i % 2
            xb = NORM["xs"][xi]
            P.op("dve", lambda e: e.scalar_tensor_tensor(
                xb[0:rows, :], src, rs[0:rows, c:c + 1], grow[0:rows, :], ALU.mult, ALU.mult),
                reads=[src_res, "rs%d" % c, "g3"], writes=["xs%d" % xi])
            nb_ = len(TR["banks"])
            ti = i % nb_
            trb = TR["banks"][ti]

            def tr(e):
                ins = None
                for kc in range(8):
                    ins = e.transpose(trb[:, kc, 0:rows], xb[0:rows, kc * 128:(kc + 1) * 128], ident[0:rows, 0:rows])
                return ins
            P.op("pe", tr, reads=["xs%d" % xi, "cst"], writes=["tr%d" % ti])
            copy_any(dst, trb[:, :, 0:rows], reads=["tr%d" % ti], writes=[dst_res])

        def norm_tile(src, rows, dst, grow, src_res, dst_res):
            c = norm_stats(src, rows, src_res)
            norm_finish(c, src, rows, dst, grow, src_res, dst_res)

        with contextlib.ExitStack() as st1:
            T1 = lambda name, shape, dt: st1.enter_context(nc.sbuf_tensor("sb_" + name, shape, dt))
            hT = T1("hT", [128, 8, NPOS], BF16)
            cpos = T1("cpos", [128, NT * 8], F32)
            off = T1("off", [128, (NT + 1) * 8], F32)
            TR["banks"] = [st1.enter_context(nc.psum_tensor("tr_ps", [128, 8, 128], BF16))]
            banks = [st1.enter_context(nc.psum_tensor("bank%d" % i, [128, 512], F32)) for i in range(7)]
            ring1 = [0, 1, 2, 3]
            O_bs, L_b = [banks[4], banks[5]], banks[6]

            with contextlib.ExitStack() as st:
                Ts = lambda name, shape, dt: st.enter_context(nc.sbuf_tensor("sb_" + name, shape, dt))
                xt = [Ts("xt%d" % i, [128, 4, D], F32) for i in range(2)]
                g1 = Ts("g1", [128, D], F32)
                NORM["xs"] = [Ts("xs1_%d" % i, [128, D], BF16) for i in range(2)]
                NORM["junk"] = Ts("junk1", [128, D], BF16)
                P.dma("sp", g1[:], g3_d[:, 0, :], "g3", writes=["g3"])
                P.op("pool", lambda e: e.memset(hT[:, :, L:NPOS], 0.0), writes=["hT"])
                P.dma("sp", xt[0][0:NMETA, 0, :], meta, "xt0", writes=["xt0"])
                norm_tile(xt[0][0:NMETA, 0, :], NMETA, hT[:, :, 0:NMETA], g1[:], "xt0", "hT")
                for b in range(8):
                    xb_ = xt[(b + 1) % 2]
                    res = "xt%d" % ((b + 1) % 2)
                    P.dma("sp", xb_[:], x[b * 512:(b + 1) * 512, :].rearrange("(t p) d -> p t d", p=128),
                          res, writes=[res])
                    cs4 = [norm_stats(xb_[:, t, :], 128, res) for t in range(4)]
                    for t in range(4):
                        p0 = NMETA + b * 512 + t * 128
                        norm_finish(cs4[t], xb_[:, t, :], 128, hT[:, :, p0:p0 + 128], g1[:], res, "hT")
            P.barrier()

            with contextlib.ExitStack() as st:
                Ts = lambda name, shape, dt: st.enter_context(nc.sbuf_tensor("sb_" + name, shape, dt))
                wf = Ts("wf", [128, 8, 8], BF16)
                bfr = Ts("bfr", [128, NT * 8], F32)
                fb = Ts("fb", [128, NT * 8], F32)
                r1 = Ts("r1", [128, NT * 8], F32)
                parts = [Ts("part%d" % i, [128, NT * 8], BF16) for i in range(3)]
                tot = Ts("tot", [128, NT * 8], F32)
                P.dma("pool", wf[:], w_in_v[:, :, FOFF:FOFF + 8], "wf", writes=["wf"])
                P.dma("sp", bfr[:], bf_d, "bfr", writes=["bfr"])
                psF, psC, psT = banks[0], banks[1], banks[2]

                def fmm(e):
                    ins = None
                    for j in range(NT):
                        for kc in range(8):
                            ins = e.matmul(psF[:, j * 8:(j + 1) * 8], lhsT=hT[:, kc, j * 128:(j + 1) * 128],
                                           rhs=wf[:, kc, :], start=(kc == 0), stop=(kc == 7))
                    return ins
                P.op("pe", fmm, reads=["hT", "wf"], writes=["bank0"])
                NF = NT * 8
                P.op("dve", lambda e: e.tensor_tensor(fb[:], psF[:, 0:NF], bfr[:], ALU.add),
                     reads=["bank0", "bfr"], writes=["fb"])
                P.op("act", lambda e: e.activation(fb[:], fb[:], AF.Exp, scale=-1.0), reads=["fb"], writes=["fb"])
                P.op("act", lambda e: e.activation(fb[:], fb[:], AF.Ln, bias=1.0), reads=["fb"], writes=["fb"])
                P.op("dve", lambda e: e.tensor_copy(parts[0][:], fb[:]), reads=["fb"], writes=["p0"])
                P.op("dve", lambda e: e.tensor_tensor(r1[:], fb[:], parts[0][:], ALU.subtract),
                     reads=["fb", "p0"], writes=["r1"])
                P.op("dve", lambda e: e.tensor_copy(parts[1][:], r1[:]), reads=["r1"], writes=["p1"])
                P.op("dve", lambda e: e.tensor_tensor(r1[:], r1[:], parts[1][:], ALU.subtract),
                     reads=["r1", "p1"], writes=["r1"])
                P.op("dve", lambda e: e.tensor_copy(parts[2][:], r1[:]), reads=["r1"], writes=["p2"])
                mm_group(psC[:, 0:NF], [(uincl, parts[i][:]) for i in range(3)],
                         reads=["cst", "p0", "p1", "p2"], writes=["bank1"])
                mm_group(psT[:, 0:NF], [(ones, parts[i][:]) for i in range(3)],
                         reads=["cst", "p0", "p1", "p2"], writes=["bank2"])
                P.op("dve", lambda e: e.tensor_copy(tot[:], psT[:, 0:NF]), reads=["bank2"], writes=["tot"])
                P.op("dve", lambda e: e.memset(off[:, 0:8], 0.0), writes=["off"])
                for j in range(1, NT + 1):
                    P.op("dve", lambda e, j=j: e.tensor_tensor(off[:, j * 8:(j + 1) * 8], off[:, (j - 1) * 8:j * 8],
                                                               tot[:, (j - 1) * 8:j * 8], ALU.add),
                         reads=["off", "tot"], writes=["off"])
                P.op("dve", lambda e: e.tensor_tensor(cpos[:], psC[:, 0:NF], off[:, 0:NF], ALU.add),
                     reads=["bank1", "off"], writes=["cpos"])
                if debug:
                    P.dma("sp", dbg_c, cpos[:], "dbgc", reads=["cpos"])
            P.barrier()

            with contextlib.ExitStack() as st:
                Ts = lambda name, shape, dt: st.enter_context(nc.sbuf_tensor("sb_" + name, shape, dt))
                qk = [(Ts("qT%d" % i, [128, NPOS], BF16), Ts("kT%d" % i, [128, NPOS], BF16)) for i in range(2)]
                Vts = [Ts("Vt%d" % i, [128, NPOS], BF16) for i in range(2)]
                wqkv = [Ts("wqkv%d" % i, [128, 8, 3, 128], BF16) for i in range(2)]
                Bh = [Ts("Bh%d" % i, [128, 17, 34], F32) for i in range(2)]
                NPT = 5
                PT = [Ts("PT%d" % i, [128, 512], BF16) for i in range(NPT)]
                rl = [Ts("rl%d" % i, [128, 512], F32) for i in range(1)]
                cpos3 = cpos[:].rearrange("p (j h) -> p j h", h=8)
                evac_state["mode"] = "dve"
                pt_i = 0
                rl_i = 0

                def load_head_w(h):
                    wq = wqkv[h % 2]
                    wres = "wqkv%d" % (h % 2)
                    for t in range(3):
                        P.dma("pool", wq[:, :, t, :], w_in_v[:, :, t * 1024 + h * 128:t * 1024 + (h + 1) * 128],
                              wres + "_%d" % t, writes=[wres])

                def proj_groups(h):
                    wq = wqkv[h % 2]
                    wres = "wqkv%d" % (h % 2)
                    out_ = []
                    for (t, dst, dres) in ((0, qk[h % 2][0], "qT%d" % (h % 2)), (1, qk[h % 2][1], "kT%d" % (h % 2))):
                        for n in range(9):
                            def g(t=t, dst=dst, dres=dres, n=n):
                                c0 = n * 512
                                w = min(512, NPOS - c0)
                                b = psalloc(ring1)
                                mm_group(banks[b][:, 0:w], [(wq[:, kc, t, :], hT[:, kc, c0:c0 + w]) for kc in range(8)],
                                         reads=[wres, "hT"], writes=["bank%d" % b])
                                copy_any(dst[:, c0:c0 + w], banks[b][:, 0:w], reads=["bank%d" % b], writes=[dres])
                            out_.append(g)
                    Vd = Vts[h % 2]
                    vres_ = "Vt%d" % (h % 2)
                    for j4 in range(0, NT, 4):
                        def gv(j4=j4):
                            nj = min(4, NT - j4)
                            b = psalloc(ring1)

                            def vmm(e, j4=j4, nj=nj, bk=banks[b], wq=wq):
                                ins = None
                                for jj in range(nj):
                                    j = j4 + jj
                                    for kc in range(8):
                                        ins = e.matmul(bk[:, jj * 128:(jj + 1) * 128],
                                                       lhsT=hT[:, kc, j * 128:(j + 1) * 128], rhs=wq[:, kc, 2, :],
                                                       start=(kc == 0), stop=(kc == 7))
                                return ins
                            P.op("pe", vmm, reads=[wres, "hT"], writes=["bank%d" % b])
                            copy_any(Vd[:, j4 * 128:(j4 + nj) * 128], banks[b][:, 0:nj * 128],
                                     reads=["bank%d" % b], writes=[vres_])
                        out_.append(gv)
                    return out_

                load_head_w(0)
                for g_ in proj_groups(0):
                    g_()
                for h in range(8):
                    wq = wqkv[h % 2]
                    wres = "wqkv%d" % (h % 2)
                    qT, kT = qk[h % 2]
                    qres, kres = "qT%d" % (h % 2), "kT%d" % (h % 2)
                    if h + 1 < 8:
                        load_head_w(h + 1)
                    convert_chunks(6)
                    bh = Bh[h % 2]
                    bres = "Bh%d" % (h % 2)
                    for m in range(17):
                        jn = min(2 * m + 2, NT)
                        P.op("dve", lambda e, m=m, jn=jn, bh=bh, h=h: e.tensor_scalar(
                            bh[:, m, 0:jn], cpos3[:, 0:jn, h], off[:, (2 * m + 1) * 8 + h:(2 * m + 1) * 8 + h + 1],
                            None, ALU.subtract), reads=["cpos", "off"], writes=[bres])
                    Vt = Vts[h % 2]
                    vres = "Vt%d" % (h % 2)
                    pending = proj_groups(h + 1) if h + 1 < 8 else []
                    steps = []
                    for qb in range(9):
                        q0 = qb * 512
                        qend = min(q0 + 512, L)
                        jlast = (qend - 1) // 128
                        for j in range(jlast + 1):
                            steps.append((qb, q0, qend, jlast, j))
                    LA = 3
                    infl = {}

                    def issue_qk(i):
                        nonlocal pt_i
                        (qb, q0, qend, jlast, j) = steps[i]
                        c0 = max(q0, 128 * j)
                        ncols = qend - c0
                        diag = (128 * j >= q0)
                        b = psalloc(ring1)
                        S = banks[b]

                        def smm(e, j=j, c0=c0, ncols=ncols, diag=diag, S=S, qT=qT, kT=kT):
                            kt = kT[:, j * 128:(j + 1) * 128]
                            if not diag:
                                return e.matmul(S[:, 0:ncols], lhsT=kt, rhs=qT[:, c0:c0 + ncols],
                                                start=True, stop=True)
                            wd = min(128, ncols)
                            e.matmul(S[:, 0:wd], lhsT=ident, rhs=negmask[:, 0:wd], start=True, stop=False,
                                     skip_group_check=True)
                            ins = e.matmul(S[:, 0:wd], lhsT=kt, rhs=qT[:, c0:c0 + wd], start=False, stop=True,
                                           skip_group_check=True)
                            if ncols > wd:
                                ins = e.matmul(S[:, wd:ncols], lhsT=kt, rhs=qT[:, c0 + wd:c0 + ncols],
                                               start=False, stop=True, skip_group_check=True)
                            return ins
                        P.op("pe", smm, reads=[qres, kres, "cst"], writes=["bank%d" % b])
                        pt = PT[pt_i % NPT]
                        pres = "PT%d" % (pt_i % NPT)
                        pt_i += 1
                        m_lo, m_hi = c0 // 256, (qend - 1) // 256
                        for m in range(m_lo, m_hi + 1):
                            a = max(c0, 256 * m) - c0
                            bnd = min(qend, 256 * m + 256) - c0
                            P.op("act", lambda e, pt=pt, S=S, a=a, bnd=bnd, bh=bh, m=m, j=j: e.activation(
                                pt[:, a:bnd], S[:, a:bnd], AF.Exp, bias=bh[:, m, j:j + 1], scale=SCALE),
                                reads=["bank%d" % b, bres], writes=[pres])
                        infl[i] = (pt, pres, c0, ncols)

                    def issue_pv(i):
                        nonlocal rl_i
                        (qb, q0, qend, jlast, j) = steps[i]
                        (pt, pres, c0, ncols) = infl.pop(i)
                        o0 = c0 - q0
                        nq = qend - q0

                        O_b = O_bs[qb % 2]
                        ores_ = "bank%d" % (4 + qb % 2)

                        def pvmm(e, j=j, o0=o0, ncols=ncols, pt=pt, jlast=jlast, O_b=O_b, Vt=Vt):
                            e.matmul(O_b[:, o0:o0 + ncols], lhsT=Vt[:, j * 128:(j + 1) * 128], rhs=pt[:, 0:ncols],
                                     start=(j == 0), stop=(j == jlast), skip_group_check=True)
                            return e.matmul(L_b[:, o0:o0 + ncols], lhsT=ones, rhs=pt[:, 0:ncols],
                                            start=(j == 0), stop=(j == jlast), skip_group_check=True)
                        P.op("pe", pvmm, reads=[pres, vres, "cst"], writes=[ores_, "bank6"])
                        if j == jlast:
                            r = rl[0]
                            rres = "rl0"
                            rl_i += 1
                            P.op("dve", lambda e, r=r, nq=nq: e.reciprocal(r[:, 0:nq], L_b[:, 0:nq]),
                                 reads=["bank6"], writes=[rres])
                            P.op("dve", lambda e, r=r, nq=nq, h=h, q0=q0, O_b=O_b: e.tensor_tensor(
                                attT[:, h, q0:q0 + nq], O_b[:, 0:nq], r[:, 0:nq], ALU.mult),
                                reads=[ores_, rres], writes=["attT"])

                    every = max(1, len(steps) // (len(pending) + 1)) if pending else 0
                    for i in range(len(steps) + LA):
                        if i < len(steps):
                            issue_qk(i)
                        if i >= LA:
                            issue_pv(i - LA)
                        if pending and i % every == every - 1:
                            pending.pop(0)()
                    while pending:
                        pending.pop(0)()
            P.barrier()

        if debug:
            with contextlib.ExitStack() as st:
                dbt = st.enter_context(nc.sbuf_tensor("dbt", [128, 4, L], F32))
                for hh in range(2):
                    P.op("dve", lambda e, hh=hh: e.tensor_copy(dbt[:], attT[:, hh * 4:(hh + 1) * 4, :]),
                         reads=["attT"], writes=["dbt"])
                    P.dma("sp", dbg_att[:, hh * 4:(hh + 1) * 4, :], dbt[:], "dbga", reads=["dbt"])
                P.barrier()

        with contextlib.ExitStack() as st:
            Ts = lambda name, shape, dt: st.enter_context(nc.sbuf_tensor("sb_" + name, shape, dt))
            TR["banks"] = [st.enter_context(nc.psum_tensor("tr2_%d" % i, [128, 8, 128], BF16)) for i in range(2)]
            banks = [st.enter_context(nc.psum_tensor("bk2_%d" % i, [128, 512], F32)) for i in range(6)]
            ring2 = [0, 1, 2, 3, 4, 5]
            evac_state["mode"] = "both"
            g3 = Ts("g3", [128, 3, D], F32)
            NORM["xs"] = [Ts("xs2_%d" % i, [128, D], BF16) for i in range(2)]
            NORM["junk"] = Ts("junk2", [128, D], BF16)
            P.dma("pool", g3[:], g3_d, "g3", writes=["g3"])
            Z = Ts("Z", [128, 4, D], F32)
            bufA = Ts("bufA", [128, 8, 512], BF16)
            bufB = Ts("bufB", [128, 8, 512], BF16)
            bufC = Ts("bufC", [128, 8, 512], BF16)
            gT = Ts("gT", [128, NFC, 512], BF16)
            NB = 4
            wr = [Ts("wr%d" % i, [128, 8, 512], BF16) for i in range(NB)]
            cub = [Ts("cub%d" % i, [128, 514], F32) for i in range(2)]
            usb = [Ts("usb%d" % i, [128, 512], F32) for i in range(2)]
            tmp = [Ts("tmp%d" % i, [128, 512], F32) for i in range(2)]
            sg = [Ts("sg%d" % i, [128, 512], F32) for i in range(2)]
            NSTG = 3
            stg = [Ts("stg%d" % i, [128, D], F32) for i in range(NSTG)]
            cw = Ts("cw", [128, 8, 3], F32)
            fcw = Ts("fcw", [128, NFC, 3], F32)
            carry_cu = Ts("carry_cu", [128, 8, 2], F32)
            carry_a = Ts("carry_a", [128, NFC, 2], F32)
            P.dma("pool", cw[:], cw_d, "cw", writes=["cw"])
            P.dma("pool", fcw[:], fcw_d, "fcw", writes=["fcw"])
            P.op("dve", lambda e: e.memset(carry_cu[:], 0.0), writes=["carry_cu"])
            P.op("dve", lambda e: e.memset(carry_a[:], 0.0), writes=["carry_a"])

            wstate = {"i": 0}

            def load_w(ci):
                s_ = wstate["i"] % NB
                wstate["i"] += 1
                nk, ncols, _ = CH[ci]
                P.dma("sp", wr[s_][:, 0:nk, 0:ncols], wscr[ci][:, 0:nk, 0:ncols], "wr%d" % s_, writes=["wr%d" % s_])
                return wr[s_], "wr%d" % s_

            stg_state = {"i": 0}

            def stg_next():
                i = stg_state["i"] % NSTG
                stg_state["i"] += 1
                return stg[i], "stg%d" % i

            out_events = []

            def conv_taps(tm, tres, src, sres, wcol, N):
                P.op("dve", lambda e: e.scalar_tensor_tensor(tm[:, 0:N], src[:, 1:1 + N], wcol[:, 1:2],
                                                             tm[:, 0:N], ALU.mult, ALU.add),
                     reads=list(sres) + [tres, "cw", "fcw"], writes=[tres])
                P.op("dve", lambda e: e.scalar_tensor_tensor(tm[:, 0:N], src[:, 0:N], wcol[:, 0:1],
                                                             tm[:, 0:N], ALU.mult, ALU.add),
                     reads=list(sres) + [tres, "cw", "fcw"], writes=[tres])

            def do_norm1(k):
                if k < 0:
                    sb, sres = stg_next()
                    P.dma("pool", sb[0:NMETA, :], meta, sres, writes=[sres])
                    norm_tile(sb[0:NMETA, :], NMETA, bufA[:, :, 0:NMETA], g3[:, 0, :], sres, "bufA")
                    return
                pend = []
                for t in range(4):
                    if t == NSTG:
                        norm_finish(*pend.pop(0))
                    sb, sres = stg_next()
                    r0 = k * 512 + t * 128
                    P.dma("pool", sb[:], x[r0:r0 + 128, :], sres, writes=[sres])
                    c = norm_stats(sb[:], 128, sres)
                    pend.append((c, sb[:], 128, bufA[:, :, t * 128:(t + 1) * 128], g3[:, 0, :], sres, "bufA"))
                for a_ in pend:
                    norm_finish(*a_)

            def block(bi, nxt):
                is_meta = bi < 0
                N = NMETA if is_meta else 512
                ntt = 1 if is_meta else 4
                rows = NMETA if is_meta else 128
                pos0 = 0 if is_meta else NMETA + bi * 512
                if is_meta:
                    P.dma("pool", Z[0:NMETA, 0, :], meta, "Z0", writes=["Zt0"])
                else:
                    for t in range(4):
                        r0 = bi * 512 + t * 128
                        P.dma("pool", Z[:, t, :], x[r0:r0 + 128, :], "Z%d" % t, writes=["Zt%d" % t])
                for cc in range(8):
                    wA, rA = load_w(CI_A + cc)
                    pU, pC, pB = psalloc(ring2), psalloc(ring2), psalloc(ring2)
                    for (pb_, c0) in ((pU, 256), (pC, 128), (pB, 0)):
                        mm_group(banks[pb_][:, 0:N], [(wA[:, kc, c0:c0 + 128], bufA[:, kc, 0:N]) for kc in range(8)],
                                 reads=[rA, "bufA"], writes=["bk%d" % pb_])
                    k = cc % 2
                    cu, us, tm = cub[k], usb[k], tmp[k]
                    P.op("act", lambda e, us=us, pU=pU: e.activation(us[:, 0:N], banks[pU][:, 0:N], AF.Copy),
                         reads=["bk%d" % pU], writes=["usb%d" % k])
                    P.op("pool", lambda e, cu=cu, cc=cc: e.tensor_copy(cu[:, 0:2], carry_cu[:, cc, :]),
                         reads=["carry_cu"], writes=["cubc%d" % k])
                    P.op("dve", lambda e, cu=cu, us=us, pC=pC: e.tensor_tensor(
                        cu[:, 2:2 + N], banks[pC][:, 0:N], us[:, 0:N], ALU.mult),
                        reads=["bk%d" % pC, "usb%d" % k], writes=["cub%d" % k])
                    P.op("act", lambda e, cu=cu, tm=tm, cc=cc: e.activation(
                        tm[:, 0:N], cu[:, 2:2 + N], AF.Copy, scale=cw[:, cc, 2:3]),
                        reads=["cub%d" % k, "cw"], writes=["tmp%d" % k])
                    conv_taps(tm, "tmp%d" % k, cu, ["cub%d" % k, "cubc%d" % k], cw[:, cc, :], N)
                    P.op("pool", lambda e, cu=cu, cc=cc: e.tensor_copy(carry_cu[:, cc, :], cu[:, N:N + 2]),
                         reads=["cub%d" % k, "cubc%d" % k], writes=["carry_cu"])
                    P.op("dve", lambda e, tm=tm, pB=pB, cc=cc: e.tensor_tensor(
                        bufB[:, cc, 0:N], banks[pB][:, 0:N], tm[:, 0:N], ALU.mult),
                        reads=["bk%d" % pB, "tmp%d" % k], writes=["bufB"])
                for oc in range(8):
                    wB, rB = load_w(CI_B + oc)
                    pGA, pGC, pYA, pYC = (psalloc(ring2) for _ in range(4))
                    mm_group(banks[pGA][:, 0:N], [(wB[:, kc, 256:384], bufA[:, kc, 0:N]) for kc in range(8)],
                             reads=[rB, "bufA"], writes=["bk%d" % pGA])
                    mm_group(banks[pGC][:, 0:N], [(wB[:, kc, 384:512], bufA[:, kc, 0:N]) for kc in range(8)],
                             reads=[rB, "bufA"], writes=["bk%d" % pGC])
                    mm_group(banks[pYA][:, 0:N], [(wB[:, kc, 0:128], attT[:, kc, pos0:pos0 + N]) for kc in range(8)],
                             reads=[rB, "attT"], writes=["bk%d" % pYA])
                    mm_group(banks[pYC][:, 0:N], [(wB[:, kc, 128:256], bufB[:, kc, 0:N]) for kc in range(8)],
                             reads=[rB, "bufB"], writes=["bk%d" % pYC])
                    k = oc % 2
                    s1, s2, tm = sg[k], usb[k], tmp[k]
                    P.op("act", lambda e, s1=s1, pGA=pGA: e.activation(s1[:, 0:N], banks[pGA][:, 0:N], AF.Sigmoid),
                         reads=["bk%d" % pGA], writes=["sg%d" % k])
                    P.op("act", lambda e, s2=s2, pGC=pGC: e.activation(s2[:, 0:N], banks[pGC][:, 0:N], AF.Sigmoid),
                         reads=["bk%d" % pGC], writes=["usb%d" % k])
                    P.op("dve", lambda e, s1=s1, pYA=pYA, tm=tm: e.tensor_tensor(
                        tm[:, 0:N], banks[pYA][:, 0:N], s1[:, 0:N], ALU.mult),
                        reads=["bk%d" % pYA, "sg%d" % k], writes=["tmp%d" % k])
                    P.op("dve", lambda e, s2=s2, pYC=pYC: e.tensor_tensor(
                        s2[:, 0:N], banks[pYC][:, 0:N], s2[:, 0:N], ALU.mult),
                        reads=["bk%d" % pYC, "usb%d" % k], writes=["usb%d" % k])
                    P.op("dve", lambda e, s2=s2, tm=tm, oc=oc: e.tensor_tensor(
                        bufC[:, oc, 0:N], tm[:, 0:N], s2[:, 0:N], ALU.add),
                        reads=["tmp%d" % k, "usb%d" % k], writes=["bufC"])
                if debug and bi == 0:
                    P.dma("pool", dbg_A, bufA[:], "dbgA", reads=["bufA"])
                    P.dma("pool", dbg_B, bufB[:], "dbgB", reads=["bufB"])
                    P.dma("pool", dbg_C, bufC[:], "dbgC", reads=["bufC"])
                wO = [load_w(CI_C + half) for half in range(2)]
                cst_ = {}
                for t in range(ntt + 1):
                    if t < ntt:
                        for half in range(2):
                            hs = slice(half * 512, (half + 1) * 512)
                            pb_ = psalloc(ring2)
                            mm_group(banks[pb_][0:rows, :],
                                     [(bufC[:, kc, t * rows:(t + 1) * rows], wO[half][0][:, kc, :]) for kc in range(8)],
                                     reads=[wO[half][1], "bufC"], writes=["bk%d" % pb_])
                            P.op("dve", lambda e, pb_=pb_, t=t, hs=hs: e.tensor_tensor(
                                Z[0:rows, t, hs], banks[pb_][0:rows, :], Z[0:rows, t, hs], ALU.add),
                                reads=["bk%d" % pb_, "Zt%d" % t], writes=["Zt%d" % t])
                    if t < ntt:
                        cst_[t] = norm_stats(Z[0:rows, t, :], rows, "Zt%d" % t)
                    if t > 0:
                        tp_ = t - 1
                        norm_finish(cst_[tp_], Z[0:rows, tp_, :], rows, bufA[:, :, tp_ * rows:(tp_ + 1) * rows],
                                    g3[:, 1, :], "Zt%d" % tp_, "bufA")
                if debug and bi == 0:
                    P.dma("pool", dbg_Z, Z[:], "dbgZ", reads=["Zt0", "Zt1", "Zt2", "Zt3"])
                for p_ in range(NFC // 2):
                    wD, rD = load_w(CI_D + p_)
                    fcs = (2 * p_, 2 * p_ + 1)
                    pA = [psalloc(ring2), psalloc(ring2)]
                    for i in range(2):
                        mm_group(banks[pA[i]][:, 0:N], [(wD[:, kc, i * 128:(i + 1) * 128], bufA[:, kc, 0:N])
                                                         for kc in range(8)],
                                 reads=[rD, "bufA"], writes=["bk%d" % pA[i]])
                    if not is_meta:
                        pV = [psalloc(ring2), psalloc(ring2)]
                        for i in range(2):
                            mm_group(banks[pV[i]][:, 0:N], [(wD[:, kc, 256 + i * 128:256 + (i + 1) * 128], bufA[:, kc, 0:N])
                                                             for kc in range(8)],
                                     reads=[rD, "bufA"], writes=["bk%d" % pV[i]])
                    for i in range(2):
                        fc = fcs[i]
                        ab, tm = cub[i], tmp[i]
                        P.op("pool", lambda e, ab=ab, fc=fc: e.tensor_copy(ab[:, 0:2], carry_a[:, fc, :]),
                             reads=["carry_a"], writes=["cubc%d" % i])
                        P.op("act", lambda e, ab=ab, pa=pA[i]: e.activation(ab[:, 2:2 + N], banks[pa][:, 0:N], AF.Copy),
                             reads=["bk%d" % pA[i]], writes=["cub%d" % i])
                        if not is_meta:
                            P.op("act", lambda e, tm=tm, pa=pA[i], fc=fc: e.activation(
                                tm[:, 0:N], banks[pa][:, 0:N], AF.Copy, scale=fcw[:, fc, 2:3]),
                                reads=["bk%d" % pA[i], "fcw"], writes=["tmp%d" % i])
                    if not is_meta:
                        for i in range(2):
                            fc = fcs[i]
                            ab, tm = cub[i], tmp[i]
                            P.op("dve", lambda e, tm=tm, ab=ab, fc=fc: e.scalar_tensor_tensor(
                                tm[:, 0:N], ab[:, 1:1 + N], fcw[:, fc, 1:2], tm[:, 0:N], ALU.mult, ALU.add),
                                reads=["cub%d" % i, "cubc%d" % i, "tmp%d" % i, "fcw"], writes=["tmp%d" % i])
                        for i in range(2):
                            fc = fcs[i]
                            ab, tm = cub[i], tmp[i]
                            P.op("dve", lambda e, tm=tm, ab=ab, fc=fc: e.scalar_tensor_tensor(
                                tm[:, 0:N], ab[:, 0:N], fcw[:, fc, 0:1], tm[:, 0:N], ALU.mult, ALU.add),
                                reads=["cub%d" % i, "cubc%d" % i, "tmp%d" % i, "fcw"], writes=["tmp%d" % i])
                    for i in range(2):
                        fc = fcs[i]
                        ab = cub[i]
                        P.op("pool", lambda e, ab=ab, fc=fc: e.tensor_copy(carry_a[:, fc, :], ab[:, N:N + 2]),
                             reads=["cub%d" % i, "cubc%d" % i], writes=["carry_a"])
                    if is_meta:
                        continue
                    for i in range(2):
                        tm, s1 = tmp[i], sg[i]
                        P.op("act", lambda e, s1=s1, tm=tm: e.activation(s1[:, 0:N], tm[:, 0:N], AF.Silu),
                             reads=["tmp%d" % i], writes=["sg%d" % i])
                    for i in range(2):
                        fc = fcs[i]
                        s1 = sg[i]
                        P.op("dve", lambda e, s1=s1, pv=pV[i], fc=fc: e.tensor_tensor(
                            gT[:, fc, 0:N], banks[pv][:, 0:N], s1[:, 0:N], ALU.mult),
                            reads=["bk%d" % pV[i], "sg%d" % i], writes=["gT"])
                if is_meta:
                    if nxt is not None:
                        do_norm1(nxt)
                    return
                if debug and bi == 0:
                    P.dma("pool", dbg_G, gT[:], "dbgG", reads=["gT"])
                for half in range(2):
                    hs = slice(half * 512, (half + 1) * 512)
                    pbs = [psalloc(ring2) for _ in range(4)]
                    for kg in range(3):
                        k0 = kg * 8
                        nk = min(8, NFC - k0)
                        wE, rE = load_w(CI_E + half * 3 + kg)
                        for t in range(4):
                            def dmm(e, t=t, k0=k0, nk=nk, wE=wE, pb_=pbs[t]):
                                ins = None
                                for kk in range(nk):
                                    ins = e.matmul(banks[pb_][:, :], lhsT=gT[:, k0 + kk, t * 128:(t + 1) * 128],
                                                   rhs=wE[:, kk, :], start=(k0 + kk == 0), stop=(k0 + kk == NFC - 1))
                                return ins
                            P.op("pe", dmm, reads=[rE, "gT"], writes=["bk%d" % pbs[t]])
                    for t in range(4):
                        P.op("dve", lambda e, t=t, hs=hs, pb_=pbs[t]: e.tensor_tensor(
                            Z[:, t, hs], banks[pb_][:, :], Z[:, t, hs], ALU.add),
                            reads=["bk%d" % pbs[t], "Zt%d" % t], writes=["Zt%d" % t])
                    if half == 0 and nxt is not None:
                        do_norm1(nxt)
                for t in range(4):
                    c = norm_stats(Z[:, t, :], 128, "Zt%d" % t)
                    o, ores = stg_next()
                    P.op("dve", lambda e, o=o, t=t, c=c: e.scalar_tensor_tensor(
                        o[:], Z[:, t, :], rs[:, c:c + 1], g3[:, 2, :], ALU.mult, ALU.mult),
                        reads=["Zt%d" % t, "rs%d" % c, "g3"], writes=[ores])
                    r0 = bi * 512 + t * 128
                    out_events.append(P.dma("pool", out[r0:r0 + 128, :], o[:], ores, reads=[ores]))

            seq = [-1] + list(range(p2_blocks))
            do_norm1(seq[0])
            for i, bi in enumerate(seq):
                block(bi, seq[i + 1] if i + 1 < len(seq) else None)
            P.barrier()
            P.wait_events("sp", out_events)
        P.emit()
    return nc


_NC_CACHE = {}


def _consts():
    s = np.arange(128)[:, None]
    t = np.arange(128)[None, :]
    c = np.zeros((128, 4, 128), np.float32)
    c[:, 0, :] = np.eye(128, dtype=np.float32)
    c[:, 1, :] = np.where(s > t, NEG, 0.0)
    c[:, 2, :] = (s <= t).astype(np.float32)
    c[:, 3, :] = 1.0
    return c


def make_in_maps(x, meta_tokens, g_mix, w_in, b_f, conv_w, w_o_attn, w_o_conv, w_o,
                 g_ffn, w_ffn_in, ffn_conv_w, w_ffn_out, g_final):
    f = lambda a: np.ascontiguousarray(np.asarray(a, dtype=np.float32))
    g3 = np.stack([f(g_mix)[0], f(g_ffn)[0], f(g_final)], 0)
    g3 = np.ascontiguousarray(np.broadcast_to(g3[None], (128, 3, D)))
    bfrep = np.ascontiguousarray(np.broadcast_to(f(b_f)[0][None, None, :], (128, NT, 8)).reshape(128, NT * 8))
    cw = np.ascontiguousarray(f(conv_w)[0].T.reshape(8, 128, 3).transpose(1, 0, 2))
    fcw = np.ascontiguousarray(f(ffn_conv_w)[0].T.reshape(NFC, 128, 3).transpose(1, 0, 2))
    shared = {
        "meta": f(meta_tokens), "w_in": f(w_in)[0], "w_o_attn": f(w_o_attn)[0], "w_o_conv": f(w_o_conv)[0],
        "w_o": f(w_o)[0], "w_ffn_in": f(w_ffn_in)[0], "w_ffn_out": f(w_ffn_out)[0],
        "g3": g3, "bfrep": bfrep, "cw": cw, "fcw": fcw, "cst": _consts(),
    }
    xs_ = f(x)
    return [dict(shared, x=xs_[b]) for b in range(xs_.shape[0])]


def kernel(**inputs):
    in_maps = make_in_maps(**inputs)
    if "nc" not in _NC_CACHE:
        _NC_CACHE["nc"] = build_nc()
    res = run_bass_kernel_spmd(_NC_CACHE["nc"], in_maps, core_ids=list(range(8)))
    return np.stack([r["out"] for r in res.results], 0).astype(np.float32)
```

```python
import contextlib
import numpy as np
import concourse.bass as bass
import concourse.mybir as mybir
from concourse.bass_utils import run_bass_kernel_spmd

F32 = mybir.dt.float32
BF16 = mybir.dt.bfloat16
AF = mybir.ActivationFunctionType
ALU = mybir.AluOpType

D = 1024
SEQ = 4096
NMETA = 16
L = SEQ + NMETA
NT = 33
NPOS = NT * 128
DFF = 2816
NFC = DFF // 128
DIN = 8200
KOFF, VOFF, FOFF, BOFF, COFF, UOFF, GAOFF, GCOFF = 1024, 2048, 3072, 3080, 4104, 5128, 6152, 7176
SCALE = float(128 ** -0.5)
EPS = 1e-6
NEG = -30000.0

ENGS = ("pe", "act", "dve", "pool", "sp")


class Prog:
    def __init__(self, nc, same_engine_sync=True):
        self.nc = nc
        self.ops = {e: [] for e in ENGS}
        self.count = {}
        self.waited = {e: {} for e in ENGS}
        self.last_write = {}
        self.readers = {}
        self.same_engine_sync = same_engine_sync
        self.dma_keys = []

    def _deps(self, eng, reads, writes):
        evs = []
        for r in reads:
            e = self.last_write.get(r)
            if e is not None:
                evs.append(e)
        for r in writes:
            e = self.last_write.get(r)
            if e is not None:
                evs.append(e)
            evs.extend(self.readers.get(r, ()))
        best = {}
        w = self.waited[eng]
        for (k, v) in evs:
            if k == ("eng", eng) and (eng == "pe" or not self.same_engine_sync):
                continue
            if w.get(k, 0) >= v:
                continue
            best[k] = max(best.get(k, 0), v)
        for k, v in best.items():
            w[k] = v
        return list(best.items())

    def _commit(self, ev, reads, writes):
        for r in reads:
            self.readers.setdefault(r, []).append(ev)
        for r in writes:
            self.last_write[r] = ev
            self.readers[r] = []

    def op(self, eng, fn, reads=(), writes=()):
        waits = self._deps(eng, reads, writes)
        k = ("eng", eng)
        self.count[k] = self.count.get(k, 0) + 1
        ev = (k, self.count[k])
        self.ops[eng].append((fn, waits, k, 1))
        self._commit(ev, reads, writes)
        return ev

    def dma(self, eng, out, in_, key, reads=(), writes=()):
        waits = self._deps(eng, reads, writes)
        k = ("dma", key)
        if k not in self.count:
            self.dma_keys.append(k)
        self.count[k] = self.count.get(k, 0) + 16
        ev = (k, self.count[k])
        fn = lambda e, out=out, in_=in_: e.dma_start(out=out, in_=in_)
        self.ops[eng].append((fn, waits, k, 16))
        self._commit(ev, reads, writes)
        return ev

    def wait_events(self, eng, events):
        waits = []
        for (k, v) in events:
            if self.waited[eng].get(k, 0) < v:
                self.waited[eng][k] = v
                waits.append((k, v))
        if waits:
            self.ops[eng].append((None, waits, None, 0))

    def barrier(self):
        evs = [(k, v) for k, v in self.count.items() if v > 0]
        for e in ENGS:
            self.wait_events(e, [(k, v) for (k, v) in evs if not (k == ("eng", e) and e in ("pe", "sp"))])
        self.last_write = {}
        self.readers = {}

    def emit(self):
        nc = self.nc
        with contextlib.ExitStack() as st:
            sems = {}
            keys = [("eng", e) for e in ENGS] + self.dma_keys
            for i, k in enumerate(keys):
                if self.count.get(k, 0) == 0:
                    continue
                sems[k] = st.enter_context(nc.semaphore("s%d" % i))
            block = st.enter_context(nc.Block())

            def run(eng_name):
                def body(e):
                    for (fn, waits, k, n) in self.ops[eng_name]:
                        for (wk, wv) in waits:
                            e.wait_ge(sems[wk], wv)
                        if fn is None:
                            continue
                        fn(e).then_inc(sems[k], n)
                return body

            block.tensor(run("pe"))
            block.scalar(run("act"))
            block.vector(run("dve"))
            block.gpsimd(run("pool"))
            block.sync(run("sp"))


def build_nc(debug=False, p2_blocks=8):
    nc = bass.Bass("TRN2", target_bir_lowering=False)
    dt_in = lambda name, shape: nc.dram_tensor(name, shape, F32, kind="ExternalInput").ap()
    x = dt_in("x", [SEQ, D])
    meta = dt_in("meta", [NMETA, D])
    w_in = dt_in("w_in", [D, DIN])
    w_oa = dt_in("w_o_attn", [D, D])
    w_oc = dt_in("w_o_conv", [D, D])
    w_o = dt_in("w_o", [D, D])
    w_fi = dt_in("w_ffn_in", [D, 2 * DFF])
    w_fo = dt_in("w_ffn_out", [DFF, D])
    g3_d = dt_in("g3", [128, 3, D])
    bf_d = dt_in("bfrep", [128, NT * 8])
    cw_d = dt_in("cw", [128, 8, 3])
    fcw_d = dt_in("fcw", [128, NFC, 3])
    cst_d = dt_in("cst", [128, 4, 128])
    out = nc.dram_tensor("out", [SEQ, D], F32, kind="ExternalOutput").ap()
    if debug:
        dbg_att = nc.dram_tensor("dbg_att", [128, 8, L], F32, kind="ExternalOutput").ap()
        dbg_c = nc.dram_tensor("dbg_c", [128, NT * 8], F32, kind="ExternalOutput").ap()
        dbg_B = nc.dram_tensor("dbg_B", [128, 8, 512], BF16, kind="ExternalOutput").ap()
        dbg_C = nc.dram_tensor("dbg_C", [128, 8, 512], BF16, kind="ExternalOutput").ap()
        dbg_A = nc.dram_tensor("dbg_A", [128, 8, 512], BF16, kind="ExternalOutput").ap()
        dbg_G = nc.dram_tensor("dbg_G", [128, NFC, 512], BF16, kind="ExternalOutput").ap()
        dbg_Z = nc.dram_tensor("dbg_Z", [128, 4, D], F32, kind="ExternalOutput").ap()

    kview = lambda w: w.rearrange("(kc p) c -> p kc c", p=128)
    w_in_v, w_oa_v, w_oc_v, w_o_v, w_fi_v, w_fo_v = map(kview, (w_in, w_oa, w_oc, w_o, w_fi, w_fo))

    CH = []
    for cc in range(8):
        CH.append((8, 384, [(i * 128, w_in_v[:, :, o_ + cc * 128:o_ + (cc + 1) * 128], 128)
                            for i, o_ in enumerate((BOFF, COFF, UOFF))]))
    for oc in range(8):
        cs_ = slice(oc * 128, (oc + 1) * 128)
        CH.append((8, 512, [(0, w_oa_v[:, :, cs_], 128), (128, w_oc_v[:, :, cs_], 128),
                            (256, w_in_v[:, :, GAOFF + oc * 128:GAOFF + (oc + 1) * 128], 128),
                            (384, w_in_v[:, :, GCOFF + oc * 128:GCOFF + (oc + 1) * 128], 128)]))
    for half in range(2):
        CH.append((8, 512, [(0, w_o_v[:, :, half * 512:(half + 1) * 512], 512)]))
    for p_ in range(NFC // 2):
        CH.append((8, 512, [(0, w_fi_v[:, :, p_ * 256:(p_ + 1) * 256], 256),
                            (256, w_fi_v[:, :, DFF + p_ * 256:DFF + (p_ + 1) * 256], 256)]))
    for half in range(2):
        for kg in range(3):
            k0_ = kg * 8
            nk_ = min(8, NFC - k0_)
            CH.append((nk_, 512, [(0, w_fo_v[:, k0_:k0_ + nk_, half * 512:(half + 1) * 512], 512)]))
    NCHUNK = len(CH)
    assert NCHUNK == 35
    CI_A, CI_B, CI_C, CI_D, CI_E = 0, 8, 16, 18, 29
    wscr = nc.dram_tensor("wscr", [NCHUNK, 128, 8, 512], BF16, kind="Internal").ap()
    conv_state = {"i": 0, "n": 0}

    P = Prog(nc)

    def convert_chunks(n):
        for _ in range(n):
            ci = conv_state["i"]
            if ci >= NCHUNK:
                return
            conv_state["i"] += 1
            nk, ncols, pieces = CH[ci]
            for (c0, src, w) in pieces:
                conv_state["n"] += 1
                P.dma("pool", wscr[ci][:, 0:nk, c0:c0 + w], src, "cv%d" % (conv_state["n"] % 4), writes=["scr%d" % conv_state["n"]])

    with contextlib.ExitStack() as st0:
        T0 = lambda name, shape, dt: st0.enter_context(nc.sbuf_tensor("sb_" + name, shape, dt))
        attT = T0("attT", [128, 8, L], BF16)
        cstb = T0("cstb", [128, 4, 128], BF16)
        ss = T0("ss", [128, 8], F32)
        rs = T0("rs", [128, 8], F32)
        NORM = {"xs": None, "junk": None}
        ident, negmask, uincl, ones = (cstb[:, i, :] for i in range(4))
        TR = {"banks": None}
        BK = {"banks": None}

        with nc.sbuf_tensor("sb_cstf", [128, 4, 128], F32) as cstf:
            P.dma("sp", cstf[:], cst_d, "cst", writes=["cstf"])
            P.op("dve", lambda e: e.tensor_copy(cstb[:], cstf[:]), reads=["cstf"], writes=["cst"])
            P.barrier()

        ring_state = {"i": 0}

        def psalloc(ring):
            i = ring[ring_state["i"] % len(ring)]
            ring_state["i"] += 1
            return i

        evac_state = {"i": 0}

        def copy_any(out_ap, in_ap, reads, writes):
            evac_state["i"] += 1
            mode = evac_state.get("mode", "both")
            if mode == "act" or (mode == "both" and evac_state["i"] % 2):
                P.op("act", lambda e: e.activation(out_ap, in_ap, AF.Copy), reads=reads, writes=writes)
            else:
                P.op("dve", lambda e: e.tensor_copy(out_ap, in_ap), reads=reads, writes=writes)

        def mm_group(out_ap, pairs, reads, writes, first_start=True):
            def fn(e):
                n = len(pairs)
                ins = None
                for i, (l, r) in enumerate(pairs):
                    ins = e.matmul(out_ap, lhsT=l, rhs=r, start=(first_start and i == 0), stop=(i == n - 1))
                return ins
            P.op("pe", fn, reads=reads, writes=writes)

        norm_ctr = {"i": 0}

        def norm_stats(src, rows, src_res):
            i = norm_ctr["i"]
            norm_ctr["i"] += 1
            c = i % 8
            junk = NORM["junk"]
            P.op("dve", lambda e: e.memset(ss[:, c:c + 1], 0.0), writes=["ss%d" % c])
            P.op("act", lambda e: e.activation(junk[0:rows, :], src, AF.Square, accum_out=ss[0:rows, c:c + 1]),
                 reads=[src_res], writes=["ss%d" % c, "junk"])
            P.op("act", lambda e: e.activation(rs[:, c:c + 1], ss[:, c:c + 1], AF.Ln, bias=EPS, scale=1.0 / D),
                 reads=["ss%d" % c], writes=["rs%d" % c])
            P.op("act", lambda e: e.activation(rs[:, c:c + 1], rs[:, c:c + 1], AF.Exp, scale=-0.5),
                 reads=["rs%d" % c], writes=["rs%d" % c])
            return c

        fin_ctr = {"i": 0}

        def norm_finish(c, src, rows, dst, grow, src_res, dst_res):
            i = fin_ctr["i"]
            fin_ctr["i"] += 1
            xi = i % 2
            xb = NORM["xs"][xi]
            P.op("dve", lambda e: e.scalar_tensor_tensor(
                xb[0:rows, :], src, rs[0:rows, c:c + 1], grow[0:rows, :], ALU.mult, ALU.mult),
                reads=[src_res, "rs%d" % c, "g3"], writes=["xs%d" % xi])
            nb_ = len(TR["banks"])
            ti = i % nb_
            trb = TR["banks"][ti]

            def tr(e):
                ins = None
                for kc in range(8):
                    ins = e.transpose(trb[:, kc, 0:rows], xb[0:rows, kc * 128:(kc + 1) * 128], ident[0:rows, 0:rows])
                return ins
            P.op("pe", tr, reads=["xs%d" % xi, "cst"], writes=["tr%d" % ti])
            copy_any(dst, trb[:, :, 0:rows], reads=["tr%d" % ti], writes=[dst_res])

        def norm_tile(src, rows, dst, grow, src_res, dst_res):
            c = norm_stats(src, rows, src_res)
            norm_finish(c, src, rows, dst, grow, src_res, dst_res)

        with contextlib.ExitStack() as st1:
            T1 = lambda name, shape, dt: st1.enter_context(nc.sbuf_tensor("sb_" + name, shape, dt))
            hT = T1("hT", [128, 8, NPOS], BF16)
            cpos = T1("cpos", [128, NT * 8], F32)
            off = T1("off", [128, (NT + 1) * 8], F32)
            TR["banks"] = [st1.enter_context(nc.psum_tensor("tr_ps", [128, 8, 128], BF16))]
            banks = [st1.enter_context(nc.psum_tensor("bank%d" % i, [128, 512], F32)) for i in range(7)]
            ring1 = [0, 1, 2, 3]
            O_bs, L_b = [banks[4], banks[5]], banks[6]

            with contextlib.ExitStack() as st:
                Ts = lambda name, shape, dt: st.enter_context(nc.sbuf_tensor("sb_" + name, shape, dt))
                xt = [Ts("xt%d" % i, [128, 4, D], F32) for i in range(2)]
                g1 = Ts("g1", [128, D], F32)
                NORM["xs"] = [Ts("xs1_%d" % i, [128, D], BF16) for i in range(2)]
                NORM["junk"] = Ts("junk1", [128, D], BF16)
                P.dma("sp", g1[:], g3_d[:, 0, :], "g3", writes=["g3"])
                P.op("pool", lambda e: e.memset(hT[:, :, L:NPOS], 0.0), writes=["hT"])
                P.dma("sp", xt[0][0:NMETA, 0, :], meta, "xt0", writes=["xt0"])
                norm_tile(xt[0][0:NMETA, 0, :], NMETA, hT[:, :, 0:NMETA], g1[:], "xt0", "hT")
                for b in range(8):
                    xb_ = xt[(b + 1) % 2]
                    res = "xt%d" % ((b + 1) % 2)
                    P.dma("sp", xb_[:], x[b * 512:(b + 1) * 512, :].rearrange("(t p) d -> p t d", p=128),
                          res, writes=[res])
                    cs4 = [norm_stats(xb_[:, t, :], 128, res) for t in range(4)]
                    for t in range(4):
                        p0 = NMETA + b * 512 + t * 128
                        norm_finish(cs4[t], xb_[:, t, :], 128, hT[:, :, p0:p0 + 128], g1[:], res, "hT")
            P.barrier()

            with contextlib.ExitStack() as st:
                Ts = lambda name, shape, dt: st.enter_context(nc.sbuf_tensor("sb_" + name, shape, dt))
                wf = Ts("wf", [128, 8, 8], BF16)
                bfr = Ts("bfr", [128, NT * 8], F32)
                fb = Ts("fb", [128, NT * 8], F32)
                r1 = Ts("r1", [128, NT * 8], F32)
                parts = [Ts("part%d" % i, [128, NT * 8], BF16) for i in range(3)]
                tot = Ts("tot", [128, NT * 8], F32)
                P.dma("pool", wf[:], w_in_v[:, :, FOFF:FOFF + 8], "wf", writes=["wf"])
                P.dma("sp", bfr[:], bf_d, "bfr", writes=["bfr"])
                psF, psC, psT = banks[0], banks[1], banks[2]

                def fmm(e):
                    ins = None
                    for j in range(NT):
                        for kc in range(8):
                            ins = e.matmul(psF[:, j * 8:(j + 1) * 8], lhsT=hT[:, kc, j * 128:(j + 1) * 128],
                                           rhs=wf[:, kc, :], start=(kc == 0), stop=(kc == 7))
                    return ins
                P.op("pe", fmm, reads=["hT", "wf"], writes=["bank0"])
                NF = NT * 8
                P.op("dve", lambda e: e.tensor_tensor(fb[:], psF[:, 0:NF], bfr[:], ALU.add),
                     reads=["bank0", "bfr"], writes=["fb"])
                P.op("act", lambda e: e.activation(fb[:], fb[:], AF.Exp, scale=-1.0), reads=["fb"], writes=["fb"])
                P.op("act", lambda e: e.activation(fb[:], fb[:], AF.Ln, bias=1.0), reads=["fb"], writes=["fb"])
                P.op("dve", lambda e: e.tensor_copy(parts[0][:], fb[:]), reads=["fb"], writes=["p0"])
                P.op("dve", lambda e: e.tensor_tensor(r1[:], fb[:], parts[0][:], ALU.subtract),
                     reads=["fb", "p0"], writes=["r1"])
                P.op("dve", lambda e: e.tensor_copy(parts[1][:], r1[:]), reads=["r1"], writes=["p1"])
                P.op("dve", lambda e: e.tensor_tensor(r1[:], r1[:], parts[1][:], ALU.subtract),
                     reads=["r1", "p1"], writes=["r1"])
                P.op("dve", lambda e: e.tensor_copy(parts[2][:], r1[:]), reads=["r1"], writes=["p2"])
                mm_group(psC[:, 0:NF], [(uincl, parts[i][:]) for i in range(3)],
                         reads=["cst", "p0", "p1", "p2"], writes=["bank1"])
                mm_group(psT[:, 0:NF], [(ones, parts[i][:]) for i in range(3)],
                         reads=["cst", "p0", "p1", "p2"], writes=["bank2"])
                P.op("dve", lambda e: e.tensor_copy(tot[:], psT[:, 0:NF]), reads=["bank2"], writes=["tot"])
                P.op("dve", lambda e: e.memset(off[:, 0:8], 0.0), writes=["off"])
                for j in range(1, NT + 1):
                    P.op("dve", lambda e, j=j: e.tensor_tensor(off[:, j * 8:(j + 1) * 8], off[:, (j - 1) * 8:j * 8],
                                                               tot[:, (j - 1) * 8:j * 8], ALU.add),
                         reads=["off", "tot"], writes=["off"])
                P.op("dve", lambda e: e.tensor_tensor(cpos[:], psC[:, 0:NF], off[:, 0:NF], ALU.add),
                     reads=["bank1", "off"], writes=["cpos"])
                if debug:
                    P.dma("sp", dbg_c, cpos[:], "dbgc", reads=["cpos"])
            P.barrier()

            with contextlib.ExitStack() as st:
                Ts = lambda name, shape, dt: st.enter_context(nc.sbuf_tensor("sb_" + name, shape, dt))
                qk = [(Ts("qT%d" % i, [128, NPOS], BF16), Ts("kT%d" % i, [128, NPOS], BF16)) for i in range(2)]
                Vts = [Ts("Vt%d" % i, [128, NPOS], BF16) for i in range(2)]
                wqkv = [Ts("wqkv%d" % i, [128, 8, 3, 128], BF16) for i in range(2)]
                Bh = [Ts("Bh%d" % i, [128, 17, 34], F32) for i in range(2)]
                NPT = 5
                PT = [Ts("PT%d" % i, [128, 512], BF16) for i in range(NPT)]
                rl = [Ts("rl%d" % i, [128, 512], F32) for i in range(1)]
                cpos3 = cpos[:].rearrange("p (j h) -> p j h", h=8)
                evac_state["mode"] = "dve"
                pt_i = 0
                rl_i = 0

                def load_head_w(h):
                    wq = wqkv[h % 2]
                    wres = "wqkv%d" % (h % 2)
                    for t in range(3):
                        P.dma("pool", wq[:, :, t, :], w_in_v[:, :, t * 1024 + h * 128:t * 1024 + (h + 1) * 128],
                              wres + "_%d" % t, writes=[wres])

                def proj_groups(h):
                    wq = wqkv[h % 2]
                    wres = "wqkv%d" % (h % 2)
                    out_ = []
                    for (t, dst, dres) in ((0, qk[h % 2][0], "qT%d" % (h % 2)), (1, qk[h % 2][1], "kT%d" % (h % 2))):
                        for n in range(9):
                            def g(t=t, dst=dst, dres=dres, n=n):
                                c0 = n * 512
                                w = min(512, NPOS - c0)
                                b = psalloc(ring1)
                                mm_group(banks[b][:, 0:w], [(wq[:, kc, t, :], hT[:, kc, c0:c0 + w]) for kc in range(8)],
                                         reads=[wres, "hT"], writes=["bank%d" % b])
                                copy_any(dst[:, c0:c0 + w], banks[b][:, 0:w], reads=["bank%d" % b], writes=[dres])
                            out_.append(g)
                    Vd = Vts[h % 2]
                    vres_ = "Vt%d" % (h % 2)
                    for j4 in range(0, NT, 4):
                        def gv(j4=j4):
                            nj = min(4, NT - j4)
                            b = psalloc(ring1)

                            def vmm(e, j4=j4, nj=nj, bk=banks[b], wq=wq):
                                ins = None
                                for jj in range(nj):
                                    j = j4 + jj
                                    for kc in range(8):
                                        ins = e.matmul(bk[:, jj * 128:(jj + 1) * 128],
                                                       lhsT=hT[:, kc, j * 128:(j + 1) * 128], rhs=wq[:, kc, 2, :],
                                                       start=(kc == 0), stop=(kc == 7))
                                return ins
                            P.op("pe", vmm, reads=[wres, "hT"], writes=["bank%d" % b])
                            copy_any(Vd[:, j4 * 128:(j4 + nj) * 128], banks[b][:, 0:nj * 128],
                                     reads=["bank%d" % b], writes=[vres_])
                        out_.append(gv)
                    return out_

                load_head_w(0)
                for g_ in proj_groups(0):
                    g_()
                for h in range(8):
                    wq = wqkv[h % 2]
                    wres = "wqkv%d" % (h % 2)
                    qT, kT = qk[h % 2]
                    qres, kres = "qT%d" % (h % 2), "kT%d" % (h % 2)
                    if h + 1 < 8:
                        load_head_w(h + 1)
                    convert_chunks(6)
                    bh = Bh[h % 2]
                    bres = "Bh%d" % (h % 2)
                    for m in range(17):
                        jn = min(2 * m + 2, NT)
                        P.op("dve", lambda e, m=m, jn=jn, bh=bh, h=h: e.tensor_scalar(
                            bh[:, m, 0:jn], cpos3[:, 0:jn, h], off[:, (2 * m + 1) * 8 + h:(2 * m + 1) * 8 + h + 1],
                            None, ALU.subtract), reads=["cpos", "off"], writes=[bres])
                    Vt = Vts[h % 2]
                    vres = "Vt%d" % (h % 2)
                    pending = proj_groups(h + 1) if h + 1 < 8 else []
                    steps = []
                    for qb in range(9):
                        q0 = qb * 512
                        qend = min(q0 + 512, L)
                        jlast = (qend - 1) // 128
                        for j in range(jlast + 1):
                            steps.append((qb, q0, qend, jlast, j))
                    LA = 3
                    infl = {}

                    def issue_qk(i):
                        nonlocal pt_i
                        (qb, q0, qend, jlast, j) = steps[i]
                        c0 = max(q0, 128 * j)
                        ncols = qend - c0
                        diag = (128 * j >= q0)
                        b = psalloc(ring1)
                        S = banks[b]

                        def smm(e, j=j, c0=c0, ncols=ncols, diag=diag, S=S, qT=qT, kT=kT):
                            kt = kT[:, j * 128:(j + 1) * 128]
                            if not diag:
                                return e.matmul(S[:, 0:ncols], lhsT=kt, rhs=qT[:, c0:c0 + ncols],
                                                start=True, stop=True)
                            wd = min(128, ncols)
                            e.matmul(S[:, 0:wd], lhsT=ident, rhs=negmask[:, 0:wd], start=True, stop=False,
                                     skip_group_check=True)
                            ins = e.matmul(S[:, 0:wd], lhsT=kt, rhs=qT[:, c0:c0 + wd], start=False, stop=True,
                                           skip_group_check=True)
                            if ncols > wd:
                                ins = e.matmul(S[:, wd:ncols], lhsT=kt, rhs=qT[:, c0 + wd:c0 + ncols],
                                               start=False, stop=True, skip_group_check=True)
                            return ins
                        P.op("pe", smm, reads=[qres, kres, "cst"], writes=["bank%d" % b])
                        pt = PT[pt_i % NPT]
                        pres = "PT%d" % (pt_i % NPT)
                        pt_i += 1
                        m_lo, m_hi = c0 // 256, (qend - 1) // 256
                        for m in range(m_lo, m_hi + 1):
                            a = max(c0, 256 * m) - c0
                            bnd = min(qend, 256 * m + 256) - c0
                            P.op("act", lambda e, pt=pt, S=S, a=a, bnd=bnd, bh=bh, m=m, j=j: e.activation(
                                pt[:, a:bnd], S[:, a:bnd], AF.Exp, bias=bh[:, m, j:j + 1], scale=SCALE),
                                reads=["bank%d" % b, bres], writes=[pres])
                        infl[i] = (pt, pres, c0, ncols)

                    def issue_pv(i):
                        nonlocal rl_i
                        (qb, q0, qend, jlast, j) = steps[i]
                        (pt, pres, c0, ncols) = infl.pop(i)
                        o0 = c0 - q0
                        nq = qend - q0

                        O_b = O_bs[qb % 2]
                        ores_ = "bank%d" % (4 + qb % 2)

                        def pvmm(e, j=j, o0=o0, ncols=ncols, pt=pt, jlast=jlast, O_b=O_b, Vt=Vt):
                            e.matmul(O_b[:, o0:o0 + ncols], lhsT=Vt[:, j * 128:(j + 1) * 128], rhs=pt[:, 0:ncols],
                                     start=(j == 0), stop=(j == jlast), skip_group_check=True)
                            return e.matmul(L_b[:, o0:o0 + ncols], lhsT=ones, rhs=pt[:, 0:ncols],
                                            start=(j == 0), stop=(j == jlast), skip_group_check=True)
                        P.op("pe", pvmm, reads=[pres, vres, "cst"], writes=[ores_, "bank6"])
                        if j == jlast:
                            r = rl[0]
                            rres = "rl0"
                            rl_i += 1
                            P.op("dve", lambda e, r=r, nq=nq: e.reciprocal(r[:, 0:nq], L_b[:, 0:nq]),
                                 reads=["bank6"], writes=[rres])
                            P.op("dve", lambda e, r=r, nq=nq, h=h, q0=q0, O_b=O_b: e.tensor_tensor(
                                attT[:, h, q0:q0 + nq], O_b[:, 0:nq], r[:, 0:nq], ALU.mult),
                                reads=[ores_, rres], writes=["attT"])

                    every = max(1, len(steps) // (len(pending) + 1)) if pending else 0
                    for i in range(len(steps) + LA):
                        if i < len(steps):
                            issue_qk(i)
                        if i >= LA:
                            issue_pv(i - LA)
                        if pending and i % every == every - 1:
                            pending.pop(0)()
                    while pending:
                        pending.pop(0)()
            P.barrier()

        if debug:
            with contextlib.ExitStack() as st:
                dbt = st.enter_context(nc.sbuf_tensor("dbt", [128, 4, L], F32))
                for hh in range(2):
                    P.op("dve", lambda e, hh=hh: e.tensor_copy(dbt[:], attT[:, hh * 4:(hh + 1) * 4, :]),
                         reads=["attT"], writes=["dbt"])
                    P.dma("sp", dbg_att[:, hh * 4:(hh + 1) * 4, :], dbt[:], "dbga", reads=["dbt"])
                P.barrier()

        with contextlib.ExitStack() as st:
            Ts = lambda name, shape, dt: st.enter_context(nc.sbuf_tensor("sb_" + name, shape, dt))
            TR["banks"] = [st.enter_context(nc.psum_tensor("tr2_%d" % i, [128, 8, 128], BF16)) for i in range(2)]
            banks = [st.enter_context(nc.psum_tensor("bk2_%d" % i, [128, 512], F32)) for i in range(6)]
            ring2 = [0, 1, 2, 3, 4, 5]
            evac_state["mode"] = "both"
            g3 = Ts("g3", [128, 3, D], F32)
            NORM["xs"] = [Ts("xs2_%d" % i, [128, D], BF16) for i in range(2)]
            NORM["junk"] = Ts("junk2", [128, D], BF16)
            P.dma("pool", g3[:], g3_d, "g3", writes=["g3"])
            Z = Ts("Z", [128, 4, D], F32)
            bufA = Ts("bufA", [128, 8, 512], BF16)
            bufB = Ts("bufB", [128, 8, 512], BF16)
            bufC = Ts("bufC", [128, 8, 512], BF16)
            gT = Ts("gT", [128, NFC, 512], BF16)
            bufAm = Ts("bufAm", [128, 8, NMETA], BF16)
            bufBm = Ts("bufBm", [128, 8, NMETA], BF16)
            bufCm = Ts("bufCm", [128, 8, NMETA], BF16)
            NB = 4
            wr = [Ts("wr%d" % i, [128, 8, 512], BF16) for i in range(NB)]
            cub = [Ts("cub%d" % i, [128, 514], F32) for i in range(2)]
            usb = [Ts("usb%d" % i, [128, 512], F32) for i in range(2)]
            tmp = [Ts("tmp%d" % i, [128, 512], F32) for i in range(2)]
            sg = [Ts("sg%d" % i, [128, 512], F32) for i in range(2)]
            NSTG = 3
            stg = [Ts("stg%d" % i, [128, D], F32) for i in range(NSTG)]
            cw = Ts("cw", [128, 8, 3], F32)
            fcw = Ts("fcw", [128, NFC, 3], F32)
            carry_cu = Ts("carry_cu", [128, 8, 2], F32)
            carry_a = Ts("carry_a", [128, NFC, 2], F32)
            P.dma("pool", cw[:], cw_d, "cw", writes=["cw"])
            P.dma("pool", fcw[:], fcw_d, "fcw", writes=["fcw"])
            P.op("dve", lambda e: e.memset(carry_cu[:], 0.0), writes=["carry_cu"])
            P.op("dve", lambda e: e.memset(carry_a[:], 0.0), writes=["carry_a"])

            wstate = {"i": 0}

            def load_w(ci):
                s_ = wstate["i"] % NB
                wstate["i"] += 1
                nk, ncols, _ = CH[ci]
                P.dma("sp", wr[s_][:, 0:nk, 0:ncols], wscr[ci][:, 0:nk, 0:ncols], "wr%d" % s_, writes=["wr%d" % s_])
                return wr[s_], "wr%d" % s_

            stg_state = {"i": 0}

            def stg_next():
                i = stg_state["i"] % NSTG
                stg_state["i"] += 1
                return stg[i], "stg%d" % i

            out_events = []

            def conv_taps(tm, tres, src, sres, wcol, N):
                P.op("dve", lambda e: e.scalar_tensor_tensor(tm[:, 0:N], src[:, 1:1 + N], wcol[:, 1:2],
                                                             tm[:, 0:N], ALU.mult, ALU.add),
                     reads=list(sres) + [tres, "cw", "fcw"], writes=[tres])
                P.op("dve", lambda e: e.scalar_tensor_tensor(tm[:, 0:N], src[:, 0:N], wcol[:, 0:1],
                                                             tm[:, 0:N], ALU.mult, ALU.add),
                     reads=list(sres) + [tres, "cw", "fcw"], writes=[tres])

            def do_norm1(k):
                if k < 0:
                    sb, sres = stg_next()
                    P.dma("pool", sb[0:NMETA, :], meta, sres, writes=[sres])
                    norm_tile(sb[0:NMETA, :], NMETA, bufAm[:, :, 0:NMETA], g3[:, 0, :], sres, "bufAm")
                    return
                pend = []
                for t in range(4):
                    if t == NSTG:
                        norm_finish(*pend.pop(0))
                    sb, sres = stg_next()
                    r0 = k * 512 + t * 128
                    P.dma("pool", sb[:], x[r0:r0 + 128, :], sres, writes=[sres])
                    c = norm_stats(sb[:], 128, sres)
                    pend.append((c, sb[:], 128, bufA[:, :, t * 128:(t + 1) * 128], g3[:, 0, :], sres, "bufA"))
                for a_ in pend:
                    norm_finish(*a_)

            def block(bi, nxt):
                is_meta = bi < 0
                N = NMETA if is_meta else 512
                ntt = 1 if is_meta else 4
                rows = NMETA if is_meta else 128
                pos0 = 0 if is_meta else NMETA + bi * 512
                bA, bB, bC = (bufAm, bufBm, bufCm) if is_meta else (bufA, bufB, bufC)
                nA, nB, nC = ("bufAm", "bufBm", "bufCm") if is_meta else ("buf" + "A", "buf" + "B", "buf" + "C")
                if is_meta:
                    Zm, zres_m = stg_next()
                zrow = (lambda t, cs: Zm[0:rows, cs]) if is_meta else (lambda t, cs: Z[0:rows, t, cs])
                zres = (lambda t: zres_m) if is_meta else (lambda t: "Zt%d" % t)
                full = slice(0, D)
                if is_meta:
                    P.dma("pool", Zm[0:NMETA, :], meta, zres_m, writes=[zres_m])
                else:
                    for t in range(4):
                        r0 = bi * 512 + t * 128
                        P.dma("pool", Z[:, t, :], x[r0:r0 + 128, :], "Z%d" % t, writes=["Zt%d" % t])
                for cc in range(8):
                    wA, rA = yield (CI_A + cc)
                    pU, pC, pB = psalloc(ring2), psalloc(ring2), psalloc(ring2)
                    for (pb_, c0) in ((pU, 256), (pC, 128), (pB, 0)):
                        mm_group(banks[pb_][:, 0:N], [(wA[:, kc, c0:c0 + 128], bA[:, kc, 0:N]) for kc in range(8)],
                                 reads=[rA, nA], writes=["bk%d" % pb_])
                    k = cc % 2
                    cu, us, tm = cub[k], usb[k], tmp[k]
                    P.op("act", lambda e, us=us, pU=pU: e.activation(us[:, 0:N], banks[pU][:, 0:N], AF.Copy),
                         reads=["bk%d" % pU], writes=["usb%d" % k])
                    P.op("pool", lambda e, cu=cu, cc=cc: e.tensor_copy(cu[:, 0:2], carry_cu[:, cc, :]),
                         reads=["carry_cu"], writes=["cubc%d" % k])
                    P.op("dve", lambda e, cu=cu, us=us, pC=pC: e.tensor_tensor(
                        cu[:, 2:2 + N], banks[pC][:, 0:N], us[:, 0:N], ALU.mult),
                        reads=["bk%d" % pC, "usb%d" % k], writes=["cub%d" % k])
                    P.op("act", lambda e, cu=cu, tm=tm, cc=cc: e.activation(
                        tm[:, 0:N], cu[:, 2:2 + N], AF.Copy, scale=cw[:, cc, 2:3]),
                        reads=["cub%d" % k, "cw"], writes=["tmp%d" % k])
                    conv_taps(tm, "tmp%d" % k, cu, ["cub%d" % k, "cubc%d" % k], cw[:, cc, :], N)
                    P.op("pool", lambda e, cu=cu, cc=cc: e.tensor_copy(carry_cu[:, cc, :], cu[:, N:N + 2]),
                         reads=["cub%d" % k, "cubc%d" % k], writes=["carry_cu"])
                    P.op("dve", lambda e, tm=tm, pB=pB, cc=cc: e.tensor_tensor(
                        bB[:, cc, 0:N], banks[pB][:, 0:N], tm[:, 0:N], ALU.mult),
                        reads=["bk%d" % pB, "tmp%d" % k], writes=[nB])
                for oc in range(8):
                    wB, rB = yield (CI_B + oc)
                    pGA, pGC, pYA, pYC = (psalloc(ring2) for _ in range(4))
                    mm_group(banks[pGA][:, 0:N], [(wB[:, kc, 256:384], bA[:, kc, 0:N]) for kc in range(8)],
                             reads=[rB, nA], writes=["bk%d" % pGA])
                    mm_group(banks[pGC][:, 0:N], [(wB[:, kc, 384:512], bA[:, kc, 0:N]) for kc in range(8)],
                             reads=[rB, nA], writes=["bk%d" % pGC])
                    mm_group(banks[pYA][:, 0:N], [(wB[:, kc, 0:128], attT[:, kc, pos0:pos0 + N]) for kc in range(8)],
                             reads=[rB, "attT"], writes=["bk%d" % pYA])
                    mm_group(banks[pYC][:, 0:N], [(wB[:, kc, 128:256], bB[:, kc, 0:N]) for kc in range(8)],
                             reads=[rB, nB], writes=["bk%d" % pYC])
                    k = oc % 2
                    s1, s2, tm = sg[k], usb[k], tmp[k]
                    P.op("act", lambda e, s1=s1, pGA=pGA: e.activation(s1[:, 0:N], banks[pGA][:, 0:N], AF.Sigmoid),
                         reads=["bk%d" % pGA], writes=["sg%d" % k])
                    P.op("act", lambda e, s2=s2, pGC=pGC: e.activation(s2[:, 0:N], banks[pGC][:, 0:N], AF.Sigmoid),
                         reads=["bk%d" % pGC], writes=["usb%d" % k])
                    P.op("dve", lambda e, s1=s1, pYA=pYA, tm=tm: e.tensor_tensor(
                        tm[:, 0:N], banks[pYA][:, 0:N], s1[:, 0:N], ALU.mult),
                        reads=["bk%d" % pYA, "sg%d" % k], writes=["tmp%d" % k])
                    P.op("dve", lambda e, s2=s2, pYC=pYC: e.tensor_tensor(
                        s2[:, 0:N], banks[pYC][:, 0:N], s2[:, 0:N], ALU.mult),
                        reads=["bk%d" % pYC, "usb%d" % k], writes=["usb%d" % k])
                    P.op("dve", lambda e, s2=s2, tm=tm, oc=oc: e.tensor_tensor(
                        bC[:, oc, 0:N], tm[:, 0:N], s2[:, 0:N], ALU.add),
                        reads=["tmp%d" % k, "usb%d" % k], writes=[nC])
                if debug and bi == 0:
                    P.dma("pool", dbg_A, bA[:], "dbgA", reads=[nA])
                    P.dma("pool", dbg_B, bB[:], "dbgB", reads=[nB])
                    P.dma("pool", dbg_C, bC[:], "dbgC", reads=[nC])
                wO0 = yield (CI_C + 0)
                wO1 = yield (CI_C + 1)
                wO = [wO0, wO1]
                cst_ = {}
                for t in range(ntt + 1):
                    if t < ntt:
                        for half in range(2):
                            hs = slice(half * 512, (half + 1) * 512)
                            pb_ = psalloc(ring2)
                            mm_group(banks[pb_][0:rows, :],
                                     [(bC[:, kc, t * rows:(t + 1) * rows], wO[half][0][:, kc, :]) for kc in range(8)],
                                     reads=[wO[half][1], nC], writes=["bk%d" % pb_])
                            zz_ = zrow(t, hs)
                            P.op("dve", lambda e, pb_=pb_, zz_=zz_: e.tensor_tensor(
                                zz_, banks[pb_][0:rows, :], zz_, ALU.add),
                                reads=["bk%d" % pb_, zres(t)], writes=[zres(t)])
                    if t < ntt:
                        cst_[t] = norm_stats(zrow(t, full), rows, zres(t))
                    if t > 0:
                        tp_ = t - 1
                        norm_finish(cst_[tp_], zrow(tp_, full), rows, bA[:, :, tp_ * rows:(tp_ + 1) * rows],
                                    g3[:, 1, :], zres(tp_), nA)
                if debug and bi == 0:
                    P.dma("pool", dbg_Z, Z[:], "dbgZ", reads=["Zt0", "Zt1", "Zt2", "Zt3"])
                for p_ in range(NFC // 2):
                    wD, rD = yield (CI_D + p_)
                    fcs = (2 * p_, 2 * p_ + 1)
                    pA = [psalloc(ring2), psalloc(ring2)]
                    for i in range(2):
                        mm_group(banks[pA[i]][:, 0:N], [(wD[:, kc, i * 128:(i + 1) * 128], bA[:, kc, 0:N])
                                                         for kc in range(8)],
                                 reads=[rD, nA], writes=["bk%d" % pA[i]])
                    if not is_meta:
                        pV = [psalloc(ring2), psalloc(ring2)]
                        for i in range(2):
                            mm_group(banks[pV[i]][:, 0:N], [(wD[:, kc, 256 + i * 128:256 + (i + 1) * 128], bA[:, kc, 0:N])
                                                             for kc in range(8)],
                                     reads=[rD, nA], writes=["bk%d" % pV[i]])
                    for i in range(2):
                        fc = fcs[i]
                        ab, tm = cub[i], tmp[i]
                        P.op("pool", lambda e, ab=ab, fc=fc: e.tensor_copy(ab[:, 0:2], carry_a[:, fc, :]),
                             reads=["carry_a"], writes=["cubc%d" % i])
                        P.op("act", lambda e, ab=ab, pa=pA[i]: e.activation(ab[:, 2:2 + N], banks[pa][:, 0:N], AF.Copy),
                             reads=["bk%d" % pA[i]], writes=["cub%d" % i])
                        if not is_meta:
                            P.op("act", lambda e, tm=tm, pa=pA[i], fc=fc: e.activation(
                                tm[:, 0:N], banks[pa][:, 0:N], AF.Copy, scale=fcw[:, fc, 2:3]),
                                reads=["bk%d" % pA[i], "fcw"], writes=["tmp%d" % i])
                    if not is_meta:
                        for i in range(2):
                            fc = fcs[i]
                            ab, tm = cub[i], tmp[i]
                            P.op("dve", lambda e, tm=tm, ab=ab, fc=fc: e.scalar_tensor_tensor(
                                tm[:, 0:N], ab[:, 1:1 + N], fcw[:, fc, 1:2], tm[:, 0:N], ALU.mult, ALU.add),
                                reads=["cub%d" % i, "cubc%d" % i, "tmp%d" % i, "fcw"], writes=["tmp%d" % i])
                        for i in range(2):
                            fc = fcs[i]
                            ab, tm = cub[i], tmp[i]
                            P.op("dve", lambda e, tm=tm, ab=ab, fc=fc: e.scalar_tensor_tensor(
                                tm[:, 0:N], ab[:, 0:N], fcw[:, fc, 0:1], tm[:, 0:N], ALU.mult, ALU.add),
                                reads=["cub%d" % i, "cubc%d" % i, "tmp%d" % i, "fcw"], writes=["tmp%d" % i])
                    for i in range(2):
                        fc = fcs[i]
                        ab = cub[i]
                        P.op("pool", lambda e, ab=ab, fc=fc: e.tensor_copy(carry_a[:, fc, :], ab[:, N:N + 2]),
                             reads=["cub%d" % i, "cubc%d" % i], writes=["carry_a"])
                    if is_meta:
                        continue
                    for i in range(2):
                        tm, s1 = tmp[i], sg[i]
                        P.op("act", lambda e, s1=s1, tm=tm: e.activation(s1[:, 0:N], tm[:, 0:N], AF.Silu),
                             reads=["tmp%d" % i], writes=["sg%d" % i])
                    for i in range(2):
                        fc = fcs[i]
                        s1 = sg[i]
                        P.op("dve", lambda e, s1=s1, pv=pV[i], fc=fc: e.tensor_tensor(
                            gT[:, fc, 0:N], banks[pv][:, 0:N], s1[:, 0:N], ALU.mult),
                            reads=["bk%d" % pV[i], "sg%d" % i], writes=["gT"])
                if is_meta:
                    return
                if debug and bi == 0:
                    P.dma("pool", dbg_G, gT[:], "dbgG", reads=["gT"])
                for half in range(2):
                    hs = slice(half * 512, (half + 1) * 512)
                    pbs = [psalloc(ring2) for _ in range(4)]
                    for kg in range(3):
                        k0 = kg * 8
                        nk = min(8, NFC - k0)
                        wE, rE = yield (CI_E + half * 3 + kg)
                        for t in range(4):
                            def dmm(e, t=t, k0=k0, nk=nk, wE=wE, pb_=pbs[t]):
                                ins = None
                                for kk in range(nk):
                                    ins = e.matmul(banks[pb_][:, :], lhsT=gT[:, k0 + kk, t * 128:(t + 1) * 128],
                                                   rhs=wE[:, kk, :], start=(k0 + kk == 0), stop=(k0 + kk == NFC - 1))
                                return ins
                            P.op("pe", dmm, reads=[rE, "gT"], writes=["bk%d" % pbs[t]])
                    for t in range(4):
                        P.op("dve", lambda e, t=t, hs=hs, pb_=pbs[t]: e.tensor_tensor(
                            Z[:, t, hs], banks[pb_][:, :], Z[:, t, hs], ALU.add),
                            reads=["bk%d" % pbs[t], "Zt%d" % t], writes=["Zt%d" % t])
                    if half == 0 and nxt is not None:
                        do_norm1(nxt)
                for t in range(4):
                    c = norm_stats(Z[:, t, :], 128, "Zt%d" % t)
                    o, ores = stg_next()
                    P.op("dve", lambda e, o=o, t=t, c=c: e.scalar_tensor_tensor(
                        o[:], Z[:, t, :], rs[:, c:c + 1], g3[:, 2, :], ALU.mult, ALU.mult),
                        reads=["Zt%d" % t, "rs%d" % c, "g3"], writes=[ores])
                    r0 = bi * 512 + t * 128
                    out_events.append(P.dma("pool", out[r0:r0 + 128, :], o[:], ores, reads=[ores]))

            def drive(gens):
                req = []
                for g_ in gens:
                    try:
                        req.append(next(g_))
                    except StopIteration:
                        req.append(None)
                while any(r is not None for r in req):
                    ci = next(r for r in req if r is not None)
                    w = load_w(ci)
                    for i_, g_ in enumerate(gens):
                        if req[i_] is None:
                            continue
                        assert req[i_] == ci, (req, ci)
                        try:
                            req[i_] = g_.send(w)
                        except StopIteration:
                            req[i_] = None

            do_norm1(-1)
            do_norm1(0)
            for bi in range(p2_blocks):
                nxt = bi + 1 if bi + 1 < p2_blocks else None
                gens = [block(bi, nxt)]
                if bi == 0:
                    gens = [block(-1, None)] + gens
                drive(gens)
            P.barrier()
            P.wait_events("sp", out_events)
        P.emit()
    return nc


_NC_CACHE = {}


def _consts():
    s = np.arange(128)[:, None]
    t = np.arange(128)[None, :]
    c = np.zeros((128, 4, 128), np.float32)
    c[:, 0, :] = np.eye(128, dtype=np.float32)
    c[:, 1, :] = np.where(s > t, NEG, 0.0)
    c[:, 2, :] = (s <= t).astype(np.float32)
    c[:, 3, :] = 1.0
    return c


def make_in_maps(x, meta_tokens, g_mix, w_in, b_f, conv_w, w_o_attn, w_o_conv, w_o,
                 g_ffn, w_ffn_in, ffn_conv_w, w_ffn_out, g_final):
    f = lambda a: np.ascontiguousarray(np.asarray(a, dtype=np.float32))
    g3 = np.stack([f(g_mix)[0], f(g_ffn)[0], f(g_final)], 0)
    g3 = np.ascontiguousarray(np.broadcast_to(g3[None], (128, 3, D)))
    bfrep = np.ascontiguousarray(np.broadcast_to(f(b_f)[0][None, None, :], (128, NT, 8)).reshape(128, NT * 8))
    cw = np.ascontiguousarray(f(conv_w)[0].T.reshape(8, 128, 3).transpose(1, 0, 2))
    fcw = np.ascontiguousarray(f(ffn_conv_w)[0].T.reshape(NFC, 128, 3).transpose(1, 0, 2))
    shared = {
        "meta": f(meta_tokens), "w_in": f(w_in)[0], "w_o_attn": f(w_o_attn)[0], "w_o_conv": f(w_o_conv)[0],
        "w_o": f(w_o)[0], "w_ffn_in": f(w_ffn_in)[0], "w_ffn_out": f(w_ffn_out)[0],
        "g3": g3, "bfrep": bfrep, "cw": cw, "fcw": fcw, "cst": _consts(),
    }
    xs_ = f(x)
    return [dict(shared, x=xs_[b]) for b in range(xs_.shape[0])]


def kernel(**inputs):
    in_maps = make_in_maps(**inputs)
    if "nc" not in _NC_CACHE:
        _NC_CACHE["nc"] = build_nc()
    res = run_bass_kernel_spmd(_NC_CACHE["nc"], in_maps, core_ids=list(range(8)))
    return np.stack([r["out"] for r in res.results], 0).astype(np.float32)
```

```python
import contextlib
import numpy as np
import concourse.bass as bass
import concourse.mybir as mybir
from concourse.bass_utils import run_bass_kernel_spmd

F32 = mybir.dt.float32
BF16 = mybir.dt.bfloat16
AF = mybir.ActivationFunctionType
ALU = mybir.AluOpType

D = 1024
SEQ = 4096
NMETA = 16
L = SEQ + NMETA
NT = 33
NPOS = NT * 128
DFF = 2816
NFC = DFF // 128
DIN = 8200
KOFF, VOFF, FOFF, BOFF, COFF, UOFF, GAOFF, GCOFF = 1024, 2048, 3072, 3080, 4104, 5128, 6152, 7176
SCALE = float(128 ** -0.5)
EPS = 1e-6
NEG = -30000.0

ENGS = ("pe", "act", "dve", "pool", "sp")


class Prog:
    def __init__(self, nc, same_engine_sync=True):
        self.nc = nc
        self.ops = {e: [] for e in ENGS}
        self.count = {}
        self.waited = {e: {} for e in ENGS}
        self.last_write = {}
        self.readers = {}
        self.same_engine_sync = same_engine_sync
        self.dma_keys = []

    def _deps(self, eng, reads, writes):
        evs = []
        for r in reads:
            e = self.last_write.get(r)
            if e is not None:
                evs.append(e)
        for r in writes:
            e = self.last_write.get(r)
            if e is not None:
                evs.append(e)
            evs.extend(self.readers.get(r, ()))
        best = {}
        w = self.waited[eng]
        for (k, v) in evs:
            if k == ("eng", eng) and (eng == "pe" or not self.same_engine_sync):
                continue
            if w.get(k, 0) >= v:
                continue
            best[k] = max(best.get(k, 0), v)
        for k, v in best.items():
            w[k] = v
        return list(best.items())

    def _commit(self, ev, reads, writes):
        for r in reads:
            self.readers.setdefault(r, []).append(ev)
        for r in writes:
            self.last_write[r] = ev
            self.readers[r] = []

    def op(self, eng, fn, reads=(), writes=()):
        waits = self._deps(eng, reads, writes)
        k = ("eng", eng)
        self.count[k] = self.count.get(k, 0) + 1
        ev = (k, self.count[k])
        self.ops[eng].append((fn, waits, k, 1))
        self._commit(ev, reads, writes)
        return ev

    def dma(self, eng, out, in_, key, reads=(), writes=()):
        waits = self._deps(eng, reads, writes)
        k = ("dma", key)
        if k not in self.count:
            self.dma_keys.append(k)
        self.count[k] = self.count.get(k, 0) + 16
        ev = (k, self.count[k])
        fn = lambda e, out=out, in_=in_: e.dma_start(out=out, in_=in_)
        self.ops[eng].append((fn, waits, k, 16))
        self._commit(ev, reads, writes)
        return ev

    def wait_events(self, eng, events):
        waits = []
        for (k, v) in events:
            if self.waited[eng].get(k, 0) < v:
                self.waited[eng][k] = v
                waits.append((k, v))
        if waits:
            self.ops[eng].append((None, waits, None, 0))

    def barrier(self):
        evs = [(k, v) for k, v in self.count.items() if v > 0]
        for e in ENGS:
            self.wait_events(e, [(k, v) for (k, v) in evs if not (k == ("eng", e) and e in ("pe", "sp"))])
        self.last_write = {}
        self.readers = {}

    def emit(self):
        nc = self.nc
        with contextlib.ExitStack() as st:
            sems = {}
            keys = [("eng", e) for e in ENGS] + self.dma_keys
            for i, k in enumerate(keys):
                if self.count.get(k, 0) == 0:
                    continue
                sems[k] = st.enter_context(nc.semaphore("s%d" % i))
            block = st.enter_context(nc.Block())

            def run(eng_name):
                def body(e):
                    for (fn, waits, k, n) in self.ops[eng_name]:
                        for (wk, wv) in waits:
                            e.wait_ge(sems[wk], wv)
                        if fn is None:
                            continue
                        fn(e).then_inc(sems[k], n)
                return body

            block.tensor(run("pe"))
            block.scalar(run("act"))
            block.vector(run("dve"))
            block.gpsimd(run("pool"))
            block.sync(run("sp"))


def build_nc(debug=False, p2_blocks=8):
    nc = bass.Bass("TRN2", target_bir_lowering=False)
    dt_in = lambda name, shape: nc.dram_tensor(name, shape, F32, kind="ExternalInput").ap()
    x = dt_in("x", [SEQ, D])
    meta = dt_in("meta", [NMETA, D])
    w_in = dt_in("w_in", [D, DIN])
    w_oa = dt_in("w_o_attn", [D, D])
    w_oc = dt_in("w_o_conv", [D, D])
    w_o = dt_in("w_o", [D, D])
    w_fi = dt_in("w_ffn_in", [D, 2 * DFF])
    w_fo = dt_in("w_ffn_out", [DFF, D])
    g3_d = dt_in("g3", [128, 3, D])
    bf_d = dt_in("bfrep", [128, NT * 8])
    cw_d = dt_in("cw", [128, 8, 3])
    fcw_d = dt_in("fcw", [128, NFC, 3])
    cst_d = dt_in("cst", [128, 4, 128])
    out = nc.dram_tensor("out", [SEQ, D], F32, kind="ExternalOutput").ap()
    if debug:
        dbg_att = nc.dram_tensor("dbg_att", [128, 8, L], F32, kind="ExternalOutput").ap()
        dbg_c = nc.dram_tensor("dbg_c", [128, NT * 8], F32, kind="ExternalOutput").ap()
        dbg_B = nc.dram_tensor("dbg_B", [128, 8, 512], BF16, kind="ExternalOutput").ap()
        dbg_C = nc.dram_tensor("dbg_C", [128, 8, 512], BF16, kind="ExternalOutput").ap()
        dbg_A = nc.dram_tensor("dbg_A", [128, 8, 512], BF16, kind="ExternalOutput").ap()
        dbg_G = nc.dram_tensor("dbg_G", [128, NFC, 512], BF16, kind="ExternalOutput").ap()
        dbg_Z = nc.dram_tensor("dbg_Z", [128, 4, D], F32, kind="ExternalOutput").ap()

    kview = lambda w: w.rearrange("(kc p) c -> p kc c", p=128)
    w_in_v, w_oa_v, w_oc_v, w_o_v, w_fi_v, w_fo_v = map(kview, (w_in, w_oa, w_oc, w_o, w_fi, w_fo))

    CH = []
    for cc in range(8):
        CH.append((8, 384, [(i * 128, w_in_v[:, :, o_ + cc * 128:o_ + (cc + 1) * 128], 128)
                            for i, o_ in enumerate((BOFF, COFF, UOFF))]))
    for oc in range(8):
        cs_ = slice(oc * 128, (oc + 1) * 128)
        CH.append((8, 512, [(0, w_oa_v[:, :, cs_], 128), (128, w_oc_v[:, :, cs_], 128),
                            (256, w_in_v[:, :, GAOFF + oc * 128:GAOFF + (oc + 1) * 128], 128),
                            (384, w_in_v[:, :, GCOFF + oc * 128:GCOFF + (oc + 1) * 128], 128)]))
    for half in range(2):
        CH.append((8, 512, [(0, w_o_v[:, :, half * 512:(half + 1) * 512], 512)]))
    for p_ in range(NFC // 2):
        CH.append((8, 512, [(0, w_fi_v[:, :, p_ * 256:(p_ + 1) * 256], 256),
                            (256, w_fi_v[:, :, DFF + p_ * 256:DFF + (p_ + 1) * 256], 256)]))
    for half in range(2):
        for kg in range(3):
            k0_ = kg * 8
            nk_ = min(8, NFC - k0_)
            CH.append((nk_, 512, [(0, w_fo_v[:, k0_:k0_ + nk_, half * 512:(half + 1) * 512], 512)]))
    NCHUNK = len(CH)
    assert NCHUNK == 35
    CI_A, CI_B, CI_C, CI_D, CI_E = 0, 8, 16, 18, 29
    wscr = nc.dram_tensor("wscr", [NCHUNK, 128, 8, 512], BF16, kind="Internal").ap()
    conv_state = {"i": 0, "n": 0}

    P = Prog(nc)

    def convert_chunks(n):
        for _ in range(n):
            ci = conv_state["i"]
            if ci >= NCHUNK:
                return
            conv_state["i"] += 1
            nk, ncols, pieces = CH[ci]
            for (c0, src, w) in pieces:
                conv_state["n"] += 1
                P.dma("pool", wscr[ci][:, 0:nk, c0:c0 + w], src, "cv%d" % (conv_state["n"] % 4), writes=["scr%d" % conv_state["n"]])

    with contextlib.ExitStack() as st0:
        T0 = lambda name, shape, dt: st0.enter_context(nc.sbuf_tensor("sb_" + name, shape, dt))
        attT = T0("attT", [128, 8, L], BF16)
        cstb = T0("cstb", [128, 4, 128], BF16)
        ss = T0("ss", [128, 8], F32)
        rs = T0("rs", [128, 8], F32)
        NORM = {"xs": None, "junk": None}
        ident, negmask, uincl, ones = (cstb[:, i, :] for i in range(4))
        TR = {"banks": None}
        BK = {"banks": None}

        with nc.sbuf_tensor("sb_cstf", [128, 4, 128], F32) as cstf:
            P.dma("sp", cstf[:], cst_d, "cst", writes=["cstf"])
            P.op("dve", lambda e: e.tensor_copy(cstb[:], cstf[:]), reads=["cstf"], writes=["cst"])
            P.barrier()

        ring_state = {"i": 0}

        def psalloc(ring):
            i = ring[ring_state["i"] % len(ring)]
            ring_state["i"] += 1
            return i

        evac_state = {"i": 0}

        def copy_any(out_ap, in_ap, reads, writes):
            evac_state["i"] += 1
            mode = evac_state.get("mode", "both")
            if mode == "act" or (mode == "both" and evac_state["i"] % 2):
                P.op("act", lambda e: e.activation(out_ap, in_ap, AF.Copy), reads=reads, writes=writes)
            else:
                P.op("dve", lambda e: e.tensor_copy(out_ap, in_ap), reads=reads, writes=writes)

        def mm_group(out_ap, pairs, reads, writes, first_start=True):
            def fn(e):
                n = len(pairs)
                ins = None
                for i, (l, r) in enumerate(pairs):
                    ins = e.matmul(out_ap, lhsT=l, rhs=r, start=(first_start and i == 0), stop=(i == n - 1))
                return ins
            P.op("pe", fn, reads=reads, writes=writes)

        norm_ctr = {"i": 0}

        def norm_stats(src, rows, src_res):
            i = norm_ctr["i"]
            norm_ctr["i"] += 1
            c = i % 8
            junk = NORM["junk"]
            P.op("dve", lambda e: e.memset(ss[:, c:c + 1], 0.0), writes=["ss%d" % c])
            P.op("act", lambda e: e.activation(junk[0:rows, :], src, AF.Square, accum_out=ss[0:rows, c:c + 1]),
                 reads=[src_res], writes=["ss%d" % c, "junk"])
            P.op("act", lambda e: e.activation(rs[:, c:c + 1], ss[:, c:c + 1], AF.Ln, bias=EPS, scale=1.0 / D),
                 reads=["ss%d" % c], writes=["rs%d" % c])
            P.op("act", lambda e: e.activation(rs[:, c:c + 1], rs[:, c:c + 1], AF.Exp, scale=-0.5),
                 reads=["rs%d" % c], writes=["rs%d" % c])
            return c

        fin_ctr = {"i": 0}

        def norm_finish(c, src, rows, dst, grow, src_res, dst_res):
            i = fin_ctr["i"]
            fin_ctr["i"] += 1
            xi = i % 2
            xb = NORM["xs"][xi]
            P.op("dve", lambda e: e.scalar_tensor_tensor(
                xb[0:rows, :], src, rs[0:rows, c:c + 1], grow[0:rows, :], ALU.mult, ALU.mult),
                reads=[src_res, "rs%d" % c, "g3"], writes=["xs%d" % xi])
            nb_ = len(TR["banks"])
            ti = i % nb_
            trb = TR["banks"][ti]

            def tr(e):
                ins = None
                for kc in range(8):
                    ins = e.transpose(trb[:, kc, 0:rows], xb[0:rows, kc * 128:(kc + 1) * 128], ident[0:rows, 0:rows])
                return ins
            P.op("pe", tr, reads=["xs%d" % xi, "cst"], writes=["tr%d" % ti])
            copy_any(dst, trb[:, :, 0:rows], reads=["tr%d" % ti], writes=[dst_res])

        def norm_tile(src, rows, dst, grow, src_res, dst_res):
            c = norm_stats(src, rows, src_res)
            norm_finish(c, src, rows, dst, grow, src_res, dst_res)

        with contextlib.ExitStack() as st1:
            T1 = lambda name, shape, dt: st1.enter_context(nc.sbuf_tensor("sb_" + name, shape, dt))
            hT = T1("hT", [128, 8, NPOS], BF16)
            cpos = T1("cpos", [128, NT * 8], F32)
            off = T1("off", [128, (NT + 1) * 8], F32)
            TR["banks"] = [st1.enter_context(nc.psum_tensor("tr_ps", [128, 8, 128], BF16))]
            banks = [st1.enter_context(nc.psum_tensor("bank%d" % i, [128, 512], F32)) for i in range(7)]
            ring1 = [0, 1, 2, 3]
            O_bs, L_b = [banks[4], banks[5]], banks[6]

            with contextlib.ExitStack() as st:
                Ts = lambda name, shape, dt: st.enter_context(nc.sbuf_tensor("sb_" + name, shape, dt))
                xt = [Ts("xt%d" % i, [128, 4, D], F32) for i in range(2)]
                g1 = Ts("g1", [128, D], F32)
                NORM["xs"] = [Ts("xs1_%d" % i, [128, D], BF16) for i in range(2)]
                NORM["junk"] = Ts("junk1", [128, D], BF16)
                P.dma("sp", g1[:], g3_d[:, 0, :], "g3", writes=["g3"])
                P.op("pool", lambda e: e.memset(hT[:, :, L:NPOS], 0.0), writes=["hT"])
                P.dma("sp", xt[0][0:NMETA, 0, :], meta, "xt0", writes=["xt0"])
                norm_tile(xt[0][0:NMETA, 0, :], NMETA, hT[:, :, 0:NMETA], g1[:], "xt0", "hT")
                for b in range(8):
                    xb_ = xt[(b + 1) % 2]
                    res = "xt%d" % ((b + 1) % 2)
                    P.dma("sp", xb_[:], x[b * 512:(b + 1) * 512, :].rearrange("(t p) d -> p t d", p=128),
                          res, writes=[res])
                    cs4 = [norm_stats(xb_[:, t, :], 128, res) for t in range(4)]
                    for t in range(4):
                        p0 = NMETA + b * 512 + t * 128
                        norm_finish(cs4[t], xb_[:, t, :], 128, hT[:, :, p0:p0 + 128], g1[:], res, "hT")
            P.barrier()

            with contextlib.ExitStack() as st:
                Ts = lambda name, shape, dt: st.enter_context(nc.sbuf_tensor("sb_" + name, shape, dt))
                wf = Ts("wf", [128, 8, 8], BF16)
                bfr = Ts("bfr", [128, NT * 8], F32)
                fb = Ts("fb", [128, NT * 8], F32)
                r1 = Ts("r1", [128, NT * 8], F32)
                parts = [Ts("part%d" % i, [128, NT * 8], BF16) for i in range(3)]
                tot = Ts("tot", [128, NT * 8], F32)
                P.dma("pool", wf[:], w_in_v[:, :, FOFF:FOFF + 8], "wf", writes=["wf"])
                P.dma("sp", bfr[:], bf_d, "bfr", writes=["bfr"])
                psF, psC, psT = banks[0], banks[1], banks[2]

                def fmm(e):
                    ins = None
                    for j in range(NT):
                        for kc in range(8):
                            ins = e.matmul(psF[:, j * 8:(j + 1) * 8], lhsT=hT[:, kc, j * 128:(j + 1) * 128],
                                           rhs=wf[:, kc, :], start=(kc == 0), stop=(kc == 7))
                    return ins
                P.op("pe", fmm, reads=["hT", "wf"], writes=["bank0"])
                NF = NT * 8
                P.op("dve", lambda e: e.tensor_tensor(fb[:], psF[:, 0:NF], bfr[:], ALU.add),
                     reads=["bank0", "bfr"], writes=["fb"])
                P.op("act", lambda e: e.activation(fb[:], fb[:], AF.Exp, scale=-1.0), reads=["fb"], writes=["fb"])
                P.op("act", lambda e: e.activation(fb[:], fb[:], AF.Ln, bias=1.0), reads=["fb"], writes=["fb"])
                P.op("dve", lambda e: e.tensor_copy(parts[0][:], fb[:]), reads=["fb"], writes=["p0"])
                P.op("dve", lambda e: e.tensor_tensor(r1[:], fb[:], parts[0][:], ALU.subtract),
                     reads=["fb", "p0"], writes=["r1"])
                P.op("dve", lambda e: e.tensor_copy(parts[1][:], r1[:]), reads=["r1"], writes=["p1"])
                P.op("dve", lambda e: e.tensor_tensor(r1[:], r1[:], parts[1][:], ALU.subtract),
                     reads=["r1", "p1"], writes=["r1"])
                P.op("dve", lambda e: e.tensor_copy(parts[2][:], r1[:]), reads=["r1"], writes=["p2"])
                mm_group(psC[:, 0:NF], [(uincl, parts[i][:]) for i in range(3)],
                         reads=["cst", "p0", "p1", "p2"], writes=["bank1"])
                mm_group(psT[:, 0:NF], [(ones, parts[i][:]) for i in range(3)],
                         reads=["cst", "p0", "p1", "p2"], writes=["bank2"])
                P.op("dve", lambda e: e.tensor_copy(tot[:], psT[:, 0:NF]), reads=["bank2"], writes=["tot"])
                P.op("dve", lambda e: e.memset(off[:, 0:8], 0.0), writes=["off"])
                for j in range(1, NT + 1):
                    P.op("dve", lambda e, j=j: e.tensor_tensor(off[:, j * 8:(j + 1) * 8], off[:, (j - 1) * 8:j * 8],
                                                               tot[:, (j - 1) * 8:j * 8], ALU.add),
                         reads=["off", "tot"], writes=["off"])
                P.op("dve", lambda e: e.tensor_tensor(cpos[:], psC[:, 0:NF], off[:, 0:NF], ALU.add),
                     reads=["bank1", "off"], writes=["cpos"])
                if debug:
                    P.dma("sp", dbg_c, cpos[:], "dbgc", reads=["cpos"])
            P.barrier()

            with contextlib.ExitStack() as st:
                Ts = lambda name, shape, dt: st.enter_context(nc.sbuf_tensor("sb_" + name, shape, dt))
                qk = [(Ts("qT%d" % i, [128, NPOS], BF16), Ts("kT%d" % i, [128, NPOS], BF16)) for i in range(2)]
                Vts = [Ts("Vt%d" % i, [128, NPOS], BF16) for i in range(2)]
                wqkv = [Ts("wqkv%d" % i, [128, 8, 3, 128], BF16) for i in range(2)]
                Bh = [Ts("Bh%d" % i, [128, 17, 34], F32) for i in range(2)]
                NPT = 5
                PT = [Ts("PT%d" % i, [128, 512], BF16) for i in range(NPT)]
                rl = [Ts("rl%d" % i, [128, 512], F32) for i in range(1)]
                cpos3 = cpos[:].rearrange("p (j h) -> p j h", h=8)
                evac_state["mode"] = "dve"
                pt_i = 0
                rl_i = 0

                def load_head_w(h):
                    wq = wqkv[h % 2]
                    wres = "wqkv%d" % (h % 2)
                    for t in range(3):
                        P.dma("pool", wq[:, :, t, :], w_in_v[:, :, t * 1024 + h * 128:t * 1024 + (h + 1) * 128],
                              wres + "_%d" % t, writes=[wres])

                def proj_groups(h):
                    wq = wqkv[h % 2]
                    wres = "wqkv%d" % (h % 2)
                    out_ = []
                    for (t, dst, dres) in ((0, qk[h % 2][0], "qT%d" % (h % 2)), (1, qk[h % 2][1], "kT%d" % (h % 2))):
                        for n in range(9):
                            def g(t=t, dst=dst, dres=dres, n=n):
                                c0 = n * 512
                                w = min(512, NPOS - c0)
                                b = psalloc(ring1)
                                mm_group(banks[b][:, 0:w], [(wq[:, kc, t, :], hT[:, kc, c0:c0 + w]) for kc in range(8)],
                                         reads=[wres, "hT"], writes=["bank%d" % b])
                                copy_any(dst[:, c0:c0 + w], banks[b][:, 0:w], reads=["bank%d" % b], writes=[dres])
                            out_.append(g)
                    Vd = Vts[h % 2]
                    vres_ = "Vt%d" % (h % 2)
                    for j4 in range(0, NT, 4):
                        def gv(j4=j4):
                            nj = min(4, NT - j4)
                            b = psalloc(ring1)

                            def vmm(e, j4=j4, nj=nj, bk=banks[b], wq=wq):
                                ins = None
                                for jj in range(nj):
                                    j = j4 + jj
                                    for kc in range(8):
                                        ins = e.matmul(bk[:, jj * 128:(jj + 1) * 128],
                                                       lhsT=hT[:, kc, j * 128:(j + 1) * 128], rhs=wq[:, kc, 2, :],
                                                       start=(kc == 0), stop=(kc == 7))
                                return ins
                            P.op("pe", vmm, reads=[wres, "hT"], writes=["bank%d" % b])
                            copy_any(Vd[:, j4 * 128:(j4 + nj) * 128], banks[b][:, 0:nj * 128],
                                     reads=["bank%d" % b], writes=[vres_])
                        out_.append(gv)
                    return out_

                load_head_w(0)
                for g_ in proj_groups(0):
                    g_()
                for h in range(8):
                    wq = wqkv[h % 2]
                    wres = "wqkv%d" % (h % 2)
                    qT, kT = qk[h % 2]
                    qres, kres = "qT%d" % (h % 2), "kT%d" % (h % 2)
                    if h + 1 < 8:
                        load_head_w(h + 1)
                    convert_chunks(6)
                    bh = Bh[h % 2]
                    bres = "Bh%d" % (h % 2)
                    for m in range(17):
                        jn = min(2 * m + 2, NT)
                        P.op("dve", lambda e, m=m, jn=jn, bh=bh, h=h: e.tensor_scalar(
                            bh[:, m, 0:jn], cpos3[:, 0:jn, h], off[:, (2 * m + 1) * 8 + h:(2 * m + 1) * 8 + h + 1],
                            None, ALU.subtract), reads=["cpos", "off"], writes=[bres])
                    Vt = Vts[h % 2]
                    vres = "Vt%d" % (h % 2)
                    pending = proj_groups(h + 1) if h + 1 < 8 else []
                    steps = []
                    for qb in range(9):
                        q0 = qb * 512
                        qend = min(q0 + 512, L)
                        jlast = (qend - 1) // 128
                        for j in range(jlast + 1):
                            steps.append((qb, q0, qend, jlast, j))
                    LA = 3
                    infl = {}

                    def issue_qk(i):
                        nonlocal pt_i
                        (qb, q0, qend, jlast, j) = steps[i]
                        c0 = max(q0, 128 * j)
                        ncols = qend - c0
                        diag = (128 * j >= q0)
                        b = psalloc(ring1)
                        S = banks[b]

                        def smm(e, j=j, c0=c0, ncols=ncols, diag=diag, S=S, qT=qT, kT=kT):
                            kt = kT[:, j * 128:(j + 1) * 128]
                            if not diag:
                                return e.matmul(S[:, 0:ncols], lhsT=kt, rhs=qT[:, c0:c0 + ncols],
                                                start=True, stop=True)
                            wd = min(128, ncols)
                            e.matmul(S[:, 0:wd], lhsT=ident, rhs=negmask[:, 0:wd], start=True, stop=False,
                                     skip_group_check=True)
                            ins = e.matmul(S[:, 0:wd], lhsT=kt, rhs=qT[:, c0:c0 + wd], start=False, stop=True,
                                           skip_group_check=True)
                            if ncols > wd:
                                ins = e.matmul(S[:, wd:ncols], lhsT=kt, rhs=qT[:, c0 + wd:c0 + ncols],
                                               start=False, stop=True, skip_group_check=True)
                            return ins
                        P.op("pe", smm, reads=[qres, kres, "cst"], writes=["bank%d" % b])
                        pt = PT[pt_i % NPT]
                        pres = "PT%d" % (pt_i % NPT)
                        pt_i += 1
                        m_lo, m_hi = c0 // 256, (qend - 1) // 256
                        for m in range(m_lo, m_hi + 1):
                            a = max(c0, 256 * m) - c0
                            bnd = min(qend, 256 * m + 256) - c0
                            P.op("act", lambda e, pt=pt, S=S, a=a, bnd=bnd, bh=bh, m=m, j=j: e.activation(
                                pt[:, a:bnd], S[:, a:bnd], AF.Exp, bias=bh[:, m, j:j + 1], scale=SCALE),
                                reads=["bank%d" % b, bres], writes=[pres])
                        infl[i] = (pt, pres, c0, ncols)

                    def issue_pv(i):
                        nonlocal rl_i
                        (qb, q0, qend, jlast, j) = steps[i]
                        (pt, pres, c0, ncols) = infl.pop(i)
                        o0 = c0 - q0
                        nq = qend - q0

                        O_b = O_bs[qb % 2]
                        ores_ = "bank%d" % (4 + qb % 2)

                        def pvmm(e, j=j, o0=o0, ncols=ncols, pt=pt, jlast=jlast, O_b=O_b, Vt=Vt):
                            e.matmul(O_b[:, o0:o0 + ncols], lhsT=Vt[:, j * 128:(j + 1) * 128], rhs=pt[:, 0:ncols],
                                     start=(j == 0), stop=(j == jlast), skip_group_check=True)
                            return e.matmul(L_b[:, o0:o0 + ncols], lhsT=ones, rhs=pt[:, 0:ncols],
                                            start=(j == 0), stop=(j == jlast), skip_group_check=True)
                        P.op("pe", pvmm, reads=[pres, vres, "cst"], writes=[ores_, "bank6"])
                        if j == jlast:
                            r = rl[0]
                            rres = "rl0"
                            rl_i += 1
                            P.op("dve", lambda e, r=r, nq=nq: e.reciprocal(r[:, 0:nq], L_b[:, 0:nq]),
                                 reads=["bank6"], writes=[rres])
                            P.op("dve", lambda e, r=r, nq=nq, h=h, q0=q0, O_b=O_b: e.tensor_tensor(
                                attT[:, h, q0:q0 + nq], O_b[:, 0:nq], r[:, 0:nq], ALU.mult),
                                reads=[ores_, rres], writes=["attT"])

                    every = max(1, len(steps) // (len(pending) + 1)) if pending else 0
                    for i in range(len(steps) + LA):
                        if i < len(steps):
                            issue_qk(i)
                        if i >= LA:
                            issue_pv(i - LA)
                        if pending and i % every == every - 1:
                            pending.pop(0)()
                    while pending:
                        pending.pop(0)()
            P.barrier()

        if debug:
            with contextlib.ExitStack() as st:
                dbt = st.enter_context(nc.sbuf_tensor("dbt", [128, 4, L], F32))
                for hh in range(2):
                    P.op("dve", lambda e, hh=hh: e.tensor_copy(dbt[:], attT[:, hh * 4:(hh + 1) * 4, :]),
                         reads=["attT"], writes=["dbt"])
                    P.dma("sp", dbg_att[:, hh * 4:(hh + 1) * 4, :], dbt[:], "dbga", reads=["dbt"])
                P.barrier()

        with contextlib.ExitStack() as st:
            Ts = lambda name, shape, dt: st.enter_context(nc.sbuf_tensor("sb_" + name, shape, dt))
            TR["banks"] = [st.enter_context(nc.psum_tensor("tr2_%d" % i, [128, 8, 128], BF16)) for i in range(2)]
            banks = [st.enter_context(nc.psum_tensor("bk2_%d" % i, [128, 512], F32)) for i in range(6)]
            ring2 = [0, 1, 2, 3, 4, 5]
            evac_state["mode"] = "both"
            g3 = Ts("g3", [128, 3, D], F32)
            NORM["xs"] = [Ts("xs2_%d" % i, [128, D], BF16) for i in range(2)]
            NORM["junk"] = Ts("junk2", [128, D], BF16)
            P.dma("pool", g3[:], g3_d, "g3", writes=["g3"])
            Z = Ts("Z", [128, 4, D], F32)
            bufA = Ts("bufA", [128, 8, 512], BF16)
            bufB = Ts("bufB", [128, 8, 512], BF16)
            bufC = Ts("bufC", [128, 8, 512], BF16)
            gT = Ts("gT", [128, NFC, 512], BF16)
            bufAm = Ts("bufAm", [128, 8, NMETA], BF16)
            bufBm = Ts("bufBm", [128, 8, NMETA], BF16)
            bufCm = Ts("bufCm", [128, 8, NMETA], BF16)
            NB = 4
            wr = [Ts("wr%d" % i, [128, 8, 512], BF16) for i in range(NB)]
            cub = [Ts("cub%d" % i, [128, 514], F32) for i in range(2)]
            usb = [Ts("usb%d" % i, [128, 512], F32) for i in range(2)]
            tmp = [Ts("tmp%d" % i, [128, 512], F32) for i in range(2)]
            sg = [Ts("sg%d" % i, [128, 512], F32) for i in range(2)]
            NSTG = 3
            stg = [Ts("stg%d" % i, [128, D], F32) for i in range(NSTG)]
            cw = Ts("cw", [128, 8, 3], F32)
            fcw = Ts("fcw", [128, NFC, 3], F32)
            carry_cu = Ts("carry_cu", [128, 8, 2], F32)
            carry_a = Ts("carry_a", [128, NFC, 2], F32)
            P.dma("pool", cw[:], cw_d, "cw", writes=["cw"])
            P.dma("pool", fcw[:], fcw_d, "fcw", writes=["fcw"])
            P.op("dve", lambda e: e.memset(carry_cu[:], 0.0), writes=["carry_cu"])
            P.op("dve", lambda e: e.memset(carry_a[:], 0.0), writes=["carry_a"])

            wstate = {"i": 0}

            def load_w(ci):
                s_ = wstate["i"] % NB
                wstate["i"] += 1
                nk, ncols, _ = CH[ci]
                P.dma("sp", wr[s_][:, 0:nk, 0:ncols], wscr[ci][:, 0:nk, 0:ncols], "wr%d" % s_, writes=["wr%d" % s_])
                return wr[s_], "wr%d" % s_

            stg_state = {"i": 0}

            def stg_next():
                i = stg_state["i"] % NSTG
                stg_state["i"] += 1
                return stg[i], "stg%d" % i

            out_events = []

            def conv_taps(tm, tres, src, sres, wcol, N):
                P.op("dve", lambda e: e.scalar_tensor_tensor(tm[:, 0:N], src[:, 1:1 + N], wcol[:, 1:2],
                                                             tm[:, 0:N], ALU.mult, ALU.add),
                     reads=list(sres) + [tres, "cw", "fcw"], writes=[tres])
                P.op("dve", lambda e: e.scalar_tensor_tensor(tm[:, 0:N], src[:, 0:N], wcol[:, 0:1],
                                                             tm[:, 0:N], ALU.mult, ALU.add),
                     reads=list(sres) + [tres, "cw", "fcw"], writes=[tres])

            def do_norm1(k, defer=False):
                if k < 0:
                    sb, sres = stg_next()
                    P.dma("pool", sb[0:NMETA, :], meta, sres, writes=[sres])
                    norm_tile(sb[0:NMETA, :], NMETA, bufAm[:, :, 0:NMETA], g3[:, 0, :], sres, "bufAm")
                    return
                acts = []
                pend = []

                def load_stats(t):
                    sb, sres = stg_next()
                    r0 = k * 512 + t * 128
                    P.dma("pool", sb[:], x[r0:r0 + 128, :], sres, writes=[sres])
                    c = norm_stats(sb[:], 128, sres)
                    pend.append((c, sb[:], 128, bufA[:, :, t * 128:(t + 1) * 128], g3[:, 0, :], sres, "bufA"))

                for t in range(min(4, NSTG)):
                    load_stats(t)

                def fin(i):
                    norm_finish(*pend[i])
                    if i + NSTG < 4:
                        load_stats(i + NSTG)
                acts = [(lambda i=i: fin(i)) for i in range(4)]
                if defer:
                    return acts
                for a_ in acts:
                    a_()
                return []

            def block(bi, nxt):
                is_meta = bi < 0
                N = NMETA if is_meta else 512
                ntt = 1 if is_meta else 4
                rows = NMETA if is_meta else 128
                pos0 = 0 if is_meta else NMETA + bi * 512
                bA, bB, bC = (bufAm, bufBm, bufCm) if is_meta else (bufA, bufB, bufC)
                nA, nB, nC = ("bufAm", "bufBm", "bufCm") if is_meta else ("buf" + "A", "buf" + "B", "buf" + "C")
                if is_meta:
                    Zm, zres_m = stg_next()
                zrow = (lambda t, cs: Zm[0:rows, cs]) if is_meta else (lambda t, cs: Z[0:rows, t, cs])
                zres = (lambda t: zres_m) if is_meta else (lambda t: "Zt%d" % t)
                full = slice(0, D)
                if is_meta:
                    P.dma("pool", Zm[0:NMETA, :], meta, zres_m, writes=[zres_m])
                else:
                    for t in range(4):
                        r0 = bi * 512 + t * 128
                        P.dma("pool", Z[:, t, :], x[r0:r0 + 128, :], "Z%d" % t, writes=["Zt%d" % t])
                for cc in range(8):
                    wA, rA = yield (CI_A + cc)
                    pU, pC, pB = psalloc(ring2), psalloc(ring2), psalloc(ring2)
                    for (pb_, c0) in ((pU, 256), (pC, 128), (pB, 0)):
                        mm_group(banks[pb_][:, 0:N], [(wA[:, kc, c0:c0 + 128], bA[:, kc, 0:N]) for kc in range(8)],
                                 reads=[rA, nA], writes=["bk%d" % pb_])
                    k = cc % 2
                    cu, us, tm = cub[k], usb[k], tmp[k]
                    P.op("act", lambda e, us=us, pU=pU: e.activation(us[:, 0:N], banks[pU][:, 0:N], AF.Copy),
                         reads=["bk%d" % pU], writes=["usb%d" % k])
                    P.op("pool", lambda e, cu=cu, cc=cc: e.tensor_copy(cu[:, 0:2], carry_cu[:, cc, :]),
                         reads=["carry_cu"], writes=["cubc%d" % k])
                    P.op("dve", lambda e, cu=cu, us=us, pC=pC: e.tensor_tensor(
                        cu[:, 2:2 + N], banks[pC][:, 0:N], us[:, 0:N], ALU.mult),
                        reads=["bk%d" % pC, "usb%d" % k], writes=["cub%d" % k])
                    P.op("act", lambda e, cu=cu, tm=tm, cc=cc: e.activation(
                        tm[:, 0:N], cu[:, 2:2 + N], AF.Copy, scale=cw[:, cc, 2:3]),
                        reads=["cub%d" % k, "cw"], writes=["tmp%d" % k])
                    conv_taps(tm, "tmp%d" % k, cu, ["cub%d" % k, "cubc%d" % k], cw[:, cc, :], N)
                    P.op("pool", lambda e, cu=cu, cc=cc: e.tensor_copy(carry_cu[:, cc, :], cu[:, N:N + 2]),
                         reads=["cub%d" % k, "cubc%d" % k], writes=["carry_cu"])
                    P.op("dve", lambda e, tm=tm, pB=pB, cc=cc: e.tensor_tensor(
                        bB[:, cc, 0:N], banks[pB][:, 0:N], tm[:, 0:N], ALU.mult),
                        reads=["bk%d" % pB, "tmp%d" % k], writes=[nB])
                for oc in range(8):
                    wB, rB = yield (CI_B + oc)
                    pGA, pGC, pYA, pYC = (psalloc(ring2) for _ in range(4))
                    mm_group(banks[pGA][:, 0:N], [(wB[:, kc, 256:384], bA[:, kc, 0:N]) for kc in range(8)],
                             reads=[rB, nA], writes=["bk%d" % pGA])
                    mm_group(banks[pGC][:, 0:N], [(wB[:, kc, 384:512], bA[:, kc, 0:N]) for kc in range(8)],
                             reads=[rB, nA], writes=["bk%d" % pGC])
                    mm_group(banks[pYA][:, 0:N], [(wB[:, kc, 0:128], attT[:, kc, pos0:pos0 + N]) for kc in range(8)],
                             reads=[rB, "attT"], writes=["bk%d" % pYA])
                    mm_group(banks[pYC][:, 0:N], [(wB[:, kc, 128:256], bB[:, kc, 0:N]) for kc in range(8)],
                             reads=[rB, nB], writes=["bk%d" % pYC])
                    k = oc % 2
                    s1, s2, tm = sg[k], usb[k], tmp[k]
                    P.op("act", lambda e, s1=s1, pGA=pGA: e.activation(s1[:, 0:N], banks[pGA][:, 0:N], AF.Sigmoid),
                         reads=["bk%d" % pGA], writes=["sg%d" % k])
                    P.op("act", lambda e, s2=s2, pGC=pGC: e.activation(s2[:, 0:N], banks[pGC][:, 0:N], AF.Sigmoid),
                         reads=["bk%d" % pGC], writes=["usb%d" % k])
                    P.op("dve", lambda e, s1=s1, pYA=pYA, tm=tm: e.tensor_tensor(
                        tm[:, 0:N], banks[pYA][:, 0:N], s1[:, 0:N], ALU.mult),
                        reads=["bk%d" % pYA, "sg%d" % k], writes=["tmp%d" % k])
                    P.op("dve", lambda e, s2=s2, pYC=pYC: e.tensor_tensor(
                        s2[:, 0:N], banks[pYC][:, 0:N], s2[:, 0:N], ALU.mult),
                        reads=["bk%d" % pYC, "usb%d" % k], writes=["usb%d" % k])
                    P.op("dve", lambda e, s2=s2, tm=tm, oc=oc: e.tensor_tensor(
                        bC[:, oc, 0:N], tm[:, 0:N], s2[:, 0:N], ALU.add),
                        reads=["tmp%d" % k, "usb%d" % k], writes=[nC])
                if debug and bi == 0:
                    P.dma("pool", dbg_A, bA[:], "dbgA", reads=[nA])
                    P.dma("pool", dbg_B, bB[:], "dbgB", reads=[nB])
                    P.dma("pool", dbg_C, bC[:], "dbgC", reads=[nC])
                wO0 = yield (CI_C + 0)
                wO1 = yield (CI_C + 1)
                wO = [wO0, wO1]
                cst_ = {}
                for t in range(ntt + 1):
                    if t < ntt:
                        for half in range(2):
                            hs = slice(half * 512, (half + 1) * 512)
                            pb_ = psalloc(ring2)
                            mm_group(banks[pb_][0:rows, :],
                                     [(bC[:, kc, t * rows:(t + 1) * rows], wO[half][0][:, kc, :]) for kc in range(8)],
                                     reads=[wO[half][1], nC], writes=["bk%d" % pb_])
                            zz_ = zrow(t, hs)
                            P.op("dve", lambda e, pb_=pb_, zz_=zz_: e.tensor_tensor(
                                zz_, banks[pb_][0:rows, :], zz_, ALU.add),
                                reads=["bk%d" % pb_, zres(t)], writes=[zres(t)])
                    if t < ntt:
                        cst_[t] = norm_stats(zrow(t, full), rows, zres(t))
                    if t > 0:
                        tp_ = t - 1
                        norm_finish(cst_[tp_], zrow(tp_, full), rows, bA[:, :, tp_ * rows:(tp_ + 1) * rows],
                                    g3[:, 1, :], zres(tp_), nA)
                if debug and bi == 0:
                    P.dma("pool", dbg_Z, Z[:], "dbgZ", reads=["Zt0", "Zt1", "Zt2", "Zt3"])
                for p_ in range(NFC // 2):
                    wD, rD = yield (CI_D + p_)
                    fcs = (2 * p_, 2 * p_ + 1)
                    pA = [psalloc(ring2), psalloc(ring2)]
                    for i in range(2):
                        mm_group(banks[pA[i]][:, 0:N], [(wD[:, kc, i * 128:(i + 1) * 128], bA[:, kc, 0:N])
                                                         for kc in range(8)],
                                 reads=[rD, nA], writes=["bk%d" % pA[i]])
                    if not is_meta:
                        pV = [psalloc(ring2), psalloc(ring2)]
                        for i in range(2):
                            mm_group(banks[pV[i]][:, 0:N], [(wD[:, kc, 256 + i * 128:256 + (i + 1) * 128], bA[:, kc, 0:N])
                                                             for kc in range(8)],
                                     reads=[rD, nA], writes=["bk%d" % pV[i]])
                    for i in range(2):
                        fc = fcs[i]
                        ab, tm = cub[i], tmp[i]
                        P.op("pool", lambda e, ab=ab, fc=fc: e.tensor_copy(ab[:, 0:2], carry_a[:, fc, :]),
                             reads=["carry_a"], writes=["cubc%d" % i])
                        P.op("act", lambda e, ab=ab, pa=pA[i]: e.activation(ab[:, 2:2 + N], banks[pa][:, 0:N], AF.Copy),
                             reads=["bk%d" % pA[i]], writes=["cub%d" % i])
                        if not is_meta:
                            P.op("act", lambda e, tm=tm, pa=pA[i], fc=fc: e.activation(
                                tm[:, 0:N], banks[pa][:, 0:N], AF.Copy, scale=fcw[:, fc, 2:3]),
                                reads=["bk%d" % pA[i], "fcw"], writes=["tmp%d" % i])
                    if not is_meta:
                        for i in range(2):
                            fc = fcs[i]
                            ab, tm = cub[i], tmp[i]
                            P.op("dve", lambda e, tm=tm, ab=ab, fc=fc: e.scalar_tensor_tensor(
                                tm[:, 0:N], ab[:, 1:1 + N], fcw[:, fc, 1:2], tm[:, 0:N], ALU.mult, ALU.add),
                                reads=["cub%d" % i, "cubc%d" % i, "tmp%d" % i, "fcw"], writes=["tmp%d" % i])
                        for i in range(2):
                            fc = fcs[i]
                            ab, tm = cub[i], tmp[i]
                            P.op("dve", lambda e, tm=tm, ab=ab, fc=fc: e.scalar_tensor_tensor(
                                tm[:, 0:N], ab[:, 0:N], fcw[:, fc, 0:1], tm[:, 0:N], ALU.mult, ALU.add),
                                reads=["cub%d" % i, "cubc%d" % i, "tmp%d" % i, "fcw"], writes=["tmp%d" % i])
                    for i in range(2):
                        fc = fcs[i]
                        ab = cub[i]
                        P.op("pool", lambda e, ab=ab, fc=fc: e.tensor_copy(carry_a[:, fc, :], ab[:, N:N + 2]),
                             reads=["cub%d" % i, "cubc%d" % i], writes=["carry_a"])
                    if is_meta:
                        continue
                    for i in range(2):
                        tm, s1 = tmp[i], sg[i]
                        P.op("act", lambda e, s1=s1, tm=tm: e.activation(s1[:, 0:N], tm[:, 0:N], AF.Silu),
                             reads=["tmp%d" % i], writes=["sg%d" % i])
                    for i in range(2):
                        fc = fcs[i]
                        s1 = sg[i]
                        P.op("dve", lambda e, s1=s1, pv=pV[i], fc=fc: e.tensor_tensor(
                            gT[:, fc, 0:N], banks[pv][:, 0:N], s1[:, 0:N], ALU.mult),
                            reads=["bk%d" % pV[i], "sg%d" % i], writes=["gT"])
                if is_meta:
                    return
                if debug and bi == 0:
                    P.dma("pool", dbg_G, gT[:], "dbgG", reads=["gT"])
                fin1 = []
                for half in range(2):
                    hs = slice(half * 512, (half + 1) * 512)
                    pbs = [psalloc(ring2) for _ in range(4)]
                    for kg in range(3):
                        k0 = kg * 8
                        nk = min(8, NFC - k0)
                        wE, rE = yield (CI_E + half * 3 + kg)
                        for t in range(4):
                            def dmm(e, t=t, k0=k0, nk=nk, wE=wE, pb_=pbs[t]):
                                ins = None
                                for kk in range(nk):
                                    ins = e.matmul(banks[pb_][:, :], lhsT=gT[:, k0 + kk, t * 128:(t + 1) * 128],
                                                   rhs=wE[:, kk, :], start=(k0 + kk == 0), stop=(k0 + kk == NFC - 1))
                                return ins
                            P.op("pe", dmm, reads=[rE, "gT"], writes=["bk%d" % pbs[t]])
                            if half == 1 and fin1:
                                fin1.pop(0)()
                    for t in range(4):
                        P.op("dve", lambda e, t=t, hs=hs, pb_=pbs[t]: e.tensor_tensor(
                            Z[:, t, hs], banks[pb_][:, :], Z[:, t, hs], ALU.add),
                            reads=["bk%d" % pbs[t], "Zt%d" % t], writes=["Zt%d" % t])
                    if half == 0 and nxt is not None:
                        fin1 = do_norm1(nxt, defer=True)
                for t in range(4):
                    c = norm_stats(Z[:, t, :], 128, "Zt%d" % t)
                    o, ores = stg_next()
                    P.op("dve", lambda e, o=o, t=t, c=c: e.scalar_tensor_tensor(
                        o[:], Z[:, t, :], rs[:, c:c + 1], g3[:, 2, :], ALU.mult, ALU.mult),
                        reads=["Zt%d" % t, "rs%d" % c, "g3"], writes=[ores])
                    r0 = bi * 512 + t * 128
                    out_events.append(P.dma("pool", out[r0:r0 + 128, :], o[:], ores, reads=[ores]))

            def drive(gens):
                req = []
                for g_ in gens:
                    try:
                        req.append(next(g_))
                    except StopIteration:
                        req.append(None)
                while any(r is not None for r in req):
                    ci = next(r for r in req if r is not None)
                    w = load_w(ci)
                    for i_, g_ in enumerate(gens):
                        if req[i_] is None:
                            continue
                        assert req[i_] == ci, (req, ci)
                        try:
                            req[i_] = g_.send(w)
                        except StopIteration:
                            req[i_] = None

            do_norm1(-1)
            do_norm1(0)
            for bi in range(p2_blocks):
                nxt = bi + 1 if bi + 1 < p2_blocks else None
                gens = [block(bi, nxt)]
                if bi == 0:
                    gens = [block(-1, None)] + gens
                drive(gens)
            P.barrier()
            P.wait_events("sp", out_events)
        P.emit()
    return nc


_NC_CACHE = {}


def _consts():
    s = np.arange(128)[:, None]
    t = np.arange(128)[None, :]
    c = np.zeros((128, 4, 128), np.float32)
    c[:, 0, :] = np.eye(128, dtype=np.float32)
    c[:, 1, :] = np.where(s > t, NEG, 0.0)
    c[:, 2, :] = (s <= t).astype(np.float32)
    c[:, 3, :] = 1.0
    return c


def make_in_maps(x, meta_tokens, g_mix, w_in, b_f, conv_w, w_o_attn, w_o_conv, w_o,
                 g_ffn, w_ffn_in, ffn_conv_w, w_ffn_out, g_final):
    f = lambda a: np.ascontiguousarray(np.asarray(a, dtype=np.float32))
    g3 = np.stack([f(g_mix)[0], f(g_ffn)[0], f(g_final)], 0)
    g3 = np.ascontiguousarray(np.broadcast_to(g3[None], (128, 3, D)))
    bfrep = np.ascontiguousarray(np.broadcast_to(f(b_f)[0][None, None, :], (128, NT, 8)).reshape(128, NT * 8))
    cw = np.ascontiguousarray(f(conv_w)[0].T.reshape(8, 128, 3).transpose(1, 0, 2))
    fcw = np.ascontiguousarray(f(ffn_conv_w)[0].T.reshape(NFC, 128, 3).transpose(1, 0, 2))
    shared = {
        "meta": f(meta_tokens), "w_in": f(w_in)[0], "w_o_attn": f(w_o_attn)[0], "w_o_conv": f(w_o_conv)[0],
        "w_o": f(w_o)[0], "w_ffn_in": f(w_ffn_in)[0], "w_ffn_out": f(w_ffn_out)[0],
        "g3": g3, "bfrep": bfrep, "cw": cw, "fcw": fcw, "cst": _consts(),
    }
    xs_ = f(x)
    return [dict(shared, x=xs_[b]) for b in range(xs_.shape[0])]


def kernel(**inputs):
    in_maps = make_in_maps(**inputs)
    if "nc" not in _NC_CACHE:
        _NC_CACHE["nc"] = build_nc()
    res = run_bass_kernel_spmd(_NC_CACHE["nc"], in_maps, core_ids=list(range(8)))
    return np.stack([r["out"] for r in res.results], 0).astype(np.float32)
```
